# Optimizing a Trainium2 kernel written in Bass

```python
import jax, jax.numpy as jnp
from jax import lax
import numpy as np

D_MODEL = 1024
BATCH = 8
SEQ = 2048
DEPTH = 1
DEC_BATCH = 128
DEC_SEQ = 1
PAST_LEN = 16384
PAGE_SIZE = 128

D_RNN = D_MODEL
RG_BLOCKS = 8
RG_BLOCK_W = D_RNN // RG_BLOCKS
RG_C = 8.0
CONV_W = 4
DN_HEADS = 8
DN_DK = 128
DN_DV = 128
DN_QK = DN_HEADS * DN_DK
DN_V = DN_HEADS * DN_DV
DN_QKV = 2 * DN_QK + DN_V
DN_CHUNK = 64
D_FF = 2816
N_ADA = 9
EPS = 1e-6
IN_SPLITS = (D_RNN, D_RNN, DN_QKV, DN_HEADS, DN_HEADS, DN_V, D_MODEL, D_MODEL)
D_IN = D_RNN * 2 + DN_QKV + 2 * DN_HEADS + DN_V + 2 * D_MODEL

kernel_name = 'hybrid_rglru_gdn_macaron_adaln_step'


def rmsnorm(x, g):
    xf = x.astype(jnp.float32)
    y = xf * lax.rsqrt(jnp.mean(xf * xf, axis=-1, keepdims=True) + EPS)
    return (y * g.astype(jnp.float32)).astype(x.dtype)


def l2norm(x):
    xf = x.astype(jnp.float32)
    return xf * lax.rsqrt(jnp.sum(xf * xf, axis=-1, keepdims=True) + EPS)


def modulate(h, shift, scale):
    return h * (1.0 + scale) + shift


def swiglu(h, w_up, w_down):
    gv = h @ w_up
    return (jax.nn.silu(gv[..., :D_FF]) * gv[..., D_FF:]) @ w_down


def split_cols(z, sizes):
    out, start = [], 0
    for s in sizes:
        out.append(z[..., start:start + s])
        start += s
    return out


def causal_dwconv(x, buf, w, b):
    L = x.shape[1]
    xp = jnp.concatenate([buf.astype(x.dtype), x], axis=1)
    y = xp[:, 0:L] * w[0]
    for j in range(1, CONV_W):
        y = y + xp[:, j:j + L] * w[j]
    if b is not None:
        y = y + b
    return y, xp[:, -(CONV_W - 1):]


def rglru(x, h0, w_a, b_a, w_x, b_x, lam, reset_first):
    B, L, _ = x.shape
    xb = x.reshape(B, L, RG_BLOCKS, RG_BLOCK_W)
    r = jax.nn.sigmoid(jnp.einsum('blnc,ncd->blnd', xb, w_a).reshape(B, L, D_RNN) + b_a)
    i = jax.nn.sigmoid(jnp.einsum('blnc,ncd->blnd', xb, w_x).reshape(B, L, D_RNN) + b_x)
    log_a = -RG_C * r.astype(jnp.float32) * jax.nn.softplus(-lam.astype(jnp.float32))
    a = jnp.exp(log_a)
    mult = jnp.sqrt(-jnp.expm1(2.0 * log_a))
    if reset_first:
        mult = mult.at[:, 0].set(1.0)
    bterm = mult * (i * x).astype(jnp.float32)
    bterm = bterm.at[:, 0].add(a[:, 0] * h0.astype(jnp.float32))

    def combine(lhs, rhs):
        a1, b1 = lhs
        a2, b2 = rhs
        return a1 * a2, a2 * b1 + b2

    _, h = lax.associative_scan(combine, (a, bterm), axis=1)
    return h.astype(x.dtype), h[:, -1].astype(x.dtype)


def gated_delta_rule(q, k, v, g, beta, S0):
    B, L = q.shape[0], q.shape[1]
    C = min(DN_CHUNK, L)
    n = -(-L // C)
    pad = n * C - L

    def prep(t):
        t = jnp.pad(t, [(0, 0), (0, pad)] + [(0, 0)] * (t.ndim - 2))
        t = t.reshape((B, n, C) + t.shape[2:])
        return jnp.moveaxis(t, 3, 1)

    q, k, v, g, beta = prep(q), prep(k), prep(v), prep(g), prep(beta)
    gcum = jnp.cumsum(g, axis=-1)
    idx = jnp.arange(C)
    incl = idx[:, None] >= idx[None, :]
    strict = idx[:, None] > idx[None, :]
    decay = jnp.exp(jnp.where(incl, gcum[..., :, None] - gcum[..., None, :], -jnp.inf))
    kb = k * beta[..., None]
    lmat = jnp.where(strict, jnp.einsum('bhnik,bhnjk->bhnij', kb, k) * decay, 0.0)
    tmat = lmat + jnp.eye(C, dtype=lmat.dtype)
    rhs = jnp.concatenate([v * beta[..., None], kb * jnp.exp(gcum)[..., None]], axis=-1)
    sol = lax.linalg.triangular_solve(tmat, rhs, left_side=True, lower=True, unit_diagonal=True)
    u, w = sol[..., :DN_DV], sol[..., DN_DV:]
    attn = jnp.einsum('bhnik,bhnjk->bhnij', q, k) * decay
    q_dec = q * jnp.exp(gcum)[..., None]
    g_last = gcum[..., -1]
    k_dec = k * jnp.exp(g_last[..., None] - gcum)[..., None]

    def step(S, xs):
        u_c, w_c, attn_c, qd_c, kd_c, gl_c = xs
        v_new = u_c - jnp.einsum('bhck,bhkv->bhcv', w_c, S)
        o = jnp.einsum('bhck,bhkv->bhcv', qd_c, S) + jnp.einsum('bhij,bhjv->bhiv', attn_c, v_new)
        S = S * jnp.exp(gl_c)[..., None, None] + jnp.einsum('bhck,bhcv->bhkv', kd_c, v_new)
        return S, o

    xs = (jnp.moveaxis(u, 2, 0), jnp.moveaxis(w, 2, 0), jnp.moveaxis(attn, 2, 0),
          jnp.moveaxis(q_dec, 2, 0), jnp.moveaxis(k_dec, 2, 0), jnp.moveaxis(g_last, 2, 0))
    S, o = lax.scan(step, S0.astype(jnp.float32), xs)
    o = jnp.moveaxis(o, 0, 2)
    o = jnp.moveaxis(o, 1, 3).reshape(B, n * C, DN_HEADS, DN_DV)[:, :L]
    return o, S


def layer(x, c, h0, conv_rnn0, S0, conv_qkv0, reset_first, p):
    B, L, _ = x.shape
    ada = (c @ p['w_ada'] + p['b_ada']).reshape(B, N_ADA, D_MODEL)[:, :, None, :]
    sh1, sc1, gt1 = ada[:, 0], ada[:, 1], ada[:, 2]
    sh2, sc2, gt2 = ada[:, 3], ada[:, 4], ada[:, 5]
    sh3, sc3, gt3 = ada[:, 6], ada[:, 7], ada[:, 8]

    h = modulate(rmsnorm(x, p['norm_ffn1']), sh1, sc1)
    x = x + 0.5 * gt1 * swiglu(h, p['w_ffn1_up'], p['w_ffn1_down'])

    h = modulate(rmsnorm(x, p['norm_mix']), sh2, sc2)
    z = h @ p['w_in']
    xr, gr, qkv, a_logit, b_logit, zg, mg_a, mg_b = split_cols(z, IN_SPLITS)

    xr, conv_rnn_new = causal_dwconv(xr, conv_rnn0, p['conv_rnn_w'], p['conv_rnn_b'])
    yr, h_new = rglru(xr, h0, p['rg_w_a'], p['rg_b_a'], p['rg_w_x'], p['rg_b_x'], p['rg_lambda'], reset_first)
    o_a = yr * jax.nn.gelu(gr)

    qkv, conv_qkv_new = causal_dwconv(qkv, conv_qkv0, p['conv_qkv_w'], None)
    qkv = jax.nn.silu(qkv)
    q, k, v = split_cols(qkv, (DN_QK, DN_QK, DN_V))
    q = l2norm(q.reshape(B, L, DN_HEADS, DN_DK)) * (DN_DK ** -0.5)
    k = l2norm(k.reshape(B, L, DN_HEADS, DN_DK))
    v = v.reshape(B, L, DN_HEADS, DN_DV).astype(jnp.float32)
    g = -jnp.exp(p['dn_a_log'].astype(jnp.float32)) * jax.nn.softplus(
        a_logit.astype(jnp.float32) + p['dn_dt_bias'].astype(jnp.float32))
    beta = jax.nn.sigmoid(b_logit.astype(jnp.float32))
    o, S_new = gated_delta_rule(q, k, v, g, beta, S0)
    o = rmsnorm(o, p['dn_norm']) * jax.nn.silu(zg.reshape(B, L, DN_HEADS, DN_DV).astype(jnp.float32))
    o_b = o.reshape(B, L, DN_V).astype(x.dtype)

    y_a = o_a @ p['w_branch'][0]
    y_b = o_b @ p['w_branch'][1]
    merged = jax.nn.sigmoid(mg_a) * y_a + jax.nn.sigmoid(mg_b) * y_b
    x = x + gt2 * (merged @ p['w_out'])

    h = modulate(rmsnorm(x, p['norm_ffn2']), sh3, sc3)
    x = x + 0.5 * gt3 * swiglu(h, p['w_ffn2_up'], p['w_ffn2_down'])
    return x, h_new, conv_rnn_new, S_new.astype(x.dtype), conv_qkv_new


def setup_inputs(seed: int = 0) -> dict:
    key = jax.random.key(seed)
    ks = jax.random.split(key, 40)
    f32 = jnp.float32
    nrm = lambda k, s, sc: jax.random.normal(k, s, f32) * sc
    a0 = jax.random.uniform(ks[30], (DEPTH, D_RNN), f32, 0.9, 0.999)
    s0 = a0 ** (1.0 / RG_C)
    dt = jnp.exp(jax.random.uniform(ks[31], (DEPTH, DN_HEADS), f32, np.log(1e-3), np.log(1e-1)))
    return {
        'x_prompt': nrm(ks[0], (BATCH, SEQ, D_MODEL), 1.0),
        'x_sample': nrm(ks[1], (DEC_BATCH, DEC_SEQ, D_MODEL), 1.0),
        'c_prompt': nrm(ks[2], (BATCH, D_MODEL), 1.0),
        'c_sample': nrm(ks[3], (DEC_BATCH, D_MODEL), 1.0),
        'state_rglru_h': nrm(ks[4], (DEPTH, DEC_BATCH, D_RNN), 0.5),
        'state_rglru_conv': nrm(ks[5], (DEPTH, DEC_BATCH, CONV_W - 1, D_RNN), 1.0),
        'state_delta_S': nrm(ks[6], (DEPTH, DEC_BATCH, DN_HEADS, DN_DK, DN_DV), 0.05),
        'state_delta_conv': nrm(ks[7], (DEPTH, DEC_BATCH, CONV_W - 1, DN_QKV), 1.0),
        'w_ada': nrm(ks[8], (DEPTH, D_MODEL, N_ADA * D_MODEL), 0.5 * D_MODEL ** -0.5),
        'b_ada': nrm(ks[9], (DEPTH, N_ADA * D_MODEL), 0.1),
        'norm_ffn1': 1.0 + nrm(ks[10], (DEPTH, D_MODEL), 0.02),
        'w_ffn1_up': nrm(ks[11], (DEPTH, D_MODEL, 2 * D_FF), D_MODEL ** -0.5),
        'w_ffn1_down': nrm(ks[12], (DEPTH, D_FF, D_MODEL), D_FF ** -0.5),
        'norm_mix': 1.0 + nrm(ks[13], (DEPTH, D_MODEL), 0.02),
        'w_in': nrm(ks[14], (DEPTH, D_MODEL, D_IN), D_MODEL ** -0.5),
        'conv_rnn_w': nrm(ks[15], (DEPTH, CONV_W, D_RNN), CONV_W ** -0.5),
        'conv_rnn_b': nrm(ks[16], (DEPTH, D_RNN), 0.02),
        'rg_w_a': nrm(ks[17], (DEPTH, RG_BLOCKS, RG_BLOCK_W, RG_BLOCK_W), RG_BLOCK_W ** -0.5),
        'rg_b_a': nrm(ks[18], (DEPTH, D_RNN), 0.02),
        'rg_w_x': nrm(ks[19], (DEPTH, RG_BLOCKS, RG_BLOCK_W, RG_BLOCK_W), RG_BLOCK_W ** -0.5),
        'rg_b_x': nrm(ks[20], (DEPTH, D_RNN), 0.02),
        'rg_lambda': jnp.log(s0) - jnp.log1p(-s0),
        'conv_qkv_w': nrm(ks[21], (DEPTH, CONV_W, DN_QKV), CONV_W ** -0.5),
        'dn_a_log': jnp.log(jax.random.uniform(ks[22], (DEPTH, DN_HEADS), f32, 1.0, 16.0)),
        'dn_dt_bias': dt + jnp.log(-jnp.expm1(-dt)),
        'dn_norm': 1.0 + nrm(ks[23], (DEPTH, DN_DV), 0.02),
        'w_branch': nrm(ks[24], (DEPTH, 2, D_RNN, D_MODEL), D_RNN ** -0.5),
        'w_out': nrm(ks[25], (DEPTH, D_MODEL, D_MODEL), D_MODEL ** -0.5),
        'norm_ffn2': 1.0 + nrm(ks[26], (DEPTH, D_MODEL), 0.02),
        'w_ffn2_up': nrm(ks[27], (DEPTH, D_MODEL, 2 * D_FF), D_MODEL ** -0.5),
        'w_ffn2_down': nrm(ks[28], (DEPTH, D_FF, D_MODEL), D_FF ** -0.5),
        'norm_final': 1.0 + nrm(ks[29], (D_MODEL,), 0.02),
    }


def reference(x_prompt, x_sample, c_prompt, c_sample, state_rglru_h, state_rglru_conv, state_delta_S,
              state_delta_conv, w_ada, b_ada, norm_ffn1, w_ffn1_up, w_ffn1_down, norm_mix, w_in,
              conv_rnn_w, conv_rnn_b, rg_w_a, rg_b_a, rg_w_x, rg_b_x, rg_lambda, conv_qkv_w,
              dn_a_log, dn_dt_bias, dn_norm, w_branch, w_out, norm_ffn2, w_ffn2_up, w_ffn2_down,
              norm_final):
    dt = x_prompt.dtype
    xp, xs = x_prompt, x_sample
    hp, cp, sp, qp = [], [], [], []
    hs, cs, ss, qs = [], [], [], []
    for l in range(DEPTH):
        p = {
            'w_ada': w_ada[l], 'b_ada': b_ada[l], 'norm_ffn1': norm_ffn1[l], 'w_ffn1_up': w_ffn1_up[l],
            'w_ffn1_down': w_ffn1_down[l], 'norm_mix': norm_mix[l], 'w_in': w_in[l],
            'conv_rnn_w': conv_rnn_w[l], 'conv_rnn_b': conv_rnn_b[l], 'rg_w_a': rg_w_a[l], 'rg_b_a': rg_b_a[l],
            'rg_w_x': rg_w_x[l], 'rg_b_x': rg_b_x[l], 'rg_lambda': rg_lambda[l], 'conv_qkv_w': conv_qkv_w[l],
            'dn_a_log': dn_a_log[l], 'dn_dt_bias': dn_dt_bias[l], 'dn_norm': dn_norm[l],
            'w_branch': w_branch[l], 'w_out': w_out[l], 'norm_ffn2': norm_ffn2[l],
            'w_ffn2_up': w_ffn2_up[l], 'w_ffn2_down': w_ffn2_down[l],
        }
        xp, h1, c1, s1, q1 = layer(
            xp, c_prompt,
            jnp.zeros((BATCH, D_RNN), dt),
            jnp.zeros((BATCH, CONV_W - 1, D_RNN), dt),
            jnp.zeros((BATCH, DN_HEADS, DN_DK, DN_DV), dt),
            jnp.zeros((BATCH, CONV_W - 1, DN_QKV), dt),
            True, p)
        xs, h2, c2, s2, q2 = layer(
            xs, c_sample, state_rglru_h[l], state_rglru_conv[l], state_delta_S[l], state_delta_conv[l],
            False, p)
        hp.append(h1); cp.append(c1); sp.append(s1); qp.append(q1)
        hs.append(h2); cs.append(c2); ss.append(s2); qs.append(q2)
    y_prompt = rmsnorm(xp, norm_final)
    y_sample = rmsnorm(xs, norm_final)
    return (y_prompt, y_sample,
            jnp.stack(hp), jnp.stack(cp), jnp.stack(sp), jnp.stack(qp),
            jnp.stack(hs), jnp.stack(cs), jnp.stack(ss), jnp.stack(qs))
```

```python
import os
import numpy as np
import concourse.bass as bass
import concourse.mybir as mybir
from concourse.bass_utils import run_bass_kernel_spmd

F32 = mybir.dt.float32
BF16 = mybir.dt.bfloat16
ALU = mybir.AluOpType
AF = mybir.ActivationFunctionType

NCORES = 8
D = 1024
FC = 8
LP = 2048
NS_ = 16
NT = LP + NS_
TILES = [(0, 512), (512, 512), (1024, 512), (1536, 512), (2048, 16)]
DFF = 2816
HC = 22
FFN_GROUPS = [(0, 8), (8, 16), (16, 22)]
NSLOT = 5
EPS = 1e-6
RX0, RA0, RB0, RC0, RBIG = 0, 66048, 99072, 132096, 143360
O_XR, O_GR, O_Q, O_K, O_V, O_A, O_B, O_ZG, O_MA, O_MB = 0, 1024, 2048, 3072, 4096, 5120, 5128, 5136, 6160, 7184


class Buf:
    __slots__ = ("name", "w", "r", "psum")

    def __init__(self, name="", psum=False):
        self.name, self.w, self.r, self.psum = name, None, {}, psum


class Tok:
    __slots__ = ("sem", "key", "val", "grp")

    def __init__(self, sem, key, val, grp=None):
        self.sem, self.key, self.val, self.grp = sem, key, val, grp

    def value(self):
        return self.grp.count * 16 if self.grp is not None else self.val


class DmaChan:
    def __init__(self, K, name, group=False):
        self.sem = K.nc.alloc_semaphore(name)
        self.key, self.count, self.group = name, 0, group


class Eng:
    def __init__(self, K, name, h, tracked_self):
        self.name, self.h = name, h
        self.sem = K.nc.alloc_semaphore("s_" + name)
        self.key = "s_" + name
        self.seq, self.waited, self.tracked_self = 0, {}, tracked_self
        self.nwait = self.ninst = 0


class K:
    def __init__(self, nc):
        self.nc = nc
        self.pe = Eng(self, "pe", nc.tensor, False)
        self.act = Eng(self, "act", nc.scalar, True)
        self.dve = Eng(self, "dve", nc.vector, True)
        self.pool = Eng(self, "pool", nc.gpsimd, True)
        self.sp = Eng(self, "sp", nc.sync, True)
        self.engs = [self.pe, self.act, self.dve, self.pool, self.sp]
        self.chans = []

    def chan(self, name, group=False):
        c = DmaChan(self, name, group)
        self.chans.append(c)
        return c

    def _collect(self, eng, reads, writes):
        need = {}

        def add(t):
            if t is None:
                return
            v = t.value()
            if t.key not in need or need[t.key][1] < v:
                need[t.key] = (t.sem, v)

        for b in reads:
            add(b.w)
            if b.psum:
                for t in b.r.values():
                    if t.key != eng.key:
                        add(t)
        for b in writes:
            add(b.w)
            for t in b.r.values():
                add(t)
        for key, (sem, v) in need.items():
            if key == eng.key and not eng.tracked_self:
                continue
            if eng.waited.get(key, 0) >= v:
                continue
            eng.h.wait_ge(sem, v)
            eng.waited[key] = v
            eng.nwait += 1

    def _commit(self, tok, reads, writes):
        for b in reads:
            old = b.r.get(tok.key)
            if old is None or old.value() <= tok.value():
                b.r[tok.key] = tok
        for b in writes:
            b.w = tok
            b.r = {}

    def op(self, eng, fn, reads=(), writes=()):
        self._collect(eng, reads, writes)
        inst = fn()
        eng.seq += 1
        eng.ninst += 1
        inst.then_inc(eng.sem, 1)
        self._commit(Tok(eng.sem, eng.key, eng.seq), reads, writes)
        return inst

    def dma(self, eng, chan, out, in_, reads=(), writes=()):
        self._collect(eng, reads, writes)
        inst = eng.h.dma_start(out=out, in_=in_)
        inst.then_inc(chan.sem, 16)
        chan.count += 1
        eng.ninst += 1
        tok = Tok(chan.sem, chan.key, None, grp=chan) if chan.group else Tok(chan.sem, chan.key, chan.count * 16)
        self._commit(tok, reads, writes)
        return inst

    def barrier(self):
        for e in self.engs:
            for o in self.engs:
                if o is e or o.seq == 0:
                    continue
                if e.waited.get(o.key, 0) < o.seq:
                    e.h.wait_ge(o.sem, o.seq)
                    e.waited[o.key] = o.seq
            for c in self.chans:
                if c.key.startswith("ringch"):
                    continue
                if c.count and e.waited.get(c.key, 0) < c.count * 16:
                    e.h.wait_ge(c.sem, c.count * 16)
                    e.waited[c.key] = c.count * 16

    def finish(self):
        for c in self.chans:
            if c.count and self.sp.waited.get(c.key, 0) < c.count * 16:
                self.sp.h.wait_ge(c.sem, c.count * 16)


def _cols(start, n):
    return list(range(start, start + n))


def build_plan():
    P = []

    def S(tag, name, k0, nk, cols, sub=None):
        P.append(dict(tag=tag, name=name, k0=k0, nk=nk, cols=cols, sub=sub))

    def ada(m0, m1):
        for m in range(m0, m1, 2):
            S(("ada", m), "w_ada", 0, 8, _cols(m * 128, 256))

    def ffn(n, ada_ms=()):
        up, dn = ("w_ffn1_up", "w_ffn1_down") if n == 1 else ("w_ffn2_up", "w_ffn2_down")
        ada_ms = list(ada_ms)

        def one_ada():
            if ada_ms:
                m = ada_ms.pop(0)
                S(("ada", m), "w_ada", 0, 8, _cols(m * 128, 256))

        for (i0, i1) in FFN_GROUPS:
            for i in range(i0, i1, 2):
                S(("up_g", n, i), up, 0, 8, _cols(i * 128, 256))
                S(("up_v", n, i), up, 0, 8, _cols(DFF + i * 128, 256))
                one_ada()
            for fp in range(4):
                S(("down", n, i0, fp), dn, i0, i1 - i0, _cols(fp * 256, 256))
                one_ada()
        while ada_ms:
            one_ada()

    ada(0, 24)
    ffn(1, ada_ms=range(24, 72, 2))
    S(("ab",), "w_in", 0, 8, _cols(O_A, 16))
    for h in range(8):
        S(("qk", h), "w_in", 0, 8, _cols(O_Q + h * 128, 128) + _cols(O_K + h * 128, 128))
        S(("vz", h), "w_in", 0, 8, _cols(O_V + h * 128, 128) + _cols(O_ZG + h * 128, 128))
    for j in range(0, 8, 2):
        S(("wb", 1, j), "w_branch", 0, 8, _cols(j * 128, 256), sub=1)
        S(("mg", 1, j), "w_in", 0, 8, _cols(O_MB + j * 128, 256))
    for c in range(8):
        S(("rgw", c), "rg_w", 0, 1, None, sub=c)
        S(("xrgr", c), "w_in", 0, 8, _cols(O_XR + c * 128, 128) + _cols(O_GR + c * 128, 128))
    for j in range(0, 8, 2):
        S(("wb", 0, j), "w_branch", 0, 8, _cols(j * 128, 256), sub=0)
        S(("mg", 0, j), "w_in", 0, 8, _cols(O_MA + j * 128, 256))
    for j in range(0, 8, 2):
        S(("wo", j), "w_out", 0, 8, _cols(j * 128, 256))
    for s in range(0, 4096, 256):
        S(("tok", s), "w_in", 0, 8, (_cols(O_XR + s, 256) if s < 1024 else _cols(O_Q + s - 1024, 256)))
    ffn(2)
    return P


def gather_slabs(plan, W):
    ns = len(plan)
    out = np.zeros((ns, 128, 8, 256), np.float32)
    for s, sp in enumerate(plan):
        if sp["name"] == "rg_w":
            c = sp["sub"]
            out[s, :, 0, 0:128] = W["rg_w_a"][0, c]
            out[s, :, 0, 128:256] = W["rg_w_x"][0, c]
            continue
        M = W[sp["name"]][0]
        if sp["sub"] is not None:
            M = M[sp["sub"]]
        rows = M[sp["k0"] * 128:(sp["k0"] + sp["nk"]) * 128][:, sp["cols"]]
        out[s, :, :sp["nk"], :len(sp["cols"])] = rows.reshape(sp["nk"], 128, -1).transpose(1, 0, 2)
    return out.reshape(ns, 128, 2048)


PV = {}
_off = 0
for _n, _w in [("b_ada", 72), ("n_ffn1", 8), ("n_mix", 8), ("n_ffn2", 8), ("n_fin", 8), ("crw", 32), ("crb", 8),
               ("rba", 8), ("rbx", 8), ("lam", 8), ("cqw", 96), ("dnn", 1), ("alog", 1), ("dtb", 1)]:
    PV[_n] = (_off, _w)
    _off += _w
NPV = _off
NSM = 1664


def pack_pvec(W):
    pv = np.zeros((128, NPV), np.float32)

    def put(name, arr):
        o, w = PV[name]
        pv[:arr.shape[0], o:o + w] = arr

    fm = lambda v: v.reshape(-1, 128).T
    put("b_ada", fm(W["b_ada"][0]))
    put("n_ffn1", fm(W["norm_ffn1"][0])); put("n_mix", fm(W["norm_mix"][0]))
    put("n_ffn2", fm(W["norm_ffn2"][0])); put("n_fin", fm(W["norm_final"]))
    put("crw", W["conv_rnn_w"][0].reshape(4, 8, 128).transpose(2, 1, 0).reshape(128, 32))
    put("crb", fm(W["conv_rnn_b"][0]))
    put("rba", fm(W["rg_b_a"][0])); put("rbx", fm(W["rg_b_x"][0])); put("lam", fm(W["rg_lambda"][0]))
    put("cqw", W["conv_qkv_w"][0].reshape(4, 24, 128).transpose(2, 1, 0).reshape(128, 96))
    put("dnn", W["dn_norm"][0].reshape(128, 1))
    put("alog", W["dn_a_log"][0].reshape(8, 1)); put("dtb", W["dn_dt_bias"][0].reshape(8, 1))
    return pv


class Prog:
    def __init__(self, dbg=(), gdn_stop=None):
        self.dbg = set(dbg)
        self.gdn_stop = gdn_stop
        nc = self.nc = bass.Bass("TRN2", target_bir_lowering=False)
        k = self.k = K(nc)
        self.plan = build_plan()
        self.nslab = len(self.plan)
        dt = lambda name, shape, kind: nc.dram_tensor(name, shape, F32, kind=kind).ap()
        self.d_x = dt("xT", [D, NT], "ExternalInput")
        self.d_c = dt("cT", [D, 17], "ExternalInput")
        self.d_w = dt("wslabs", [self.nslab, 128, 2048], "ExternalInput")
        self.d_pv = dt("pvec", [128, NPV], "ExternalInput")
        self.d_y = dt("yT", [D, NT], "ExternalOutput")
        self.dbg_out = {}
        self.big = nc.alloc_sbuf_tensor("big", [128, RBIG // 4], F32)
        self.xT = self.view(RX0, [FC, NT], F32)
        self.d_xsp = nc.dram_tensor("xspill", [128, FC * NT], F32, kind="Internal").ap()
        self.d_obsp = nc.dram_tensor("obspill", [128, 8 * NT], BF16, kind="Internal").ap()
        self.hT = nc.alloc_sbuf_tensor("hT_sb", [128, FC, NT], BF16)
        self.ring = [nc.alloc_sbuf_tensor(f"ring{i}", [128, 8, 256], BF16) for i in range(NSLOT)]
        self.pv = nc.alloc_sbuf_tensor("pv_sb", [128, NPV], F32)
        self.adaT = nc.alloc_sbuf_tensor("adaT", [128, 72, 17], F32)
        self.msc = nc.alloc_sbuf_tensor("msc", [128, 3, FC, 17], F32)
        self.gat = nc.alloc_sbuf_tensor("gat", [128, 3, FC, 17], F32)
        self.cb = nc.alloc_sbuf_tensor("cb", [128, 8, 17], BF16)
        self.ones_bf = nc.alloc_sbuf_tensor("ones_bf", [128, 128], BF16)
        self.epsc = nc.alloc_sbuf_tensor("epsc", [128, 1], F32)
        self.rgc = nc.alloc_sbuf_tensor("rgc", [128, 4, 8], F32)
        self.hnewT = nc.alloc_sbuf_tensor("hnewT", [128, 8, 17], F32)
        self.b_rgc = Buf("rgc"); self.b_hnew = Buf("hnew")
        self.d_small = dt("smallT", [128, NSM], "ExternalInput")
        self.d_hnew = dt("hnewT_o", [128, 8 * 17], "ExternalOutput")
        self.d_S0 = dt("S0", [16, 8, 128, 128], "ExternalInput")
        self.d_cmk = dt("cmask", [128, 7 * 128], "ExternalInput")
        self.d_crn = dt("cr_nat", [16, 3, 1024], "ExternalInput")
        self.d_cqn = dt("cq_nat", [16, 3, 3072], "ExternalInput")
        self.d_crp = dt("cr_p", [3, 1024], "ExternalOutput")
        self.d_cqp = dt("cq_p", [3, 3072], "ExternalOutput")
        self.d_crs = dt("cr_s", [16, 3, 1024], "ExternalOutput")
        self.d_cqs = dt("cq_s", [16, 3, 3072], "ExternalOutput")
        self.d_Sp = dt("S_p", [8, 128, 128], "ExternalOutput")
        self.d_Ss = dt("S_s", [16, 8, 128, 128], "ExternalOutput")
        self.ps = [nc.alloc_psum_tensor(f"ps{i}", [128, 512], F32) for i in range(8)]
        self.b_x = [[Buf(f"x{c}_{t}") for t in range(5)] for c in range(FC)]
        self.b_h = [[Buf(f"h{c}_{t}") for t in range(5)] for c in range(FC)]
        self.b_ring = [Buf(f"ring{i}") for i in range(NSLOT)]
        self.b_ps = [Buf(f"ps{i}", psum=True) for i in range(8)]
        self.b_pv = Buf("pv"); self.b_ada = Buf("ada"); self.b_mod = Buf("mod"); self.b_cb = Buf("cb")
        self.b_const = Buf("const")
        self.ring_ch = [k.chan(f"ringch{i}") for i in range(NSLOT)]
        self.ld = k.chan("ld", group=True)
        self.st = k.chan("st", group=True)
        self.ld2 = k.chan("ld2", group=True)
        self.ldp = k.chan("ldp", group=True)
        self.ps_i = 0
        self.ps_freelist = list(range(8))
        self.ada_bank = None
        self.ada_pending = []
        self.slab_i = 0
        self.slab_ld = 0
        self.slab_done = 0

    def view(self, off, shape, dtype):
        esz = 4 if dtype == F32 else 2
        n = int(np.prod(shape))
        nb = (n * esz + 3) // 4 * 4
        assert off % 4 == 0 and off + nb <= RBIG, (off, nb)
        ap = self.big[:, off // 4:(off + nb) // 4]
        if dtype != F32:
            ap = ap.bitcast(dtype)[:, 0:n]
        if len(shape) == 2:
            ap = ap.rearrange("p (a b) -> p a b", a=shape[0])
        elif len(shape) == 3:
            ap = ap.rearrange("p (a b c) -> p a b c", a=shape[0], b=shape[1])
        return ap

    def carve(self, off, specs):
        out = []
        for shape, dtype in specs:
            esz = 4 if dtype == F32 else 2
            nb = (int(np.prod(shape)) * esz + 31) // 32 * 32
            out.append(self.view(off, shape, dtype))
            off += nb
        return out, off

    def next_ps(self):
        i = self.ps_freelist.pop(0)
        self.ps_freelist.append(i)
        return i

    def ps_alloc(self):
        assert self.ps_freelist, "out of PSUM banks"
        return self.ps_freelist.pop(0)

    def ps_free(self, *idx):
        self.ps_freelist.extend(idx)

    def _issue_loads(self):
        k = self.k
        while self.slab_ld < self.nslab and self.slab_ld < self.slab_done + NSLOT:
            s = self.slab_ld
            sp = self.plan[s]
            slot = s % NSLOT
            nk = sp["nk"]
            src = self.d_w[s, :, 0:nk * 256].rearrange("p (k c) -> p k c", k=nk)
            k.dma(k.pool, self.ring_ch[slot], self.ring[slot][:, 0:nk, :], src, writes=[self.b_ring[slot]])
            self.slab_ld += 1

    def take_slab(self, tag):
        s = self.slab_i
        assert self.plan[s]["tag"] == tag, (self.plan[s]["tag"], tag)
        self._issue_loads()
        assert s < self.slab_ld, "slab not loaded (too many slabs held)"
        self.slab_i += 1
        slot = s % NSLOT
        return self.ring[slot], self.b_ring[slot]

    def done_slab(self, n=1):
        self.slab_done += n
        self._issue_loads()

    def dbg_dump(self, name, ap_sb, shape, reads):
        if name not in self.dbg:
            return
        o = self.nc.dram_tensor("dbg_" + name, list(shape), F32, kind="ExternalOutput").ap()
        self.dbg_out[name] = o
        self.k.dma(self.k.sp, self.st, o, ap_sb, reads=reads)

    def dbg_bf(self, name, ap_bf, reads, scratch_f32):
        if name not in self.dbg:
            return
        bb = Buf()
        sc = scratch_f32[:].rearrange("p a b -> p (a b)")
        self.k.op(self.k.dve, lambda: self.nc.vector.tensor_copy(sc, ap_bf), reads=reads, writes=[bb])
        self.dbg_dump(name, sc, [128, 512], [bb])
        self.k.barrier()

    def pvs(self, name, i=0, n=1):
        o, w = PV[name]
        return self.pv[:, o + i:o + i + n]

    def prologue(self):
        nc, k = self.nc, self.k
        xs = self.d_x.rearrange("(c p) t -> p c t", p=128)
        for c in range(FC):
            k.dma(k.sp, self.ld, self.xT[:, c, :], xs[:, c, :], writes=self.b_x[c])
        k.dma(k.sp, self.ld, self.pv[:], self.d_pv, writes=[self.b_pv])
        k.dma(k.pool, self.ldp, self.cb[:], self.d_c.rearrange("(c p) t -> p c t", p=128), writes=[self.b_cb])
        k.op(k.dve, lambda: nc.vector.memset(self.ones_bf[:], 1.0), writes=[self.b_const])
        k.op(k.dve, lambda: nc.vector.memset(self.epsc[:], EPS), writes=[self.b_const])

    def ada(self, m0, m1):
        for m in range(m0, m1, 2):
            self.ada_step(m)

    def ada_step(self, m):
        nc, k = self.nc, self.k
        if self.ada_bank is None:
            self.ada_bank = self.ps_alloc()
            self.ada_mb = m
        pi, mb = self.ada_bank, self.ada_mb
        ps = self.ps[pi]
        slab, bslab = self.take_slab(("ada", m))
        for sub in range(2):
            j = m + sub - mb
            for kc in range(8):
                k.op(k.pe, lambda: nc.tensor.matmul(ps[:, j * 17:(j + 1) * 17], slab[:, kc, sub * 128:(sub + 1) * 128], self.cb[:, kc, :],
                                                    start=(kc == 0), stop=(kc == 7)), reads=[bslab, self.b_cb], writes=[self.b_ps[pi]])
        self.done_slab()
        if m + 2 - mb == 8:
            o, _ = PV["b_ada"]
            bias = self.pv[:, o + mb:o + mb + 8].unsqueeze(2).broadcast_to([128, 8, 17])
            k.op(k.dve, lambda: nc.vector.tensor_tensor(
                self.adaT[:, mb:mb + 8, :], ps[:, 0:136].rearrange("p (m t) -> p m t", m=8), bias, ALU.add),
                reads=[self.b_ps[pi], self.b_pv], writes=[self.b_ada])
            self.ps_free(pi)
            self.ada_bank = None

    def mod_prep(self, n):
        nc, k = self.nc, self.k
        gname = ["n_ffn1", "n_mix", "n_ffn2"][n]
        o, _ = PV[gname]
        g = self.pv[:, o:o + 8].unsqueeze(2).broadcast_to([128, 8, 17])
        sc = self.adaT[:, (3 * n + 1) * 8:(3 * n + 2) * 8, :]
        gt = self.adaT[:, (3 * n + 2) * 8:(3 * n + 3) * 8, :]
        k.op(k.dve, lambda: nc.vector.scalar_tensor_tensor(self.msc[:, n], sc, 1.0, g, ALU.add, ALU.mult),
             reads=[self.b_ada, self.b_pv], writes=[self.b_mod])
        k.op(k.dve, lambda: nc.vector.tensor_scalar(self.gat[:, n], gt, 0.5 if n != 1 else 1.0, None, ALU.mult),
             reads=[self.b_ada], writes=[self.b_mod])

    def norm_mod(self, n, es):
        nc, k = self.nc, self.k
        (sq0, sq1, tmp0, tmp1, rs0, rs1), _ = self.carve(RA0, [([FC, 512], BF16)] * 2 + [([FC, 512], F32)] * 2 + [([512], F32)] * 2)
        sq, tmp, rs = [sq0, sq1], [tmp0, tmp1], [rs0, rs1]
        b_sq = [Buf(), Buf()]; b_tmp = [Buf(), Buf()]; b_rs = [Buf(), Buf()]
        sh0 = (3 * n) * 8
        for t, (c0, w) in enumerate(TILES):
            i = t % 2
            xs = self.xT[:, :, c0:c0 + w]
            bx = [self.b_x[c][t] for c in range(FC)]
            k.op(k.act, lambda: nc.scalar.activation(sq[i][:, :, 0:w], xs, AF.Square), reads=bx, writes=[b_sq[i]])
            pi = self.next_ps()
            for c in range(FC):
                k.op(k.pe, lambda c=c: nc.tensor.matmul(self.ps[pi][:, 0:w], self.ones_bf[:], sq[i][:, c, 0:w],
                                                       start=(c == 0), stop=(c == FC - 1)),
                     reads=[b_sq[i], self.b_const], writes=[self.b_ps[pi]])
            k.op(k.act, lambda: nc.scalar.activation(rs[i][:, 0:w], self.ps[pi][:, 0:w], AF.Ln, bias=self.epsc[:], scale=1.0 / D),
                 reads=[self.b_ps[pi], self.b_const], writes=[b_rs[i]])
            k.op(k.act, lambda: nc.scalar.activation(rs[i][:, 0:w], rs[i][:, 0:w], AF.Exp, scale=-0.5), reads=[b_rs[i]], writes=[b_rs[i]])
            k.op(k.dve, lambda: nc.vector.tensor_tensor(tmp[i][:, :, 0:w], xs, rs[i][:, 0:w].unsqueeze(1).broadcast_to([128, FC, w]), ALU.mult),
                 reads=bx + [b_rs[i]], writes=[b_tmp[i]])
            bh = [self.b_h[c][t] for c in range(FC)]
            if t < 4:
                for c in range(FC):
                    if c % 2 == 0:
                        k.op(k.act, lambda c=c: nc.scalar.activation(
                            self.hT[:, c, c0:c0 + w], tmp[i][:, c, 0:w], AF.Identity,
                            bias=self.adaT[:, sh0 + c, 0:1], scale=self.msc[:, n, c, 0:1]),
                            reads=[b_tmp[i], self.b_ada, self.b_mod], writes=[bh[c]])
                    else:
                        k.op(k.dve, lambda c=c: nc.vector.tensor_scalar(
                            self.hT[:, c, c0:c0 + w], tmp[i][:, c, 0:w], self.msc[:, n, c, 0:1], self.adaT[:, sh0 + c, 0:1], ALU.mult, ALU.add),
                            reads=[b_tmp[i], self.b_ada, self.b_mod], writes=[bh[c]])
            else:
                k.op(k.dve, lambda: nc.vector.tensor_tensor(tmp[i][:, :, 0:w], tmp[i][:, :, 0:w], self.msc[:, n, :, 1:17], ALU.mult),
                     reads=[b_tmp[i], self.b_mod], writes=[b_tmp[i]])
                k.op(k.dve, lambda: nc.vector.tensor_tensor(self.hT[:, :, c0:c0 + w], tmp[i][:, :, 0:w], self.adaT[:, sh0:sh0 + 8, 1:17], ALU.add),
                     reads=[b_tmp[i], self.b_ada], writes=bh)

    def ffn(self, n, es):
        nc, k = self.nc, self.k
        gi = 0 if n == 1 else 2
        act = self.view(RA0, [8, NT], BF16)
        (sg0, sg1, stmp), _ = self.carve(RB0, [([512], F32)] * 2 + [([16], F32)])
        sg = [sg0, sg1]
        b_act = [[Buf() for _ in range(5)] for _ in range(8)]
        b_sg = [Buf(), Buf()]; b_stmp = Buf()
        sgi = 0
        for (i0, i1) in FFN_GROUPS:
            for i in range(i0, i1, 2):
                sl_g, b_g = self.take_slab(("up_g", n, i))
                sl_v, b_v = self.take_slab(("up_v", n, i))
                for t, (c0, w) in enumerate(TILES):
                    bh = [self.b_h[c][t] for c in range(FC)]
                    for sub in range(2):
                        pg, pv_ = self.next_ps(), self.next_ps()
                        for (pi, sl, bs) in ((pg, sl_g, b_g), (pv_, sl_v, b_v)):
                            for kc in range(8):
                                k.op(k.pe, lambda pi=pi, sl=sl, kc=kc: nc.tensor.matmul(
                                    self.ps[pi][:, 0:w], sl[:, kc, sub * 128:(sub + 1) * 128], self.hT[:, kc, c0:c0 + w],
                                    start=(kc == 0), stop=(kc == 7)), reads=[bs] + bh, writes=[self.b_ps[pi]])
                        s_ = sgi % 2
                        sgi += 1
                        k.op(k.act, lambda: nc.scalar.activation(sg[s_][:, 0:w], self.ps[pg][:, 0:w], AF.Silu),
                             reads=[self.b_ps[pg]], writes=[b_sg[s_]])
                        ci = i + sub - i0
                        k.op(k.dve, lambda: nc.vector.tensor_tensor(act[:, ci, c0:c0 + w], sg[s_][:, 0:w], self.ps[pv_][:, 0:w], ALU.mult),
                             reads=[b_sg[s_], self.b_ps[pv_]], writes=[b_act[ci][t]])
                self.done_slab(2)
                if self.ada_pending:
                    self.ada_step(self.ada_pending.pop(0))
            nk = i1 - i0
            for fp in range(4):
                sl_d, b_d = self.take_slab(("down", n, i0, fp))
                for t, (c0, w) in enumerate(TILES):
                    for sub in range(2):
                        fc = fp * 2 + sub
                        pi = self.next_ps()
                        for kc in range(nk):
                            k.op(k.pe, lambda kc=kc: nc.tensor.matmul(
                                self.ps[pi][:, 0:w], sl_d[:, kc, sub * 128:(sub + 1) * 128], act[:, kc, c0:c0 + w],
                                start=(kc == 0), stop=(kc == nk - 1)), reads=[b_d, b_act[kc][t]], writes=[self.b_ps[pi]])
                        xs = self.xT[:, fc, c0:c0 + w]
                        if t < 4:
                            k.op(k.dve, lambda: nc.vector.scalar_tensor_tensor(xs, self.ps[pi][:, 0:w], self.gat[:, gi, fc, 0:1], xs, ALU.mult, ALU.add),
                                 reads=[self.b_ps[pi], self.b_mod, self.b_x[fc][t]], writes=[self.b_x[fc][t]])
                        else:
                            k.op(k.dve, lambda: nc.vector.tensor_tensor(stmp[:], self.ps[pi][:, 0:w], self.gat[:, gi, fc, 1:17], ALU.mult),
                                 reads=[self.b_ps[pi], self.b_mod], writes=[b_stmp])
                            k.op(k.dve, lambda: nc.vector.tensor_tensor(xs, xs, stmp[:], ALU.add),
                                 reads=[b_stmp, self.b_x[fc][t]], writes=[self.b_x[fc][t]])
                self.done_slab()
                if self.ada_pending:
                    self.ada_step(self.ada_pending.pop(0))


    def spill_x(self):
        k = self.k
        self.b_xsp = Buf("xsp")
        k.dma(k.sp, self.st, self.d_xsp, self.xT.rearrange("p c t -> p (c t)"),
              reads=[b for r in self.b_x for b in r], writes=[self.b_xsp])

    def reload_x(self):
        k = self.k
        k.barrier()
        k.dma(k.sp, self.st, self.xT.rearrange("p c t -> p (c t)"), self.d_xsp,
              reads=[self.b_xsp], writes=[b for r in self.b_x for b in r])

    def mix_prologue(self):
        nc, k = self.nc, self.k
        (self.smallT,), _ = self.carve(RC0, [([NSM], F32)])
        self.b_small = Buf("small")
        k.dma(k.sp, self.ld2, self.smallT, self.d_small, writes=[self.b_small])
        self.crT = self.smallT[:, 0:384].rearrange("p (c j s) -> p c j s", c=8, j=3)
        self.h0T = self.smallT[:, 384:512].rearrange("p (c s) -> p c s", c=8)
        self.cqT = self.smallT[:, 512:1664].rearrange("p (c j s) -> p c j s", c=24, j=3)
        rc = self.rgc
        b = self.b_rgc
        k.op(k.act, lambda: nc.scalar.activation(rc[:, 0], self.pvs("lam", 0, 8), AF.Exp, scale=-1.0), reads=[self.b_pv], writes=[b])
        k.op(k.act, lambda: nc.scalar.activation(rc[:, 0], rc[:, 0], AF.Ln, bias=1.0), reads=[b], writes=[b])
        k.op(k.dve, lambda: nc.vector.tensor_scalar(rc[:, 1], rc[:, 0], -4.0, None, ALU.mult), reads=[b], writes=[b])
        k.op(k.dve, lambda: nc.vector.tensor_scalar(rc[:, 0], rc[:, 0], -8.0, None, ALU.mult), reads=[b], writes=[b])
        k.op(k.dve, lambda: nc.vector.tensor_scalar(rc[:, 2], self.pvs("rba", 0, 8), 0.5, None, ALU.mult), reads=[self.b_pv, b], writes=[b])
        k.op(k.dve, lambda: nc.vector.tensor_scalar(rc[:, 3], self.pvs("rbx", 0, 8), 0.5, None, ALU.mult), reads=[self.b_pv, b], writes=[b])

    def branch_a(self):
        nc, k = self.nc, self.k
        oa = self.view(RX0, [8, NT], BF16)
        self.oa = oa
        self.b_oa = [[Buf() for _ in range(5)] for _ in range(8)]
        (xrp, bt, aa, m2, ge), off = self.carve(RX0 + 33024, [([8], F32), ([NT], F32), ([NT], F32), ([NT], F32), ([NT], F32)])
        tl, off2 = self.carve(off, [([512], F32)] * 10 + [([512], BF16)] * 2 + [([16], F32)] * 2 + [([516], BF16)] * 2 + [([4, 128], BF16)])
        assert off2 <= RB0, off2
        preA, dgA = tl[14:16], tl[16]
        b_preA = [Buf(), Buf()]; b_dgA = Buf()
        XC, TR, TI, GRS, SQ = tl[0:2], tl[2:4], tl[4:6], tl[6:8], tl[8:10]
        XCB = tl[10:12]
        xsn, stmp = tl[12], tl[13]
        b_full = {n: [Buf() for _ in range(5)] for n in ("xrp", "bt", "aa", "m2", "ge")}
        b_t = {n: [Buf(), Buf()] for n in ("xc", "tr", "ti", "grs", "sq", "xcb")}
        b_xsn = Buf(); b_stmp = Buf(); b_pad = Buf()
        rc = self.rgc
        k.op(k.dve, lambda: nc.vector.memset(xrp[:, 0:3], 0.0), writes=[b_pad])
        it = 0
        for c in range(8):
            sl_w, b_w = self.take_slab(("rgw", c))
            sl_p, b_p = self.take_slab(("xrgr", c))
            cw = lambda j: self.pvs("crw", c * 4 + j)
            for j_ in range(4):
                k.op(k.dve, lambda: nc.vector.tensor_scalar(dgA[:, j_, :], self.identb[:], cw(j_), None, ALU.mult),
                     reads=[self.b_gc, self.b_pv], writes=[b_dgA])
            for t, (c0, w) in enumerate(TILES):
                i = it % 2
                it += 1
                bh = [self.b_h[kc][t] for kc in range(FC)]
                px, pg = self.next_ps(), self.next_ps()
                for (pi, sub) in ((px, 0), (pg, 1)):
                    for kc in range(8):
                        k.op(k.pe, lambda: nc.tensor.matmul(self.ps[pi][:, 0:w], sl_p[:, kc, sub * 128:(sub + 1) * 128], self.hT[:, kc, c0:c0 + w],
                                                            start=(kc == 0), stop=(kc == 7)), reads=[b_p] + bh, writes=[self.b_ps[pi]])
                xc = XC[i][:, 0:w]
                if t < 4:
                    p_, bp_ = preA[t % 2], b_preA[t % 2]
                    if t == 0:
                        k.op(k.dve, lambda: nc.vector.memset(p_[:, 0:3], 0.0), writes=[bp_])
                    else:
                        k.op(k.dve, lambda: nc.vector.tensor_copy(p_[:, 0:3], preA[(t - 1) % 2][:, 512:515]), reads=[b_preA[(t - 1) % 2]], writes=[bp_])
                    k.op(k.dve, lambda: nc.vector.tensor_copy(p_[:, 3:515], self.ps[px][:, 0:w]), reads=[self.b_ps[px]], writes=[bp_])
                    pc = self.next_ps()
                    for j in range(4):
                        k.op(k.pe, lambda: nc.tensor.matmul(self.ps[pc][:, 0:w], dgA[:, j, :], p_[:, j:j + 512], start=(j == 0), stop=(j == 3)),
                             reads=[b_dgA, bp_], writes=[self.b_ps[pc]])
                    k.op(k.act, lambda: nc.scalar.activation(xc, self.ps[pc][:, 0:w], AF.Identity, bias=self.pvs("crb", c)),
                         reads=[self.b_ps[pc], self.b_pv], writes=[b_t["xc"][i]])
                    k.op(k.dve, lambda: nc.vector.tensor_scalar(XCB[i][:, 0:w], self.ps[pc][:, 0:w], self.pvs("crb", c), None, ALU.add),
                         reads=[self.b_ps[pc], self.b_pv], writes=[b_t["xcb"][i]])
                else:
                    k.op(k.act, lambda: nc.scalar.copy(xsn, self.ps[px][:, 0:w]), reads=[self.b_ps[px]], writes=[b_xsn])
                    k.op(k.dve, lambda: nc.vector.tensor_scalar(xc, xsn, cw(3), self.pvs("crb", c), ALU.mult, ALU.add),
                         reads=[b_xsn, self.b_pv], writes=[b_t["xc"][i]])
                    for j in (2, 1, 0):
                        k.op(k.dve, lambda: nc.vector.scalar_tensor_tensor(xc, self.crT[:, c, j, :], cw(j), xc, ALU.mult, ALU.add),
                             reads=[self.b_small, self.b_pv, b_t["xc"][i]], writes=[b_t["xc"][i]])
                if t == 4:
                    k.op(k.act, lambda: nc.scalar.copy(XCB[i][:, 0:w], xc), reads=[b_t["xc"][i]], writes=[b_t["xcb"][i]])
                pr, pi_ = self.next_ps(), self.next_ps()
                k.op(k.pe, lambda: nc.tensor.matmul(self.ps[pr][:, 0:w], sl_w[:, 0, 0:128], XCB[i][:, 0:w], start=True, stop=True),
                     reads=[b_w, b_t["xcb"][i]], writes=[self.b_ps[pr]])
                k.op(k.pe, lambda: nc.tensor.matmul(self.ps[pi_][:, 0:w], sl_w[:, 0, 128:256], XCB[i][:, 0:w], start=True, stop=True),
                     reads=[b_w, b_t["xcb"][i]], writes=[self.b_ps[pi_]])
                tr, ti = TR[i][:, 0:w], TI[i][:, 0:w]
                k.op(k.act, lambda: nc.scalar.activation(tr, self.ps[pr][:, 0:w], AF.Tanh, bias=rc[:, 2, c:c + 1], scale=0.5),
                     reads=[self.b_ps[pr], self.b_rgc], writes=[b_t["tr"][i]])
                k.op(k.act, lambda: nc.scalar.activation(ti, self.ps[pi_][:, 0:w], AF.Tanh, bias=rc[:, 3, c:c + 1], scale=0.5),
                     reads=[self.b_ps[pi_], self.b_rgc], writes=[b_t["ti"][i]])
                k.op(k.act, lambda: nc.scalar.activation(aa[:, c0:c0 + w], tr, AF.Exp, bias=rc[:, 1, c:c + 1], scale=rc[:, 1, c:c + 1]),
                     reads=[b_t["tr"][i], self.b_rgc], writes=[b_full["aa"][t]])
                k.op(k.act, lambda: nc.scalar.activation(tr, tr, AF.Exp, bias=rc[:, 0, c:c + 1], scale=rc[:, 0, c:c + 1]),
                     reads=[b_t["tr"][i], self.b_rgc], writes=[b_t["tr"][i]])
                k.op(k.dve, lambda: nc.vector.tensor_scalar(m2[:, c0:c0 + w], tr, -0.25, 0.25, ALU.mult, ALU.add),
                     reads=[b_t["tr"][i]], writes=[b_full["m2"][t]])
                k.op(k.dve, lambda: nc.vector.scalar_tensor_tensor(bt[:, c0:c0 + w], ti, 1.0, xc, ALU.add, ALU.mult),
                     reads=[b_t["ti"][i], b_t["xc"][i]], writes=[b_full["bt"][t]])
                grs, sq = GRS[i][:, 0:w], SQ[i][:, 0:w]
                k.op(k.act, lambda: nc.scalar.copy(grs, self.ps[pg][:, 0:w]), reads=[self.b_ps[pg]], writes=[b_t["grs"][i]])
                k.op(k.act, lambda: nc.scalar.activation(sq, self.ps[pg][:, 0:w], AF.Square), reads=[self.b_ps[pg]], writes=[b_t["sq"][i]])
                k.op(k.dve, lambda: nc.vector.tensor_scalar(sq, sq, 0.044715, 1.0, ALU.mult, ALU.add), reads=[b_t["sq"][i]], writes=[b_t["sq"][i]])
                k.op(k.dve, lambda: nc.vector.tensor_tensor(sq, sq, grs, ALU.mult), reads=[b_t["sq"][i], b_t["grs"][i]], writes=[b_t["sq"][i]])
                k.op(k.act, lambda: nc.scalar.activation(sq, sq, AF.Tanh, scale=0.7978845608028654), reads=[b_t["sq"][i]], writes=[b_t["sq"][i]])
                k.op(k.dve, lambda: nc.vector.scalar_tensor_tensor(ge[:, c0:c0 + w], sq, 1.0, grs, ALU.add, ALU.mult),
                     reads=[b_t["sq"][i], b_t["grs"][i]], writes=[b_full["ge"][t]])
            self.done_slab(2)
            allb = lambda n: b_full[n]
            k.op(k.act, lambda: nc.scalar.activation(m2[:, :], m2[:, :], AF.Sqrt), reads=allb("m2"), writes=allb("m2"))
            k.op(k.dve, lambda: nc.vector.memset(m2[:, 0:1], 0.5), reads=allb("m2"), writes=allb("m2"))
            k.op(k.dve, lambda: nc.vector.tensor_tensor(bt[:, :], bt[:, :], m2[:, :], ALU.mult), reads=allb("m2") + allb("bt"), writes=allb("bt"))
            k.op(k.dve, lambda: nc.vector.tensor_tensor_scan(m2[:, 0:LP], aa[:, 0:LP], bt[:, 0:LP], 0.0, ALU.mult, ALU.add),
                 reads=allb("aa") + allb("bt") + allb("m2"), writes=allb("m2"))
            k.op(k.dve, lambda: nc.vector.tensor_tensor(stmp, aa[:, LP:NT], self.h0T[:, c, :], ALU.mult),
                 reads=allb("aa") + [self.b_small], writes=[b_stmp])
            k.op(k.dve, lambda: nc.vector.tensor_tensor(m2[:, LP:NT], bt[:, LP:NT], stmp, ALU.add),
                 reads=allb("bt") + [b_stmp] + allb("m2"), writes=allb("m2"))
            k.op(k.dve, lambda: nc.vector.scalar_tensor_tensor(oa[:, c, :], ge[:, :], 0.5, m2[:, :], ALU.mult, ALU.mult),
                 reads=allb("ge") + allb("m2"), writes=self.b_oa[c])
            k.op(k.act, lambda: nc.scalar.copy(self.hnewT[:, c, 0:1], m2[:, LP - 1:LP]), reads=allb("m2"), writes=[self.b_hnew])
            k.op(k.act, lambda: nc.scalar.copy(self.hnewT[:, c, 1:17], m2[:, LP:NT]), reads=allb("m2"), writes=[self.b_hnew])
        k.barrier()

    def merge(self, br):
        nc, k = self.nc, self.k
        ob = self.oa if br == 0 else self.ob
        b_ob = self.b_oa if br == 0 else self.b_ob
        m = self.view(RA0 if br == 0 else RB0, [8, NT], BF16)
        b_m = [[Buf() for _ in range(5)] for _ in range(8)]
        if br == 0:
            self.m_a, self.b_ma = m, b_m
        else:
            self.m_b, self.b_mb = m, b_m
        (sg0, sg1), _ = self.carve(RC0 + 6656, [([512], F32)] * 2)
        sg = [sg0, sg1]; b_sg = [Buf(), Buf()]
        it = 0
        for j in range(0, 8, 2):
            sl_b, b_b = self.take_slab(("wb", br, j))
            sl_m, b_g = self.take_slab(("mg", br, j))
            for t, (c0, w) in enumerate(TILES):
                bh = [self.b_h[kc][t] for kc in range(FC)]
                for sub in range(2):
                    jj = j + sub
                    py, pm = self.next_ps(), self.next_ps()
                    for kc in range(8):
                        k.op(k.pe, lambda: nc.tensor.matmul(self.ps[py][:, 0:w], sl_b[:, kc, sub * 128:(sub + 1) * 128], ob[:, kc, c0:c0 + w],
                                                            start=(kc == 0), stop=(kc == 7)), reads=[b_b, b_ob[kc][t]], writes=[self.b_ps[py]])
                    for kc in range(8):
                        k.op(k.pe, lambda: nc.tensor.matmul(self.ps[pm][:, 0:w], sl_m[:, kc, sub * 128:(sub + 1) * 128], self.hT[:, kc, c0:c0 + w],
                                                            start=(kc == 0), stop=(kc == 7)), reads=[b_g] + bh, writes=[self.b_ps[pm]])
                    i = it % 2
                    it += 1
                    k.op(k.act, lambda: nc.scalar.activation(sg[i][:, 0:w], self.ps[pm][:, 0:w], AF.Sigmoid), reads=[self.b_ps[pm]], writes=[b_sg[i]])
                    k.op(k.dve, lambda: nc.vector.tensor_tensor(m[:, jj, c0:c0 + w], sg[i][:, 0:w], self.ps[py][:, 0:w], ALU.mult),
                         reads=[b_sg[i], self.b_ps[py]], writes=[b_m[jj][t]])
            self.done_slab(2)
        k.barrier()

    def out_proj(self, branches=(0, 1)):
        nc, k = self.nc, self.k
        (stmp,), _ = self.carve(RC0 + 6656, [([16], F32)])
        b_stmp = Buf()
        ms = [(self.m_a, self.b_ma), (self.m_b, self.b_mb)] if len(branches) == 2 else [(self.m_a, self.b_ma)]
        for j in range(0, 8, 2):
            sl, b_s = self.take_slab(("wo", j))
            for t, (c0, w) in enumerate(TILES):
                for sub in range(2):
                    fc = j + sub
                    pi = self.next_ps()
                    n = 8 * len(ms)
                    q = 0
                    for (m, b_m) in ms:
                        for kc in range(8):
                            k.op(k.pe, lambda: nc.tensor.matmul(self.ps[pi][:, 0:w], sl[:, kc, sub * 128:(sub + 1) * 128], m[:, kc, c0:c0 + w],
                                                                start=(q == 0), stop=(q == n - 1)), reads=[b_s, b_m[kc][t]], writes=[self.b_ps[pi]])
                            q += 1
                    xs = self.xT[:, fc, c0:c0 + w]
                    if t < 4:
                        k.op(k.dve, lambda: nc.vector.scalar_tensor_tensor(xs, self.ps[pi][:, 0:w], self.gat[:, 1, fc, 0:1], xs, ALU.mult, ALU.add),
                             reads=[self.b_ps[pi], self.b_mod, self.b_x[fc][t]], writes=[self.b_x[fc][t]])
                    else:
                        k.op(k.dve, lambda: nc.vector.tensor_tensor(stmp, self.ps[pi][:, 0:w], self.gat[:, 1, fc, 1:17], ALU.mult),
                             reads=[self.b_ps[pi], self.b_mod], writes=[b_stmp])
                        k.op(k.dve, lambda: nc.vector.tensor_tensor(xs, xs, stmp, ALU.add),
                             reads=[b_stmp, self.b_x[fc][t]], writes=[self.b_x[fc][t]])
            self.done_slab()
        k.barrier()

    def dump_bf(self, name, ap, bufs):
        if name not in self.dbg:
            return
        nc, k = self.nc, self.k
        o = nc.dram_tensor("dbg_" + name, [D, NT], F32, kind="ExternalOutput").ap()
        self.dbg_out[name] = o
        k.barrier()
        hf = self.view(RA0, [FC, NT], F32)
        bb = Buf()
        k.op(k.dve, lambda: nc.vector.tensor_copy(hf, ap), reads=bufs, writes=[bb])
        k.dma(k.sp, self.st, o.rearrange("(c p) t -> p c t", p=128), hf, reads=[bb])
        k.barrier()


    def gdn_consts(self):
        nc, k = self.nc, self.k
        al = lambda n, sh, dt_: nc.alloc_sbuf_tensor(n, sh, dt_)
        self.ident_f = al("ident_f", [128, 128], F32); self.identb = al("identb", [128, 128], BF16)
        self.uinc_b = al("uinc_b", [128, 128], BF16); self.sm_b = al("sm_b", [128, 128], BF16); self.negs_b = al("negs_b", [128, 128], BF16)
        self.ones_f = al("ones_f", [128, 128], F32)
        self.nA = al("nA", [8, 1], F32)
        self.cmk = al("cmk", [128, 7, 128], BF16)
        b = self.b_gc = Buf("gdnconst")
        k.dma(k.pool, self.ldp, self.cmk[:], self.d_cmk.rearrange("p (l j) -> p l j", l=7), writes=[b])
        BIGN = -30000.0
        k.op(k.pool, lambda: nc.gpsimd.memset(self.ones_f[:], 1.0), writes=[b])
        k.op(k.pool, lambda: nc.gpsimd.affine_select(self.ident_f[:], self.ones_f[:], [[-1, 128]], ALU.is_equal, 0.0, base=0, channel_multiplier=1), reads=[b], writes=[b])
        k.op(k.pool, lambda: nc.gpsimd.affine_select(self.uinc_b[:], self.ones_f[:], [[1, 128]], ALU.is_ge, 0.0, base=0, channel_multiplier=-1), reads=[b], writes=[b])
        k.op(k.pool, lambda: nc.gpsimd.affine_select(self.sm_b[:], self.ones_f[:], [[-1, 128]], ALU.is_gt, 0.0, base=0, channel_multiplier=1), reads=[b], writes=[b])
        k.op(k.pool, lambda: nc.gpsimd.tensor_scalar(self.negs_b[:], self.uinc_b[:], BIGN, None, ALU.mult), reads=[b], writes=[b])
        k.op(k.dve, lambda: nc.vector.tensor_copy(self.identb[:], self.ident_f[:]), reads=[b], writes=[b])
        k.op(k.act, lambda: nc.scalar.activation(self.nA[:], self.pv[0:8, PV["alog"][0]:PV["alog"][0] + 1], AF.Exp), reads=[self.b_pv, b], writes=[b])
        k.op(k.dve, lambda: nc.vector.tensor_scalar(self.nA[:], self.nA[:], -1.0, None, ALU.mult), reads=[b], writes=[b])

    def gdn_prologue(self):
        nc, k = self.nc, self.k
        base = RX0
        (self.obst0, self.obs_s), base = self.carve(base, [([NT], BF16)] + [([8, 16], BF16)])
        self.obst = [self.obst0, self.obst0]
        _bo = Buf()
        self.b_obst = [_bo, _bo]
        self.ch_obsp = [k.chan("obsp0"), k.chan("obsp1")]
        self.b_obsp = [Buf() for _ in range(9)]
        (self.cols, self.egrow, self.srow, self.ssave, self.osave), off = self.carve(
            base, [([16, 56], F32), ([NT], F32), ([64], F32), ([8, 80], F32), ([8, 16], F32)])
        self.b_ssave = [Buf() for _ in range(8)]
        self.b_osave = Buf()
        self.gdn_off = off
        self.b_cols = Buf("cols"); self.b_eg = Buf("egrow"); self.b_srow = Buf("srow")
        (g, beta, gcum, glb, egl, begr, eglb, ghi, glo), off2 = self.carve(off, [([NT], F32)] * 9)
        (gbf,), off2 = self.carve(off2, [([LP], BF16)])
        bgh, bgl_, bgb = Buf(), Buf(), Buf()
        assert off2 <= RC0
        eg = self.egrow
        bg, bb, bc, bl, bel, bbe, bgl = [Buf() for _ in range(7)]
        sl, b_s = self.take_slab(("ab",))
        dtb = self.pv[0:8, PV["dtb"][0]:PV["dtb"][0] + 1]
        for t, (c0, w) in enumerate(TILES):
            bh = [self.b_h[kc][t] for kc in range(FC)]
            pa, pb = self.next_ps(), self.next_ps()
            for (pi, o) in ((pa, 0), (pb, 8)):
                for kc in range(8):
                    k.op(k.pe, lambda: nc.tensor.matmul(self.ps[pi][0:8, 0:w], sl[:, kc, o:o + 8], self.hT[:, kc, c0:c0 + w],
                                                        start=(kc == 0), stop=(kc == 7)), reads=[b_s] + bh, writes=[self.b_ps[pi]])
            k.op(k.act, lambda: nc.scalar.activation(g[0:8, c0:c0 + w], self.ps[pa][0:8, 0:w], AF.Exp, bias=dtb), reads=[self.b_ps[pa], self.b_pv], writes=[bg])
            k.op(k.act, lambda: nc.scalar.activation(beta[0:8, c0:c0 + w], self.ps[pb][0:8, 0:w], AF.Sigmoid), reads=[self.b_ps[pb]], writes=[bb])
        self.done_slab()
        k.op(k.act, lambda: nc.scalar.activation(g[0:8, :], g[0:8, :], AF.Ln, bias=1.0), reads=[bg], writes=[bg])
        k.op(k.dve, lambda: nc.vector.tensor_scalar(g[0:8, :], g[0:8, :], self.nA[:], None, ALU.mult), reads=[bg, self.b_gc], writes=[bg])
        for n in range(16):
            cs = slice(n * 128, (n + 1) * 128)
            k.op(k.dve, lambda: nc.vector.tensor_tensor_scan(gcum[0:8, cs], self.ones_f[0:8, :], g[0:8, cs], 0.0, ALU.mult, ALU.add),
                 reads=[bg, self.b_gc], writes=[bc])
        g3 = lambda a: a[0:8, 0:LP].rearrange("p (n j) -> p n j", n=16)
        k.op(k.dve, lambda: nc.vector.tensor_copy(g3(glb), g3(gcum)[:, :, 127:128].broadcast_to([8, 16, 128])), reads=[bc], writes=[bl])
        k.op(k.act, lambda: nc.scalar.activation(eg[0:8, 0:LP], gcum[0:8, 0:LP], AF.Exp), reads=[bc], writes=[self.b_eg])
        k.op(k.dve, lambda: nc.vector.tensor_tensor(egl[0:8, 0:LP], glb[0:8, 0:LP], gcum[0:8, 0:LP], ALU.subtract), reads=[bl, bc], writes=[bel])
        k.op(k.act, lambda: nc.scalar.activation(egl[0:8, 0:LP], egl[0:8, 0:LP], AF.Exp), reads=[bel], writes=[bel])
        k.op(k.dve, lambda: nc.vector.tensor_tensor(begr[0:8, 0:LP], beta[0:8, 0:LP], eg[0:8, 0:LP], ALU.mult), reads=[bb, self.b_eg], writes=[bbe])
        k.op(k.dve, lambda: nc.vector.tensor_copy(g3(eglb), g3(eg)[:, :, 127:128].broadcast_to([8, 16, 128])), reads=[self.b_eg], writes=[bgl])
        k.op(k.act, lambda: nc.scalar.activation(self.srow[0:8, 0:16], g[0:8, LP:NT], AF.Exp), reads=[bg], writes=[self.b_srow])
        k.op(k.dve, lambda: nc.vector.tensor_copy(self.srow[0:8, 16:32], beta[0:8, LP:NT]), reads=[bb, self.b_srow], writes=[self.b_srow])
        k.op(k.dve, lambda: nc.vector.tensor_copy(gbf[0:8, :], g[0:8, 0:LP]), reads=[bg], writes=[bgb])
        k.op(k.dve, lambda: nc.vector.tensor_copy(ghi[0:8, 0:LP], gbf[0:8, :]), reads=[bgb], writes=[bgh])
        k.op(k.dve, lambda: nc.vector.tensor_tensor(glo[0:8, 0:LP], g[0:8, 0:LP], ghi[0:8, 0:LP], ALU.subtract), reads=[bg, bgh], writes=[bgl_])
        for nq in range(4):
            pi = self.next_ps()
            for nn in range(4):
                n = nq * 4 + nn
                for q, (rw, br) in enumerate(((g, bg), (beta, bb), (begr, bbe), (egl, bel), (eglb, bgl), (ghi, bgh), (glo, bgl_))):
                    k.op(k.pe, lambda: nc.tensor.transpose(self.ps[pi][:, nn * 56 + q * 8: nn * 56 + q * 8 + 8], rw[0:8, n * 128:(n + 1) * 128], self.ident_f[0:8, 0:8]),
                         reads=[br, self.b_gc], writes=[self.b_ps[pi]])
            k.op(k.act, lambda: nc.scalar.copy(self.cols[:, nq * 4:nq * 4 + 4, :], self.ps[pi][:, 0:224].rearrange("p (n q) -> p n q", n=4)),
                 reads=[self.b_ps[pi]], writes=[self.b_cols])
        k.barrier()

    def gdn_all(self):
        nc, k = self.nc, self.k
        off = self.gdn_off
        (qn, oT), off = self.carve(off, [([NT], F32)] * 2)
        (zs0, zs1, qT, kT), off = self.carve(off, [([NT], BF16)] * 4)
        (qdT, vT), off = self.carve(off, [([LP], BF16)] * 2)
        zsb = [zs0, zs1]
        (S, Sbf, vnew, egm), off = self.carve(off, [([128], F32), ([128], BF16), ([128], BF16), ([512], F32)])
        pool0 = off
        b_qn = [Buf() for _ in range(5)]; b_oT = [Buf() for _ in range(4)]; b_kn = [Buf() for _ in range(5)]; b_vs = [Buf() for _ in range(5)]
        b_zsb = [[Buf() for _ in range(5)] for _ in range(2)]
        b_qT = [Buf() for _ in range(5)]; b_kT = [Buf() for _ in range(5)]; b_qd = [Buf() for _ in range(4)]; b_vT = [Buf() for _ in range(4)]
        b_S, b_Sbf, b_vnew, b_egm = [Buf() for _ in range(4)]
        col_h = lambda h: (lambda n, q: self.cols[:, n, q * 8 + h:q * 8 + h + 1])
        psb = lambda pi: self.ps[pi][:].bitcast(BF16)
        flat = lambda a: a[:].rearrange("p a b -> p (a b)")
        sub = lambda jj: slice(jj * 128, (jj + 1) * 128)
        mk = lambda l: self.cmk[:, l, :].unsqueeze(1).broadcast_to([128, 4, 128])
        idb4 = self.identb[:].unsqueeze(1).broadcast_to([128, 4, 128])
        recin = []
        o_ = pool0
        for _q in range(4):
            st = {}
            (st['attnT'], st['kd'], st['wT']), o_ = self.carve(o_, [([4, 128], BF16)] * 3)
            (st['u'],), o_ = self.carve(o_, [([4, 128], F32)])
            recin.append(st)
        work0 = o_

        def make_shared(o):
            sh = {}
            (sh["DN"], sh["DT"]), o = self.carve(o, [([4, 128], F32)] * 2)
            (sh["GUh"], sh["GUl"]), o = self.carve(o, [([4, 128], BF16)] * 2)
            for n_ in list(sh.keys()):
                sh["b_" + n_] = Buf()
            return sh, o

        def make_set(o, sh, rin):
            st = dict(sh)
            st.update(rin)
            names = ["L0", "NO", "X", "V", "M0", "M1", "kbg", "bv"]
            vs_, o = self.carve(o, [([4, 128], BF16)] * len(names))
            for n_, v_ in zip(names, vs_):
                st[n_] = v_
            for n_ in names + ["attnT", "kd", "wT", "u"]:
                st["b_" + n_] = Buf()
            return st, o

        sh0, e2 = make_shared(work0)
        sh1, e2 = make_shared(e2)
        sets = []
        for _q in range(4):
            st, e2 = make_set(e2, sh0 if _q % 2 == 0 else sh1, recin[_q])
            sets.append(st)
        assert e2 <= RC0, e2

        (kn, vs, pre0, pre1, pre2, pre3, pre4, pre5, xc0, xc1, sq0, sq1, rs0, rs1, xsn, dg), e1 = self.carve(
            work0, [([NT], F32)] * 2 + [([516], BF16)] * 6 + [([512], F32)] * 2 + [([512], BF16)] * 2 + [([512], F32)] * 2 + [([16], F32)] + [([12, 128], BF16)])
        b_dg = Buf()
        assert e1 <= RC0, e1
        pre = [[pre0, pre1], [pre2, pre3], [pre4, pre5]]
        b_pre = [[Buf(), Buf()] for _ in range(3)]
        xcb = [xc0, xc1]; b_xc = [Buf(), Buf()]
        sq = [sq0, sq1]; b_sq = [Buf(), Buf()]; rs = [rs0, rs1]; b_rs = [Buf(), Buf()]
        b_xsn = Buf()

        def front(h):
            zs, b_zs = zsb[h % 2], b_zsb[h % 2]
            for qi_ in range(3):
                for j_ in range(4):
                    k.op(k.dve, lambda: nc.vector.tensor_scalar(dg[:, qi_ * 4 + j_, :], self.identb[:], self.pvs("cqw", (qi_ * 8 + h) * 4 + j_), None, ALU.mult),
                         reads=[self.b_gc, self.b_pv], writes=[b_dg])
            sl_qk, b_qk = self.take_slab(("qk", h))
            sl_vz, b_vz = self.take_slab(("vz", h))
            dst = [(qn, b_qn), (kn, b_kn), (vs, b_vs)]
            xi = 0
            for t, (c0, w) in enumerate(TILES):
                bh = [self.b_h[kc][t] for kc in range(FC)]
                pss = [self.ps_alloc() for _ in range(4)]
                for (pi, sl, bs, sub) in ((pss[0], sl_qk, b_qk, 0), (pss[1], sl_qk, b_qk, 1), (pss[2], sl_vz, b_vz, 0), (pss[3], sl_vz, b_vz, 1)):
                    for kc in range(8):
                        k.op(k.pe, lambda: nc.tensor.matmul(self.ps[pi][:, 0:w], sl[:, kc, sub * 128:(sub + 1) * 128], self.hT[:, kc, c0:c0 + w],
                                                            start=(kc == 0), stop=(kc == 7)), reads=[bs] + bh, writes=[self.b_ps[pi]])
                        if kc == 3:
                            yield
                    yield
                for qi in range(3):
                    ch = qi * 8 + h
                    cw = lambda j: self.pvs("cqw", ch * 4 + j)
                    pi = pss[qi]
                    x_ = xcb[xi % 2]; bx_ = b_xc[xi % 2]
                    xi += 1
                    d_, bd_ = dst[qi]
                    if t < 4:
                        p_ = pre[qi][t % 2]; bp_ = b_pre[qi][t % 2]
                        if t == 0:
                            k.op(k.dve, lambda: nc.vector.memset(p_[:, 0:3], 0.0), writes=[bp_])
                        else:
                            k.op(k.dve, lambda: nc.vector.tensor_copy(p_[:, 0:3], pre[qi][(t - 1) % 2][:, 512:515]),
                                 reads=[b_pre[qi][(t - 1) % 2]], writes=[bp_])
                        if qi == 1:
                            k.op(k.act, lambda: nc.scalar.copy(p_[:, 3:515], self.ps[pi][:, 0:w]), reads=[self.b_ps[pi]], writes=[bp_])
                        else:
                            k.op(k.dve, lambda: nc.vector.tensor_copy(p_[:, 3:515], self.ps[pi][:, 0:w]), reads=[self.b_ps[pi]], writes=[bp_])
                        pc = self.ps_alloc()
                        for j in range(4):
                            k.op(k.pe, lambda: nc.tensor.matmul(self.ps[pc][:, 0:w], dg[:, qi * 4 + j, :], p_[:, j:j + 512], start=(j == 0), stop=(j == 3)),
                                 reads=[b_dg, bp_], writes=[self.b_ps[pc]])
                        k.op(k.act, lambda: nc.scalar.activation(d_[:, c0:c0 + w], self.ps[pc][:, 0:w], AF.Silu), reads=[self.b_ps[pc]], writes=[bd_[t]])
                        self.ps_free(pc)
                        if qi < 2:
                            self.ps_free(pi)
                        if qi == 2:
                            k.op(k.dve, lambda: nc.vector.tensor_copy(vT[:, c0:c0 + w], d_[:, c0:c0 + w]), reads=[bd_[t]], writes=[b_vT[t]])
                        yield
                        continue
                    else:
                        k.op(k.act, lambda: nc.scalar.copy(xsn, self.ps[pi][:, 0:w]), reads=[self.b_ps[pi]], writes=[b_xsn])
                        k.op(k.dve, lambda: nc.vector.tensor_scalar(x_[:, 0:w], xsn, cw(3), None, ALU.mult), reads=[b_xsn, self.b_pv], writes=[bx_])
                        for j in (2, 1, 0):
                            k.op(k.dve, lambda: nc.vector.scalar_tensor_tensor(x_[:, 0:w], self.cqT[:, ch, j, :], cw(j), x_[:, 0:w], ALU.mult, ALU.add),
                                 reads=[self.b_small, self.b_pv, bx_], writes=[bx_])
                    k.op(k.act, lambda: nc.scalar.activation(d_[:, c0:c0 + w], x_[:, 0:w], AF.Silu), reads=[bx_], writes=[bd_[t]])
                    if qi == 2 and t < 4:
                        k.op(k.dve, lambda: nc.vector.tensor_copy(vT[:, c0:c0 + w], d_[:, c0:c0 + w]), reads=[bd_[t]], writes=[b_vT[t]])
                k.op(k.act, lambda: nc.scalar.activation(zs[:, c0:c0 + w], self.ps[pss[3]][:, 0:w], AF.Silu), reads=[self.b_ps[pss[3]]], writes=[b_zs[t]])
                if t == 4:
                    self.ps_free(pss[0], pss[1])
                self.ps_free(pss[2], pss[3])
                yield
            self.done_slab(2)

        def mid(h):
            xi = 0
            for t, (c0, w) in enumerate(TILES):
                for (src, bsrc, dT, bdT, scl) in ((qn, b_qn, qT, b_qT, 128.0 ** -0.5), (kn, b_kn, kT, b_kT, 1.0)):
                    i = xi % 2
                    xi += 1
                    k.op(k.act, lambda: nc.scalar.activation(sq[i][:, 0:w], src[:, c0:c0 + w], AF.Square), reads=[bsrc[t]], writes=[b_sq[i]])
                    pi = self.next_ps()
                    k.op(k.pe, lambda: nc.tensor.matmul(self.ps[pi][:, 0:w], self.ones_bf[:], sq[i][:, 0:w], start=True, stop=True),
                         reads=[b_sq[i], self.b_const], writes=[self.b_ps[pi]])
                    k.op(k.act, lambda: nc.scalar.activation(rs[i][:, 0:w], self.ps[pi][:, 0:w], AF.Ln, bias=self.epsc[:], scale=1.0),
                         reads=[self.b_ps[pi], self.b_const], writes=[b_rs[i]])
                    k.op(k.act, lambda: nc.scalar.activation(rs[i][:, 0:w], rs[i][:, 0:w], AF.Exp, scale=-0.5), reads=[b_rs[i]], writes=[b_rs[i]])
                    k.op(k.dve, lambda: nc.vector.scalar_tensor_tensor(src[:, c0:c0 + w], src[:, c0:c0 + w], scl, rs[i][:, 0:w], ALU.mult, ALU.mult),
                         reads=[bsrc[t], b_rs[i]], writes=[bsrc[t]])
                    k.op(k.dve, lambda: nc.vector.tensor_copy(dT[:, c0:c0 + w], src[:, c0:c0 + w]), reads=[bsrc[t]], writes=[bdT[t]])
                    yield
            for t in range(4):
                c0 = t * 512
                k.op(k.dve, lambda: nc.vector.tensor_scalar(egm[0:8, :], self.egrow[0:8, c0:c0 + 512], self.ident_f[0:8, h:h + 1], None, ALU.mult),
                     reads=[self.b_eg, self.b_gc], writes=[b_egm])
                pi = self.next_ps()
                k.op(k.pe, lambda: nc.tensor.matmul(self.ps[pi][:, :], self.ones_f[0:8, :], egm[0:8, :], start=True, stop=True),
                     reads=[b_egm, self.b_gc], writes=[self.b_ps[pi]])
                k.op(k.dve, lambda: nc.vector.tensor_tensor(qdT[:, c0:c0 + 512], qn[:, c0:c0 + 512], self.ps[pi][:, :], ALU.mult),
                     reads=[b_qn[t], self.b_ps[pi]], writes=[b_qd[t]])
                yield
            ssv = self.ssave[:, h, :]
            kq_sv = ssv[:, 0:32].rearrange("p (s two) -> p s two", two=2)
            bsv = self.b_ssave[h]
            k.op(k.act, lambda: nc.scalar.copy(kq_sv[:, :, 0], kn[:, LP:NT]), reads=[b_kn[4]], writes=[bsv])
            k.op(k.act, lambda: nc.scalar.copy(kq_sv[:, :, 1], qn[:, LP:NT]), reads=[b_qn[4], bsv], writes=[bsv])
            k.op(k.act, lambda: nc.scalar.copy(ssv[:, 32:48], kn[:, LP:NT]), reads=[b_kn[4], bsv], writes=[bsv])
            k.op(k.act, lambda: nc.scalar.copy(ssv[:, 48:64], vs[:, LP:NT]), reads=[b_vs[4], bsv], writes=[bsv])
            k.op(k.act, lambda: nc.scalar.copy(ssv[:, 64:80], zsb[h % 2][:, LP:NT]), reads=[b_zsb[h % 2][4], bsv], writes=[bsv])
            if "gdn_qk" in self.dbg and h == 0:
                self.dbg_dump("qn0", qn, [128, NT], b_qn); self.dbg_dump("kn0", kn, [128, NT], b_kn); self.dbg_dump("vs0", vs, [128, NT], b_vs)
            yield

        def prep(h, Q, st):
            col = col_h(h)
            ch = lambda jj: slice((Q * 4 + jj) * 128, (Q * 4 + jj + 1) * 128)
            DN, DT, L0, NO, Xb, Vb = st["DN"], st["DT"], st["L0"], st["NO"], st["X"], st["V"]
            GUh, GUl = st["GUh"], st["GUl"]
            Mb = [st["M0"], st["M1"]]; b_M = [st["b_M0"], st["b_M1"]]
            for jj in range(4):
                k.op(k.act, lambda: nc.scalar.activation(GUh[:, jj, :], self.uinc_b[:], AF.Identity, scale=col(Q * 4 + jj, 5)),
                     reads=[self.b_cols, self.b_gc], writes=[st["b_GUh"]])
                k.op(k.act, lambda: nc.scalar.activation(GUl[:, jj, :], self.uinc_b[:], AF.Identity, scale=col(Q * 4 + jj, 6)),
                     reads=[self.b_cols, self.b_gc], writes=[st["b_GUl"]])
            pEN = self.ps_alloc()
            for jj in range(4):
                k.op(k.pe, lambda: nc.tensor.matmul(self.ps[pEN][:, sub(jj)], GUh[:, jj, :], self.sm_b[:], start=True, stop=False),
                     reads=[st["b_GUh"], self.b_gc], writes=[self.b_ps[pEN]])
                k.op(k.pe, lambda: nc.tensor.matmul(self.ps[pEN][:, sub(jj)], GUl[:, jj, :], self.sm_b[:], start=False, stop=False),
                     reads=[st["b_GUl"], self.b_gc], writes=[self.b_ps[pEN]])
                k.op(k.pe, lambda: nc.tensor.matmul(self.ps[pEN][:, sub(jj)], self.identb[:], self.negs_b[:], start=False, stop=True),
                     reads=[self.b_gc], writes=[self.b_ps[pEN]])
            k.op(k.act, lambda: nc.scalar.activation(flat(DN), self.ps[pEN][:, :], AF.Exp), reads=[self.b_ps[pEN]], writes=[st["b_DN"]])
            self.ps_free(pEN)
            pDT = self.ps_alloc()
            for jj in range(4):
                k.op(k.pe, lambda: nc.tensor.transpose(self.ps[pDT][:, sub(jj)], DN[:, jj, :], self.ident_f[:]), reads=[st["b_DN"], self.b_gc], writes=[self.b_ps[pDT]])
            k.op(k.dve, lambda: nc.vector.tensor_tensor(DT[:], self.ps[pDT][:, :].rearrange("p (a b) -> p a b", a=4),
                                                        self.ident_f[:].unsqueeze(1).broadcast_to([128, 4, 128]), ALU.add),
                 reads=[self.b_ps[pDT], self.b_gc], writes=[st["b_DT"]])
            self.ps_free(pDT)
            pKK, pQK = self.ps_alloc(), self.ps_alloc()
            for jj in range(4):
                k.op(k.pe, lambda: nc.tensor.matmul(self.ps[pKK][:, sub(jj)], kT[:, ch(jj)], kT[:, ch(jj)], start=True, stop=True),
                     reads=[b_kT[Q]], writes=[self.b_ps[pKK]])
                k.op(k.pe, lambda: nc.tensor.matmul(self.ps[pQK][:, sub(jj)], kT[:, ch(jj)], qT[:, ch(jj)], start=True, stop=True),
                     reads=[b_kT[Q], b_qT[Q]], writes=[self.b_ps[pQK]])
            for jj in range(4):
                k.op(k.dve, lambda: nc.vector.scalar_tensor_tensor(L0[:, jj, :], self.ps[pKK][:, sub(jj)], col(Q * 4 + jj, 1), DN[:, jj, :], ALU.mult, ALU.mult),
                     reads=[self.b_ps[pKK], self.b_cols, st["b_DN"]], writes=[st["b_L0"]])
            k.op(k.dve, lambda: nc.vector.tensor_tensor(flat(st["attnT"]), self.ps[pQK][:, :], flat(DT), ALU.mult),
                 reads=[self.b_ps[pQK], st["b_DT"]], writes=[st["b_attnT"]])
            self.ps_free(pKK, pQK)
            yield
            pk = self.ps_alloc()
            for jj in range(4):
                k.op(k.pe, lambda: nc.tensor.transpose(psb(pk)[:, sub(jj)], kT[:, ch(jj)], self.identb[:]), reads=[b_kT[Q], self.b_gc], writes=[self.b_ps[pk]])
                k.op(k.pe, lambda: nc.tensor.transpose(psb(pk)[:, 512 + jj * 128:512 + (jj + 1) * 128], vT[:, ch(jj)], self.identb[:]),
                     reads=[b_vT[Q], self.b_gc], writes=[self.b_ps[pk]])
            for jj in range(4):
                n = Q * 4 + jj
                k.op(k.act, lambda: nc.scalar.activation(st["kbg"][:, jj, :], psb(pk)[:, sub(jj)], AF.Identity, scale=col(n, 2)),
                     reads=[self.b_ps[pk], self.b_cols], writes=[st["b_kbg"]])
                k.op(k.act, lambda: nc.scalar.activation(st["kd"][:, jj, :], psb(pk)[:, sub(jj)], AF.Identity, scale=col(n, 3)),
                     reads=[self.b_ps[pk], self.b_cols], writes=[st["b_kd"]])
                k.op(k.act, lambda: nc.scalar.activation(st["bv"][:, jj, :], psb(pk)[:, 512 + jj * 128:512 + (jj + 1) * 128], AF.Identity, scale=col(n, 1)),
                     reads=[self.b_ps[pk], self.b_cols], writes=[st["b_bv"]])
            self.ps_free(pk)
            yield
            k.op(k.dve, lambda: nc.vector.tensor_tensor(NO[:], L0[:], mk(0), ALU.mult), reads=[st["b_L0"], self.b_gc], writes=[st["b_NO"]])
            k.op(k.dve, lambda: nc.vector.tensor_tensor(Xb[:], idb4, NO[:], ALU.add), reads=[st["b_NO"], self.b_gc], writes=[st["b_X"]])
            pT = self.ps_alloc()
            for jj in range(4):
                k.op(k.pe, lambda: nc.tensor.transpose(psb(pT)[:, sub(jj)], Xb[:, jj, :], self.identb[:]), reads=[st["b_X"], self.b_gc], writes=[self.b_ps[pT]])
            k.op(k.act, lambda: nc.scalar.copy(flat(Mb[0]), psb(pT)[:, 0:512]), reads=[self.b_ps[pT]], writes=[b_M[0]])
            self.ps_free(pT)
            yield
            cur = 0
            for l in range(1, 7):
                k.op(k.pool, lambda: nc.gpsimd.tensor_tensor(NO[:], L0[:], mk(l), ALU.mult), reads=[st["b_L0"], self.b_gc], writes=[st["b_NO"]])
                pV = self.ps_alloc()
                for jj in range(4):
                    k.op(k.pe, lambda: nc.tensor.matmul(self.ps[pV][:, sub(jj)], NO[:, jj, :], Mb[cur][:, jj, :], start=True, stop=True),
                         reads=[st["b_NO"], b_M[cur]], writes=[self.b_ps[pV]])
                if l > 1:
                    pT = self.ps_alloc()
                    for jj in range(4):
                        k.op(k.pe, lambda: nc.tensor.transpose(psb(pT)[:, sub(jj)], Mb[cur][:, jj, :], self.identb[:]),
                             reads=[b_M[cur], self.b_gc], writes=[self.b_ps[pT]])
                    k.op(k.act, lambda: nc.scalar.copy(flat(Xb), psb(pT)[:, 0:512]), reads=[self.b_ps[pT]], writes=[st["b_X"]])
                    self.ps_free(pT)
                if Q % 2 == 0:
                    k.op(k.dve, lambda: nc.vector.tensor_copy(flat(Vb), self.ps[pV][:, :]), reads=[self.b_ps[pV]], writes=[st["b_V"]])
                else:
                    k.op(k.act, lambda: nc.scalar.copy(flat(Vb), self.ps[pV][:, :]), reads=[self.b_ps[pV]], writes=[st["b_V"]])
                self.ps_free(pV)
                yield
                pM = self.ps_alloc()
                for jj in range(4):
                    k.op(k.pe, lambda: nc.tensor.matmul(self.ps[pM][:, sub(jj)], Xb[:, jj, :], Vb[:, jj, :], start=True, stop=True),
                         reads=[st["b_X"], st["b_V"]], writes=[self.b_ps[pM]])
                k.op(k.dve, lambda: nc.vector.tensor_tensor(flat(Mb[1 - cur]), self.ps[pM][:, :], flat(Mb[cur]), ALU.add),
                     reads=[self.b_ps[pM], b_M[cur]], writes=[b_M[1 - cur]])
                self.ps_free(pM)
                cur = 1 - cur
                yield
            XT, b_XT = Mb[cur], b_M[cur]
            pu, pw = self.ps_alloc(), self.ps_alloc()
            for jj in range(4):
                k.op(k.pe, lambda: nc.tensor.matmul(self.ps[pu][:, sub(jj)], XT[:, jj, :], st["bv"][:, jj, :], start=True, stop=True),
                     reads=[b_XT, st["b_bv"]], writes=[self.b_ps[pu]])
                k.op(k.pe, lambda: nc.tensor.matmul(self.ps[pw][:, sub(jj)], st["kbg"][:, jj, :], XT[:, jj, :], start=True, stop=True),
                     reads=[b_XT, st["b_kbg"]], writes=[self.b_ps[pw]])
            k.op(k.act, lambda: nc.scalar.copy(flat(st["u"]), self.ps[pu][:, :]), reads=[self.b_ps[pu]], writes=[st["b_u"]])
            k.op(k.dve, lambda: nc.vector.tensor_copy(flat(st["wT"]), self.ps[pw][:, :]), reads=[self.b_ps[pw]], writes=[st["b_wT"]])
            self.ps_free(pu, pw)
            yield

        def rec(h, Q, st):
            col = col_h(h)
            ch = lambda jj: slice((Q * 4 + jj) * 128, (Q * 4 + jj + 1) * 128)
            for jj in range(4):
                n = Q * 4 + jj
                pws, po, pds = self.ps_alloc(), self.ps_alloc(), self.ps_alloc()
                k.op(k.pe, lambda: nc.tensor.matmul(self.ps[pws][:, 0:128], st["wT"][:, jj, :], Sbf, start=True, stop=True),
                     reads=[st["b_wT"], b_Sbf], writes=[self.b_ps[pws]])
                k.op(k.dve, lambda: nc.vector.tensor_tensor(vnew, st["u"][:, jj, :], self.ps[pws][:, 0:128], ALU.subtract),
                     reads=[st["b_u"], self.b_ps[pws]], writes=[b_vnew])
                yield
                k.op(k.pe, lambda: nc.tensor.matmul(self.ps[po][:, 0:128], Sbf, qdT[:, ch(jj)], start=True, stop=False),
                     reads=[b_Sbf, b_qd[Q]], writes=[self.b_ps[po]])
                k.op(k.pe, lambda: nc.tensor.matmul(self.ps[po][:, 0:128], vnew, st["attnT"][:, jj, :], start=False, stop=True),
                     reads=[b_vnew, st["b_attnT"]], writes=[self.b_ps[po]])
                k.op(k.pe, lambda: nc.tensor.matmul(self.ps[pds][:, 0:128], st["kd"][:, jj, :], vnew, start=True, stop=True),
                     reads=[st["b_kd"], b_vnew], writes=[self.b_ps[pds]])
                k.op(k.act, lambda: nc.scalar.copy(oT[:, ch(jj)], self.ps[po][:, 0:128]), reads=[self.b_ps[po]], writes=[b_oT[Q]])
                k.op(k.dve, lambda: nc.vector.scalar_tensor_tensor(S, S, col(n, 4), self.ps[pds][:, 0:128], ALU.mult, ALU.add),
                     reads=[b_S, self.b_cols, self.b_ps[pds]], writes=[b_S])
                k.op(k.pool, lambda: nc.gpsimd.tensor_copy(Sbf, S), reads=[b_S], writes=[b_Sbf])
                self.ps_free(pws, po, pds)
                yield

        def run(gens):
            gens = list(gens)
            while gens:
                for g in list(gens):
                    try:
                        next(g)
                    except StopIteration:
                        gens.remove(g)

        def chain(*gs):
            for g in gs:
                yield from g


        def b4(h):
            (sq0, sq1, rs0, rs1, tmp0, tmp1), e4 = self.carve(pool0, [([512], BF16)] * 2 + [([512], F32)] * 4)
            assert e4 <= RC0
            sq = [sq0, sq1]; rs = [rs0, rs1]; tmp = [tmp0, tmp1]
            b_sq = [Buf(), Buf()]; b_rs = [Buf(), Buf()]; b_tmp = [Buf(), Buf()]
            for t, (c0, w) in enumerate(TILES[:4]):
                i = t % 2
                k.op(k.act, lambda: nc.scalar.activation(sq[i][:, 0:w], oT[:, c0:c0 + w], AF.Square), reads=[b_oT[t]], writes=[b_sq[i]])
                pi = self.next_ps()
                k.op(k.pe, lambda: nc.tensor.matmul(self.ps[pi][:, 0:w], self.ones_bf[:], sq[i][:, 0:w], start=True, stop=True),
                     reads=[b_sq[i], self.b_const], writes=[self.b_ps[pi]])
                k.op(k.act, lambda: nc.scalar.activation(rs[i][:, 0:w], self.ps[pi][:, 0:w], AF.Ln, bias=self.epsc[:], scale=1.0 / 128),
                     reads=[self.b_ps[pi], self.b_const], writes=[b_rs[i]])
                k.op(k.act, lambda: nc.scalar.activation(rs[i][:, 0:w], rs[i][:, 0:w], AF.Exp, scale=-0.5), reads=[b_rs[i]], writes=[b_rs[i]])
                k.op(k.dve, lambda: nc.vector.scalar_tensor_tensor(tmp[i][:, 0:w], oT[:, c0:c0 + w], self.pvs("dnn"), rs[i][:, 0:w], ALU.mult, ALU.mult),
                     reads=[b_oT[t], b_rs[i], self.b_pv], writes=[b_tmp[i]])
                k.op(k.dve, lambda: nc.vector.tensor_tensor(self.obst[h % 2][:, c0:c0 + w], tmp[i][:, 0:w], zsb[h % 2][:, c0:c0 + w], ALU.mult),
                     reads=[b_tmp[i], b_zsb[h % 2][t]], writes=[self.b_obst[h % 2]])
                yield
            k.dma(k.sp, self.ch_obsp[h % 2], self.d_obsp[:, h * NT:h * NT + LP], self.obst[h % 2][:, 0:LP],
                  reads=[self.b_obst[h % 2]], writes=[self.b_obsp[h]])
            yield

        for _ in front(0):
            pass
        run([mid(0)])
        for h in range(8):
            k.barrier()
            k.op(k.dve, lambda: nc.vector.memset(S, 0.0), writes=[b_S])
            k.op(k.dve, lambda: nc.vector.memset(Sbf, 0.0), writes=[b_Sbf])
            run([prep(h, q_, sets[q_]) for q_ in range(4)])
            k.barrier()
            g_rec = chain(*[rec(h, q_, sets[q_]) for q_ in range(4)])
            g_front = front(h + 1) if h < 7 else iter(())
            i_ = 0
            for _ in g_rec:
                for _n in range(1 if i_ % 2 == 0 else 2):
                    next(g_front, None)
                i_ += 1
            for _ in g_front:
                pass
            k.dma(k.sp, self.st, self.d_Sp[h], S, reads=[b_S])
            k.barrier()
            gens = [b4(h)]
            if h < 7:
                gens.append(mid(h + 1))
            run(gens)
        k.barrier()

    def gdn_samples(self):
        nc, k = self.nc, self.k
        pool0 = self.gdn_off

        def make_set(o, i):
            st = {}
            (st["S0b"],), o = self.carve(o, [([16, 128], F32)])
            (st["Vm"], st["S0h"], st["kqb"]), o = self.carve(o, [([16, 128], BF16), ([16, 128], BF16), ([16, 2], BF16)])
            (st["egm"], st["Bs"], st["prod"], st["vnT"], st["osm"]), o = self.carve(o, [([32], F32), ([32], F32), ([16], F32), ([16], F32), ([16], F32)])
            (st["ktok"],), o = self.carve(o, [([128], BF16)])
            for n_ in list(st.keys()):
                st["b_" + n_] = Buf()
            st["ch_in"] = k.chan(f"s0in{i}")
            st["ch_out"] = k.chan(f"s0out{i}")
            return st, o

        sets = []
        o = pool0
        for i_ in range(4):
            st_, o = make_set(o, i_)
            sets.append(st_)
        assert o <= RC0

        def head(h, st):
            S0b, Vm, egm, Bs, prod, vnT, osm, ktok = [st[n] for n in ("S0b", "Vm", "egm", "Bs", "prod", "vnT", "osm", "ktok")]
            ssv = self.ssave[:, h, :]
            kq = ssv[:, 0:32].rearrange("p (s two) -> p s two", two=2)
            kns, vss = ssv[:, 32:48], ssv[:, 48:64]
            bsv = self.b_ssave[h]
            k.dma(k.act, st["ch_in"], S0b, self.d_S0[:, h].rearrange("s a b -> a s b"), writes=[st["b_S0b"]])
            k.op(k.dve, lambda: nc.vector.tensor_scalar(egm[0:8, :], self.srow[0:8, 0:32], self.ident_f[0:8, h:h + 1], None, ALU.mult),
                 reads=[self.b_srow, self.b_gc], writes=[st["b_egm"]])
            pi = self.ps_alloc()
            k.op(k.pe, lambda: nc.tensor.matmul(self.ps[pi][:, 0:32], self.ones_f[0:8, :], egm[0:8, :], start=True, stop=True),
                 reads=[st["b_egm"], self.b_gc], writes=[self.b_ps[pi]])
            k.op(k.act, lambda: nc.scalar.copy(Bs, self.ps[pi][:, 0:32]), reads=[self.b_ps[pi]], writes=[st["b_Bs"]])
            self.ps_free(pi)
            Beg, Bbe = Bs[:, 0:16], Bs[:, 16:32]
            yield
            pkq, pqk = self.ps_alloc(), self.ps_alloc()
            k.op(k.dve, lambda: nc.vector.tensor_copy(st["S0h"][:], S0b[:]), reads=[st["b_S0b"]], writes=[st["b_S0h"]])
            k.op(k.act, lambda: nc.scalar.copy(st["kqb"][:], kq), reads=[bsv], writes=[st["b_kqb"]])
            for s_ in range(16):
                k.op(k.pe, lambda: nc.tensor.matmul(self.ps[pkq][:, 2 * s_:2 * s_ + 2], st["S0h"][:, s_, :], st["kqb"][:, s_, :], start=True, stop=True),
                     reads=[st["b_S0h"], st["b_kqb"]], writes=[self.b_ps[pkq]])
            kqv = self.ps[pkq][:, 0:32].rearrange("p (s two) -> p s two", two=2)
            k.op(k.dve, lambda: nc.vector.tensor_tensor(vnT, kqv[:, :, 0], Beg, ALU.mult), reads=[self.b_ps[pkq], st["b_Bs"]], writes=[st["b_vnT"]])
            k.op(k.dve, lambda: nc.vector.tensor_tensor(vnT, vss, vnT, ALU.subtract), reads=[bsv, st["b_vnT"]], writes=[st["b_vnT"]])
            k.op(k.dve, lambda: nc.vector.tensor_tensor(vnT, vnT, Bbe, ALU.mult), reads=[st["b_vnT"], st["b_Bs"]], writes=[st["b_vnT"]])
            k.op(k.dve, lambda: nc.vector.tensor_tensor(prod, kq[:, :, 1], kns, ALU.mult), reads=[bsv], writes=[st["b_prod"]])
            k.op(k.pe, lambda: nc.tensor.matmul(self.ps[pqk][:, 0:16], self.ones_f[:], prod, start=True, stop=True),
                 reads=[st["b_prod"], self.b_gc], writes=[self.b_ps[pqk]])
            k.op(k.dve, lambda: nc.vector.tensor_tensor(osm, kqv[:, :, 1], Beg, ALU.mult), reads=[self.b_ps[pkq], st["b_Bs"]], writes=[st["b_osm"]])
            k.op(k.dve, lambda: nc.vector.tensor_tensor(prod, self.ps[pqk][:, 0:16], vnT, ALU.mult),
                 reads=[self.b_ps[pqk], st["b_vnT"], st["b_prod"]], writes=[st["b_prod"]])
            k.op(k.dve, lambda: nc.vector.tensor_tensor(self.osave[:, h, :], osm, prod, ALU.add), reads=[st["b_osm"], st["b_prod"]], writes=[self.b_osave])
            self.ps_free(pkq, pqk)
            yield
            pkt, pvt = self.ps_alloc(), self.ps_alloc()
            k.op(k.pe, lambda: nc.tensor.transpose(self.ps[pkt][0:16, 0:128], kns, self.ident_f[:]), reads=[bsv, self.b_gc], writes=[self.b_ps[pkt]])
            k.op(k.pe, lambda: nc.tensor.transpose(self.ps[pvt][0:16, 0:128], vnT, self.ident_f[:]), reads=[st["b_vnT"], self.b_gc], writes=[self.b_ps[pvt]])
            k.op(k.act, lambda: nc.scalar.copy(ktok[0:16, :], self.ps[pkt][0:16, 0:128]), reads=[self.b_ps[pkt]], writes=[st["b_ktok"]])
            k.op(k.dve, lambda: nc.vector.tensor_tensor(Vm[0:16], self.ps[pvt][0:16, 0:128].unsqueeze(1).broadcast_to([16, 16, 128]),
                                                        self.ident_f[0:16, 0:16].unsqueeze(2).broadcast_to([16, 16, 128]), ALU.mult),
                 reads=[self.b_ps[pvt], self.b_gc], writes=[st["b_Vm"]])
            self.ps_free(pkt, pvt)
            yield
            for g4 in range(4):
                pi = self.ps_alloc()
                k.op(k.pe, lambda: nc.tensor.matmul(self.ps[pi][:, :], ktok[0:16, :], Vm[0:16, 4 * g4:4 * g4 + 4, :].rearrange("p a b -> p (a b)"), start=True, stop=True),
                     reads=[st["b_ktok"], st["b_Vm"]], writes=[self.b_ps[pi]])
                for ss in range(4):
                    s_ = 4 * g4 + ss
                    k.op(k.dve, lambda: nc.vector.scalar_tensor_tensor(S0b[:, s_, :], S0b[:, s_, :], Bs[:, s_:s_ + 1], self.ps[pi][:, ss * 128:(ss + 1) * 128], ALU.mult, ALU.add),
                         reads=[st["b_S0b"], st["b_Bs"], self.b_ps[pi]], writes=[st["b_S0b"]])
                self.ps_free(pi)
                yield
            k.dma(k.sp, st["ch_out"], self.d_Ss[:, h].rearrange("s a b -> a s b"), S0b, reads=[st["b_S0b"]])
            yield

        def run(gens):
            gens = list(gens)
            while gens:
                for g in list(gens):
                    try:
                        next(g)
                    except StopIteration:
                        gens.remove(g)

        for hp in range(0, 8, 4):
            run([head(hp + i_, sets[i_]) for i_ in range(4)])
        (sq, rs, tmp), o2 = self.carve(o, [([128], BF16), ([128], F32), ([128], F32)])
        assert o2 <= RC0
        bq, br, bt_ = Buf(), Buf(), Buf()
        osf = self.osave[:].rearrange("p h s -> p (h s)")
        k.op(k.act, lambda: nc.scalar.activation(sq, osf, AF.Square), reads=[self.b_osave], writes=[bq])
        pi = self.next_ps()
        k.op(k.pe, lambda: nc.tensor.matmul(self.ps[pi][:, 0:128], self.ones_bf[:], sq, start=True, stop=True), reads=[bq, self.b_const], writes=[self.b_ps[pi]])
        k.op(k.act, lambda: nc.scalar.activation(rs, self.ps[pi][:, 0:128], AF.Ln, bias=self.epsc[:], scale=1.0 / 128), reads=[self.b_ps[pi], self.b_const], writes=[br])
        k.op(k.act, lambda: nc.scalar.activation(rs, rs, AF.Exp, scale=-0.5), reads=[br], writes=[br])
        k.op(k.dve, lambda: nc.vector.scalar_tensor_tensor(tmp, osf, self.pvs("dnn"), rs, ALU.mult, ALU.mult), reads=[self.b_osave, br, self.b_pv], writes=[bt_])
        b_obs = Buf()
        k.op(k.dve, lambda: nc.vector.tensor_tensor(self.obs_s, tmp.rearrange("p (h s) -> p h s", h=8), self.ssave[:, :, 64:80], ALU.mult),
             reads=[bt_] + self.b_ssave, writes=[b_obs])
        with nc.allow_non_contiguous_dma(reason="16-token sample columns of the spilled branch output"):
            k.dma(k.sp, self.ch_obsp[0], self.d_obsp.rearrange("p (h t) -> p h t", h=8)[:, :, LP:NT], self.obs_s, reads=[b_obs], writes=[self.b_obsp[8]])
        k.barrier()
        self.ob = self.view(RX0, [8, NT], BF16)
        k.dma(k.sp, self.ch_obsp[1], self.ob.rearrange("p h t -> p (h t)"), self.d_obsp, reads=self.b_obsp,
              writes=[b for r in self.b_ob for b in r])

    def branch_b(self):
        self.b_ob = [[Buf() for _ in range(5)] for _ in range(8)]
        self.gdn_prologue()
        if self.gdn_stop == "pro":
            return
        self.gdn_all()
        self.gdn_samples()


    def tok_phase(self):
        nc, k = self.nc, self.k
        (rows,), _ = self.carve(RA0, [([4096], F32)])
        b_rows = Buf()
        for sp in range(0, 16, 2):
            pi = self.next_ps()
            for half in range(2):
                sl, b_s = self.take_slab(("tok", (sp + half) * 256))
                for kc in range(8):
                    k.op(k.pe, lambda: nc.tensor.matmul(self.ps[pi][0:19, half * 256:(half + 1) * 256], self.hT[:, kc, LP - 3:NT], sl[:, kc, :],
                                                        start=(kc == 0), stop=(kc == 7)),
                         reads=[b_s, self.b_h[kc][3], self.b_h[kc][4]], writes=[self.b_ps[pi]])
                self.done_slab()
            k.op(k.act, lambda: nc.scalar.copy(rows[0:19, sp * 256:(sp + 2) * 256], self.ps[pi][0:19, :]), reads=[self.b_ps[pi]], writes=[b_rows])
        k.dma(k.sp, self.st, self.d_crp, rows[0:3, 0:1024], reads=[b_rows])
        k.dma(k.sp, self.st, self.d_cqp, rows[0:3, 1024:4096], reads=[b_rows])
        k.dma(k.sp, self.st, self.d_crs[:, 2, :], rows[3:19, 0:1024], reads=[b_rows])
        k.dma(k.sp, self.st, self.d_cqs[:, 2, :], rows[3:19, 1024:4096], reads=[b_rows])
        k.dma(k.sp, self.st, self.d_crs[:, 0:2, :], self.d_crn[:, 1:3, :])
        k.dma(k.sp, self.st, self.d_cqs[:, 0:2, :], self.d_cqn[:, 1:3, :])
        k.barrier()

    def skip_slabs_until(self, pred):
        while self.slab_i < self.nslab and not pred(self.plan[self.slab_i]["tag"]):
            self.take_slab(self.plan[self.slab_i]["tag"])
            self.done_slab()

    def final_norm_out(self, es):
        nc, k = self.nc, self.k
        (sq0, sq1, tmp0, tmp1, rs0, rs1), _ = self.carve(RA0, [([FC, 512], BF16)] * 2 + [([FC, 512], F32)] * 2 + [([512], F32)] * 2)
        sq, tmp, rs = [sq0, sq1], [tmp0, tmp1], [rs0, rs1]
        b_sq = [Buf(), Buf()]; b_tmp = [Buf(), Buf()]; b_rs = [Buf(), Buf()]
        o, _ = PV["n_fin"]
        ys = self.d_y.rearrange("(c p) t -> p c t", p=128)
        for t, (c0, w) in enumerate(TILES):
            i = t % 2
            xs = self.xT[:, :, c0:c0 + w]
            bx = [self.b_x[c][t] for c in range(FC)]
            k.op(k.act, lambda: nc.scalar.activation(sq[i][:, :, 0:w], xs, AF.Square), reads=bx, writes=[b_sq[i]])
            pi = self.next_ps()
            for c in range(FC):
                k.op(k.pe, lambda c=c: nc.tensor.matmul(self.ps[pi][:, 0:w], self.ones_bf[:], sq[i][:, c, 0:w],
                                                       start=(c == 0), stop=(c == FC - 1)),
                     reads=[b_sq[i], self.b_const], writes=[self.b_ps[pi]])
            k.op(k.act, lambda: nc.scalar.activation(rs[i][:, 0:w], self.ps[pi][:, 0:w], AF.Ln, bias=self.epsc[:], scale=1.0 / D),
                 reads=[self.b_ps[pi], self.b_const], writes=[b_rs[i]])
            k.op(k.act, lambda: nc.scalar.activation(rs[i][:, 0:w], rs[i][:, 0:w], AF.Exp, scale=-0.5), reads=[b_rs[i]], writes=[b_rs[i]])
            k.op(k.dve, lambda: nc.vector.tensor_tensor(tmp[i][:, :, 0:w], xs, rs[i][:, 0:w].unsqueeze(1).broadcast_to([128, FC, w]), ALU.mult),
                 reads=bx + [b_rs[i]], writes=[b_tmp[i]])
            k.op(k.dve, lambda: nc.vector.tensor_tensor(tmp[i][:, :, 0:w], tmp[i][:, :, 0:w],
                                                        self.pv[:, o:o + 8].unsqueeze(2).broadcast_to([128, FC, w]), ALU.mult),
                 reads=[b_tmp[i], self.b_pv], writes=[b_tmp[i]])
            k.dma(k.sp, self.st, ys[:, :, c0:c0 + w], tmp[i][:, :, 0:w], reads=[b_tmp[i]])

    def dump_x(self, name):
        if name in self.dbg:
            o = self.nc.dram_tensor("dbg_" + name, [D, NT], F32, kind="ExternalOutput").ap()
            self.dbg_out[name] = o
            self.k.dma(self.k.sp, self.st, o.rearrange("(c p) t -> p c t", p=128), self.xT[:],
                       reads=[b for r in self.b_x for b in r])

    def dump_h(self, name):
        if name in self.dbg:
            nc, k = self.nc, self.k
            o = nc.dram_tensor("dbg_" + name, [D, NT], F32, kind="ExternalOutput").ap()
            self.dbg_out[name] = o
            hf = self.view(RA0, [FC, NT], F32)
            bb = Buf()
            k.barrier()
            k.op(k.dve, lambda: nc.vector.tensor_copy(hf, self.hT[:]), reads=[b for r in self.b_h for b in r], writes=[bb])
            k.dma(k.sp, self.st, o.rearrange("(c p) t -> p c t", p=128), hf, reads=[bb])
            k.barrier()


def build_program(dbg=(), stop_after=None, gdn_stop=None):
    from contextlib import ExitStack
    P = Prog(dbg, gdn_stop)
    nc, k = P.nc, P.k
    P.prologue()
    P.ada(0, 24)
    P.mod_prep(0)
    with ExitStack() as es:
        P.norm_mod(0, es)
        k.barrier()
    P.dump_h("h1")
    P.ada_pending = list(range(24, 72, 2))
    with ExitStack() as es:
        P.ffn(1, es)
        while P.ada_pending:
            P.ada_step(P.ada_pending.pop(0))
        k.barrier()
    P.dump_x("x1")
    if stop_after == "ffn1":
        with ExitStack() as es:
            P.final_norm_out(es)
            k.barrier()
        k.finish()
        return P
    P.mod_prep(1)
    P.mod_prep(2)
    P.norm_mod(1, None)
    k.barrier()
    P.dump_h("h2")
    P.spill_x()
    P.gdn_consts()
    P.mix_prologue()
    k.barrier()
    P.branch_b()
    P.dump_bf("ob", P.ob, [b for r in P.b_ob for b in r])
    if stop_after == "gdn":
        P.skip_slabs_until(lambda tag: False)
        k.finish()
        return P
    P.merge(1)
    P.branch_a()
    P.dump_bf("oa", P.oa, [b for r in P.b_oa for b in r])
    P.merge(0)
    P.reload_x()
    P.out_proj()
    P.dump_x("x2")
    k.dma(k.sp, P.st, P.d_hnew, P.hnewT[:].rearrange("p c s -> p (c s)"), reads=[P.b_hnew])
    P.tok_phase()
    with ExitStack() as es:
        P.norm_mod(2, es)
        k.barrier()
    with ExitStack() as es:
        P.ffn(2, es)
        k.barrier()
    with ExitStack() as es:
        P.final_norm_out(es)
        k.barrier()
    k.finish()
    return P


def doubling_masks():
    i = np.arange(128)
    out = np.zeros((128, 7, 128), np.float32)
    for l in range(7):
        bi, bj = (i >> l)[:, None], (i >> l)[None, :]
        out[:, l, :] = -(((bi & 1) == 1) & (bj == bi - 1)).astype(np.float32)
    return out.reshape(128, 7 * 128)


def host_inputs(inputs):
    W = {n: np.asarray(v, np.float32) for n, v in inputs.items()}
    plan = build_plan()
    wslabs = gather_slabs(plan, W)
    pvec = pack_pvec(W)
    cmask = doubling_masks()
    maps = []
    for b in range(NCORES):
        sl = slice(NS_ * b, NS_ * (b + 1))
        xT = np.ascontiguousarray(np.concatenate([W["x_prompt"][b], W["x_sample"][sl, 0, :]], axis=0).T)
        cT = np.ascontiguousarray(np.concatenate([W["c_prompt"][b:b + 1], W["c_sample"][sl]], axis=0).T)
        sm = np.zeros((128, NSM), np.float32)
        cr = W["state_rglru_conv"][0, sl]
        sm[:, 0:384] = cr.reshape(16, 3, 8, 128).transpose(3, 2, 1, 0).reshape(128, 384)
        sm[:, 384:512] = W["state_rglru_h"][0, sl].reshape(16, 8, 128).transpose(2, 1, 0).reshape(128, 128)
        cq = W["state_delta_conv"][0, sl]
        sm[:, 512:1664] = cq.reshape(16, 3, 24, 128).transpose(3, 2, 1, 0).reshape(128, 1152)
        maps.append(dict(xT=xT, cT=cT, wslabs=wslabs, pvec=pvec, smallT=sm, S0=np.ascontiguousarray(W["state_delta_S"][0, sl]), cmask=cmask,
                         cr_nat=np.ascontiguousarray(cr), cq_nat=np.ascontiguousarray(cq)))
    return maps


_PROG = None


def kernel(**inputs):
    global _PROG
    if _PROG is None:
        _PROG = build_program()
    P = _PROG
    maps = host_inputs(inputs)
    res = run_bass_kernel_spmd(P.nc, maps, core_ids=list(range(NCORES)))
    R = res.results
    B, NSQ = NCORES, NCORES * NS_
    y_p = np.zeros((B, LP, D), np.float32); y_s = np.zeros((NSQ, 1, D), np.float32)
    h_p = np.zeros((1, B, D), np.float32); h_s = np.zeros((1, NSQ, D), np.float32)
    cr_p = np.zeros((1, B, 3, D), np.float32); cr_s = np.zeros((1, NSQ, 3, D), np.float32)
    S_p = np.zeros((1, B, 8, 128, 128), np.float32); S_s = np.zeros((1, NSQ, 8, 128, 128), np.float32)
    cq_p = np.zeros((1, B, 3, 3072), np.float32); cq_s = np.zeros((1, NSQ, 3, 3072), np.float32)
    for b in range(B):
        r = R[b]
        sl = slice(NS_ * b, NS_ * (b + 1))
        yT = np.asarray(r["yT"])
        y_p[b] = yT[:, :LP].T
        y_s[sl, 0] = yT[:, LP:].T
        hn = np.asarray(r["hnewT_o"]).reshape(128, 8, 17)
        hh = hn.transpose(2, 1, 0).reshape(17, D)
        h_p[0, b] = hh[0]; h_s[0, sl] = hh[1:]
        cr_p[0, b] = np.asarray(r["cr_p"]); cq_p[0, b] = np.asarray(r["cq_p"])
        cr_s[0, sl] = np.asarray(r["cr_s"]); cq_s[0, sl] = np.asarray(r["cq_s"])
        S_p[0, b] = np.asarray(r["S_p"]); S_s[0, sl] = np.asarray(r["S_s"])
    return (y_p, y_s, h_p, cr_p, S_p, cq_p, h_s, cr_s, S_s, cq_s)
```

```python
import os
import numpy as np
import concourse.bass as bass
import concourse.mybir as mybir
from concourse.bass_utils import run_bass_kernel_spmd

F32 = mybir.dt.float32
BF16 = mybir.dt.bfloat16
ALU = mybir.AluOpType
AF = mybir.ActivationFunctionType

NCORES = 8
D = 1024
FC = 8
LP = 2048
NS_ = 16
NT = LP + NS_
TILES = [(0, 512), (512, 512), (1024, 512), (1536, 512), (2048, 16)]
DFF = 2816
HC = 22
FFN_GROUPS = [(0, 8), (8, 16), (16, 22)]
NSLOT = 5
EPS = 1e-6
RX0, RA0, RB0, RC0, RBIG = 0, 66048, 99072, 132096, 143360
O_XR, O_GR, O_Q, O_K, O_V, O_A, O_B, O_ZG, O_MA, O_MB = 0, 1024, 2048, 3072, 4096, 5120, 5128, 5136, 6160, 7184


class Buf:
    __slots__ = ("name", "w", "r", "psum")

    def __init__(self, name="", psum=False):
        self.name, self.w, self.r, self.psum = name, None, {}, psum


class Tok:
    __slots__ = ("sem", "key", "val", "grp")

    def __init__(self, sem, key, val, grp=None):
        self.sem, self.key, self.val, self.grp = sem, key, val, grp

    def value(self):
        return self.grp.count * 16 if self.grp is not None else self.val


class DmaChan:
    def __init__(self, K, name, group=False):
        self.sem = K.nc.alloc_semaphore(name)
        self.key, self.count, self.group = name, 0, group


class Eng:
    def __init__(self, K, name, h, tracked_self):
        self.name, self.h = name, h
        self.sem = K.nc.alloc_semaphore("s_" + name)
        self.key = "s_" + name
        self.seq, self.waited, self.tracked_self = 0, {}, tracked_self
        self.nwait = self.ninst = 0


class K:
    def __init__(self, nc):
        self.nc = nc
        self.pe = Eng(self, "pe", nc.tensor, False)
        self.act = Eng(self, "act", nc.scalar, True)
        self.dve = Eng(self, "dve", nc.vector, True)
        self.pool = Eng(self, "pool", nc.gpsimd, True)
        self.sp = Eng(self, "sp", nc.sync, True)
        self.engs = [self.pe, self.act, self.dve, self.pool, self.sp]
        self.chans = []

    def chan(self, name, group=False):
        c = DmaChan(self, name, group)
        self.chans.append(c)
        return c

    def _collect(self, eng, reads, writes):
        need = {}

        def add(t):
            if t is None:
                return
            v = t.value()
            if t.key not in need or need[t.key][1] < v:
                need[t.key] = (t.sem, v)

        for b in reads:
            add(b.w)
            if b.psum:
                for t in b.r.values():
                    if t.key != eng.key:
                        add(t)
        for b in writes:
            add(b.w)
            for t in b.r.values():
                add(t)
        for key, (sem, v) in need.items():
            if key == eng.key and not eng.tracked_self:
                continue
            if eng.waited.get(key, 0) >= v:
                continue
            eng.h.wait_ge(sem, v)
            eng.waited[key] = v
            eng.nwait += 1

    def _commit(self, tok, reads, writes):
        for b in reads:
            old = b.r.get(tok.key)
            if old is None or old.value() <= tok.value():
                b.r[tok.key] = tok
        for b in writes:
            b.w = tok
            b.r = {}

    def op(self, eng, fn, reads=(), writes=()):
        self._collect(eng, reads, writes)
        inst = fn()
        eng.seq += 1
        eng.ninst += 1
        inst.then_inc(eng.sem, 1)
        self._commit(Tok(eng.sem, eng.key, eng.seq), reads, writes)
        return inst

    def dma(self, eng, chan, out, in_, reads=(), writes=()):
        self._collect(eng, reads, writes)
        inst = eng.h.dma_start(out=out, in_=in_)
        inst.then_inc(chan.sem, 16)
        chan.count += 1
        eng.ninst += 1
        tok = Tok(chan.sem, chan.key, None, grp=chan) if chan.group else Tok(chan.sem, chan.key, chan.count * 16)
        self._commit(tok, reads, writes)
        return inst

    def barrier(self):
        for e in self.engs:
            for o in self.engs:
                if o is e or o.seq == 0:
                    continue
                if e.waited.get(o.key, 0) < o.seq:
                    e.h.wait_ge(o.sem, o.seq)
                    e.waited[o.key] = o.seq
            for c in self.chans:
                if c.key.startswith("ringch"):
                    continue
                if c.count and e.waited.get(c.key, 0) < c.count * 16:
                    e.h.wait_ge(c.sem, c.count * 16)
                    e.waited[c.key] = c.count * 16

    def finish(self):
        for c in self.chans:
            if c.count and self.sp.waited.get(c.key, 0) < c.count * 16:
                self.sp.h.wait_ge(c.sem, c.count * 16)


def _cols(start, n):
    return list(range(start, start + n))


def build_plan():
    P = []

    def S(tag, name, k0, nk, cols, sub=None):
        P.append(dict(tag=tag, name=name, k0=k0, nk=nk, cols=cols, sub=sub))

    def ada(m0, m1):
        for m in range(m0, m1, 2):
            S(("ada", m), "w_ada", 0, 8, _cols(m * 128, 256))

    def ffn(n, ada_ms=()):
        up, dn = ("w_ffn1_up", "w_ffn1_down") if n == 1 else ("w_ffn2_up", "w_ffn2_down")
        ada_ms = list(ada_ms)

        def one_ada():
            if ada_ms:
                m = ada_ms.pop(0)
                S(("ada", m), "w_ada", 0, 8, _cols(m * 128, 256))

        for (i0, i1) in FFN_GROUPS:
            for i in range(i0, i1, 2):
                S(("up_g", n, i), up, 0, 8, _cols(i * 128, 256))
                S(("up_v", n, i), up, 0, 8, _cols(DFF + i * 128, 256))
                one_ada()
            for fp in range(4):
                S(("down", n, i0, fp), dn, i0, i1 - i0, _cols(fp * 256, 256))
                one_ada()
        while ada_ms:
            one_ada()

    ada(0, 24)
    ffn(1, ada_ms=range(24, 72, 2))
    S(("ab",), "w_in", 0, 8, _cols(O_A, 16))
    for h in range(8):
        S(("qk", h), "w_in", 0, 8, _cols(O_Q + h * 128, 128) + _cols(O_K + h * 128, 128))
        S(("vz", h), "w_in", 0, 8, _cols(O_V + h * 128, 128) + _cols(O_ZG + h * 128, 128))
    for j in range(0, 8, 2):
        S(("wb", 1, j), "w_branch", 0, 8, _cols(j * 128, 256), sub=1)
        S(("mg", 1, j), "w_in", 0, 8, _cols(O_MB + j * 128, 256))
    for c in range(8):
        S(("rgw", c), "rg_w", 0, 1, None, sub=c)
        S(("xrgr", c), "w_in", 0, 8, _cols(O_XR + c * 128, 128) + _cols(O_GR + c * 128, 128))
    for j in range(0, 8, 2):
        S(("wb", 0, j), "w_branch", 0, 8, _cols(j * 128, 256), sub=0)
        S(("mg", 0, j), "w_in", 0, 8, _cols(O_MA + j * 128, 256))
    for j in range(0, 8, 2):
        S(("wo", j), "w_out", 0, 8, _cols(j * 128, 256))
    for s in range(0, 4096, 256):
        S(("tok", s), "w_in", 0, 8, (_cols(O_XR + s, 256) if s < 1024 else _cols(O_Q + s - 1024, 256)))
    ffn(2)
    return P


def gather_slabs(plan, W):
    ns = len(plan)
    out = np.zeros((ns, 128, 8, 256), np.float32)
    for s, sp in enumerate(plan):
        if sp["name"] == "rg_w":
            c = sp["sub"]
            out[s, :, 0, 0:128] = W["rg_w_a"][0, c]
            out[s, :, 0, 128:256] = W["rg_w_x"][0, c]
            continue
        M = W[sp["name"]][0]
        if sp["sub"] is not None:
            M = M[sp["sub"]]
        rows = M[sp["k0"] * 128:(sp["k0"] + sp["nk"]) * 128][:, sp["cols"]]
        out[s, :, :sp["nk"], :len(sp["cols"])] = rows.reshape(sp["nk"], 128, -1).transpose(1, 0, 2)
    return out.reshape(ns, 128, 2048)


PV = {}
_off = 0
for _n, _w in [("b_ada", 72), ("n_ffn1", 8), ("n_mix", 8), ("n_ffn2", 8), ("n_fin", 8), ("crw", 32), ("crb", 8),
               ("rba", 8), ("rbx", 8), ("lam", 8), ("cqw", 96), ("dnn", 1), ("alog", 1), ("dtb", 1)]:
    PV[_n] = (_off, _w)
    _off += _w
NPV = _off
NSM = 1664


def pack_pvec(W):
    pv = np.zeros((128, NPV), np.float32)

    def put(name, arr):
        o, w = PV[name]
        pv[:arr.shape[0], o:o + w] = arr

    fm = lambda v: v.reshape(-1, 128).T
    put("b_ada", fm(W["b_ada"][0]))
    put("n_ffn1", fm(W["norm_ffn1"][0])); put("n_mix", fm(W["norm_mix"][0]))
    put("n_ffn2", fm(W["norm_ffn2"][0])); put("n_fin", fm(W["norm_final"]))
    put("crw", W["conv_rnn_w"][0].reshape(4, 8, 128).transpose(2, 1, 0).reshape(128, 32))
    put("crb", fm(W["conv_rnn_b"][0]))
    put("rba", fm(W["rg_b_a"][0])); put("rbx", fm(W["rg_b_x"][0])); put("lam", fm(W["rg_lambda"][0]))
    put("cqw", W["conv_qkv_w"][0].reshape(4, 24, 128).transpose(2, 1, 0).reshape(128, 96))
    put("dnn", W["dn_norm"][0].reshape(128, 1))
    put("alog", W["dn_a_log"][0].reshape(8, 1)); put("dtb", W["dn_dt_bias"][0].reshape(8, 1))
    return pv


class Prog:
    def __init__(self, dbg=(), gdn_stop=None):
        self.dbg = set(dbg)
        self.gdn_stop = gdn_stop
        nc = self.nc = bass.Bass("TRN2", target_bir_lowering=False)
        k = self.k = K(nc)
        self.plan = build_plan()
        self.nslab = len(self.plan)
        dt = lambda name, shape, kind: nc.dram_tensor(name, shape, F32, kind=kind).ap()
        self.d_x = dt("xT", [D, NT], "ExternalInput")
        self.d_c = dt("cT", [D, 17], "ExternalInput")
        self.d_w = dt("wslabs", [self.nslab, 128, 2048], "ExternalInput")
        self.d_pv = dt("pvec", [128, NPV], "ExternalInput")
        self.d_y = dt("yT", [D, NT], "ExternalOutput")
        self.dbg_out = {}
        self.big = nc.alloc_sbuf_tensor("big", [128, RBIG // 4], F32)
        self.xT = self.view(RX0, [FC, NT], F32)
        self.d_xsp = nc.dram_tensor("xspill", [128, FC * NT], F32, kind="Internal").ap()
        self.d_obsp = nc.dram_tensor("obspill", [128, 8 * NT], BF16, kind="Internal").ap()
        self.hT = nc.alloc_sbuf_tensor("hT_sb", [128, FC, NT], BF16)
        self.ring = [nc.alloc_sbuf_tensor(f"ring{i}", [128, 8, 256], BF16) for i in range(NSLOT)]
        self.pv = nc.alloc_sbuf_tensor("pv_sb", [128, NPV], F32)
        self.adaT = nc.alloc_sbuf_tensor("adaT", [128, 72, 17], F32)
        self.msc = nc.alloc_sbuf_tensor("msc", [128, 3, FC, 17], F32)
        self.gat = nc.alloc_sbuf_tensor("gat", [128, 3, FC, 17], F32)
        self.cb = nc.alloc_sbuf_tensor("cb", [128, 8, 17], BF16)
        self.ones_bf = nc.alloc_sbuf_tensor("ones_bf", [128, 128], BF16)
        self.epsc = nc.alloc_sbuf_tensor("epsc", [128, 1], F32)
        self.rgc = nc.alloc_sbuf_tensor("rgc", [128, 4, 8], F32)
        self.hnewT = nc.alloc_sbuf_tensor("hnewT", [128, 8, 17], F32)
        self.b_rgc = Buf("rgc"); self.b_hnew = Buf("hnew")
        self.d_small = dt("smallT", [128, NSM], "ExternalInput")
        self.d_hnew = dt("hnewT_o", [128, 8 * 17], "ExternalOutput")
        self.d_S0 = dt("S0", [16, 8, 128, 128], "ExternalInput")
        self.d_cmk = dt("cmask", [128, 7 * 128], "ExternalInput")
        self.d_crn = dt("cr_nat", [16, 3, 1024], "ExternalInput")
        self.d_cqn = dt("cq_nat", [16, 3, 3072], "ExternalInput")
        self.d_crp = dt("cr_p", [3, 1024], "ExternalOutput")
        self.d_cqp = dt("cq_p", [3, 3072], "ExternalOutput")
        self.d_crs = dt("cr_s", [16, 3, 1024], "ExternalOutput")
        self.d_cqs = dt("cq_s", [16, 3, 3072], "ExternalOutput")
        self.d_Sp = dt("S_p", [8, 128, 128], "ExternalOutput")
        self.d_Ss = dt("S_s", [16, 8, 128, 128], "ExternalOutput")
        self.ps = [nc.alloc_psum_tensor(f"ps{i}", [128, 512], F32) for i in range(8)]
        self.b_x = [[Buf(f"x{c}_{t}") for t in range(5)] for c in range(FC)]
        self.b_h = [[Buf(f"h{c}_{t}") for t in range(5)] for c in range(FC)]
        self.b_ring = [Buf(f"ring{i}") for i in range(NSLOT)]
        self.b_ps = [Buf(f"ps{i}", psum=True) for i in range(8)]
        self.b_pv = Buf("pv"); self.b_ada = Buf("ada"); self.b_mod = Buf("mod"); self.b_cb = Buf("cb")
        self.b_const = Buf("const")
        self.ring_ch = [k.chan(f"ringch{i}") for i in range(NSLOT)]
        self.ld = k.chan("ld", group=True)
        self.st = k.chan("st", group=True)
        self.ld2 = k.chan("ld2", group=True)
        self.ldp = k.chan("ldp", group=True)
        self.ps_i = 0
        self.ps_freelist = list(range(8))
        self.ada_bank = None
        self.ada_pending = []
        self.slab_i = 0
        self.slab_ld = 0
        self.slab_done = 0

    def view(self, off, shape, dtype):
        esz = 4 if dtype == F32 else 2
        n = int(np.prod(shape))
        nb = (n * esz + 3) // 4 * 4
        assert off % 4 == 0 and off + nb <= RBIG, (off, nb)
        ap = self.big[:, off // 4:(off + nb) // 4]
        if dtype != F32:
            ap = ap.bitcast(dtype)[:, 0:n]
        if len(shape) == 2:
            ap = ap.rearrange("p (a b) -> p a b", a=shape[0])
        elif len(shape) == 3:
            ap = ap.rearrange("p (a b c) -> p a b c", a=shape[0], b=shape[1])
        return ap

    def carve(self, off, specs):
        out = []
        for shape, dtype in specs:
            esz = 4 if dtype == F32 else 2
            nb = (int(np.prod(shape)) * esz + 31) // 32 * 32
            out.append(self.view(off, shape, dtype))
            off += nb
        return out, off

    def next_ps(self):
        i = self.ps_freelist.pop(0)
        self.ps_freelist.append(i)
        return i

    def ps_alloc(self):
        assert self.ps_freelist, "out of PSUM banks"
        return self.ps_freelist.pop(0)

    def ps_free(self, *idx):
        self.ps_freelist.extend(idx)

    def _issue_loads(self):
        k = self.k
        while self.slab_ld < self.nslab and self.slab_ld < self.slab_done + NSLOT:
            s = self.slab_ld
            sp = self.plan[s]
            slot = s % NSLOT
            nk = sp["nk"]
            src = self.d_w[s, :, 0:nk * 256].rearrange("p (k c) -> p k c", k=nk)
            k.dma(k.pool, self.ring_ch[slot], self.ring[slot][:, 0:nk, :], src, writes=[self.b_ring[slot]])
            self.slab_ld += 1

    def take_slab(self, tag):
        s = self.slab_i
        assert self.plan[s]["tag"] == tag, (self.plan[s]["tag"], tag)
        self._issue_loads()
        assert s < self.slab_ld, "slab not loaded (too many slabs held)"
        self.slab_i += 1
        slot = s % NSLOT
        return self.ring[slot], self.b_ring[slot]

    def done_slab(self, n=1):
        self.slab_done += n
        self._issue_loads()

    def dbg_dump(self, name, ap_sb, shape, reads):
        if name not in self.dbg:
            return
        o = self.nc.dram_tensor("dbg_" + name, list(shape), F32, kind="ExternalOutput").ap()
        self.dbg_out[name] = o
        self.k.dma(self.k.sp, self.st, o, ap_sb, reads=reads)

    def dbg_bf(self, name, ap_bf, reads, scratch_f32):
        if name not in self.dbg:
            return
        bb = Buf()
        sc = scratch_f32[:].rearrange("p a b -> p (a b)")
        self.k.op(self.k.dve, lambda: self.nc.vector.tensor_copy(sc, ap_bf), reads=reads, writes=[bb])
        self.dbg_dump(name, sc, [128, 512], [bb])
        self.k.barrier()

    def pvs(self, name, i=0, n=1):
        o, w = PV[name]
        return self.pv[:, o + i:o + i + n]

    def prologue(self):
        nc, k = self.nc, self.k
        xs = self.d_x.rearrange("(c p) t -> p c t", p=128)
        for c in range(FC):
            k.dma(k.sp, self.ld, self.xT[:, c, :], xs[:, c, :], writes=self.b_x[c])
        k.dma(k.sp, self.ld, self.pv[:], self.d_pv, writes=[self.b_pv])
        k.dma(k.pool, self.ldp, self.cb[:], self.d_c.rearrange("(c p) t -> p c t", p=128), writes=[self.b_cb])
        k.op(k.dve, lambda: nc.vector.memset(self.ones_bf[:], 1.0), writes=[self.b_const])
        k.op(k.dve, lambda: nc.vector.memset(self.epsc[:], EPS), writes=[self.b_const])

    def ada(self, m0, m1):
        for m in range(m0, m1, 2):
            self.ada_step(m)

    def ada_step(self, m):
        nc, k = self.nc, self.k
        if self.ada_bank is None:
            self.ada_bank = self.ps_alloc()
            self.ada_mb = m
        pi, mb = self.ada_bank, self.ada_mb
        ps = self.ps[pi]
        slab, bslab = self.take_slab(("ada", m))
        for sub in range(2):
            j = m + sub - mb
            for kc in range(8):
                k.op(k.pe, lambda: nc.tensor.matmul(ps[:, j * 17:(j + 1) * 17], slab[:, kc, sub * 128:(sub + 1) * 128], self.cb[:, kc, :],
                                                    start=(kc == 0), stop=(kc == 7)), reads=[bslab, self.b_cb], writes=[self.b_ps[pi]])
        self.done_slab()
        if m + 2 - mb == 8:
            o, _ = PV["b_ada"]
            bias = self.pv[:, o + mb:o + mb + 8].unsqueeze(2).broadcast_to([128, 8, 17])
            k.op(k.dve, lambda: nc.vector.tensor_tensor(
                self.adaT[:, mb:mb + 8, :], ps[:, 0:136].rearrange("p (m t) -> p m t", m=8), bias, ALU.add),
                reads=[self.b_ps[pi], self.b_pv], writes=[self.b_ada])
            self.ps_free(pi)
            self.ada_bank = None

    def mod_prep(self, n):
        nc, k = self.nc, self.k
        gname = ["n_ffn1", "n_mix", "n_ffn2"][n]
        o, _ = PV[gname]
        g = self.pv[:, o:o + 8].unsqueeze(2).broadcast_to([128, 8, 17])
        sc = self.adaT[:, (3 * n + 1) * 8:(3 * n + 2) * 8, :]
        gt = self.adaT[:, (3 * n + 2) * 8:(3 * n + 3) * 8, :]
        k.op(k.dve, lambda: nc.vector.scalar_tensor_tensor(self.msc[:, n], sc, 1.0, g, ALU.add, ALU.mult),
             reads=[self.b_ada, self.b_pv], writes=[self.b_mod])
        k.op(k.dve, lambda: nc.vector.tensor_scalar(self.gat[:, n], gt, 0.5 if n != 1 else 1.0, None, ALU.mult),
             reads=[self.b_ada], writes=[self.b_mod])

    def norm_mod(self, n, es):
        nc, k = self.nc, self.k
        (sq0, sq1, tmp0, tmp1, rs0, rs1), _ = self.carve(RA0, [([FC, 512], BF16)] * 2 + [([FC, 512], F32)] * 2 + [([512], F32)] * 2)
        sq, tmp, rs = [sq0, sq1], [tmp0, tmp1], [rs0, rs1]
        b_sq = [Buf(), Buf()]; b_tmp = [Buf(), Buf()]; b_rs = [Buf(), Buf()]
        sh0 = (3 * n) * 8
        for t, (c0, w) in enumerate(TILES):
            i = t % 2
            xs = self.xT[:, :, c0:c0 + w]
            bx = [self.b_x[c][t] for c in range(FC)]
            k.op(k.act, lambda: nc.scalar.activation(sq[i][:, :, 0:w], xs, AF.Square), reads=bx, writes=[b_sq[i]])
            pi = self.next_ps()
            for c in range(FC):
                k.op(k.pe, lambda c=c: nc.tensor.matmul(self.ps[pi][:, 0:w], self.ones_bf[:], sq[i][:, c, 0:w],
                                                       start=(c == 0), stop=(c == FC - 1)),
                     reads=[b_sq[i], self.b_const], writes=[self.b_ps[pi]])
            k.op(k.act, lambda: nc.scalar.activation(rs[i][:, 0:w], self.ps[pi][:, 0:w], AF.Ln, bias=self.epsc[:], scale=1.0 / D),
                 reads=[self.b_ps[pi], self.b_const], writes=[b_rs[i]])
            k.op(k.act, lambda: nc.scalar.activation(rs[i][:, 0:w], rs[i][:, 0:w], AF.Exp, scale=-0.5), reads=[b_rs[i]], writes=[b_rs[i]])
            k.op(k.dve, lambda: nc.vector.tensor_tensor(tmp[i][:, :, 0:w], xs, rs[i][:, 0:w].unsqueeze(1).broadcast_to([128, FC, w]), ALU.mult),
                 reads=bx + [b_rs[i]], writes=[b_tmp[i]])
            bh = [self.b_h[c][t] for c in range(FC)]
            if t < 4:
                for c in range(FC):
                    if c % 2 == 0:
                        k.op(k.act, lambda c=c: nc.scalar.activation(
                            self.hT[:, c, c0:c0 + w], tmp[i][:, c, 0:w], AF.Identity,
                            bias=self.adaT[:, sh0 + c, 0:1], scale=self.msc[:, n, c, 0:1]),
                            reads=[b_tmp[i], self.b_ada, self.b_mod], writes=[bh[c]])
                    else:
                        k.op(k.dve, lambda c=c: nc.vector.tensor_scalar(
                            self.hT[:, c, c0:c0 + w], tmp[i][:, c, 0:w], self.msc[:, n, c, 0:1], self.adaT[:, sh0 + c, 0:1], ALU.mult, ALU.add),
                            reads=[b_tmp[i], self.b_ada, self.b_mod], writes=[bh[c]])
            else:
                k.op(k.dve, lambda: nc.vector.tensor_tensor(tmp[i][:, :, 0:w], tmp[i][:, :, 0:w], self.msc[:, n, :, 1:17], ALU.mult),
                     reads=[b_tmp[i], self.b_mod], writes=[b_tmp[i]])
                k.op(k.dve, lambda: nc.vector.tensor_tensor(self.hT[:, :, c0:c0 + w], tmp[i][:, :, 0:w], self.adaT[:, sh0:sh0 + 8, 1:17], ALU.add),
                     reads=[b_tmp[i], self.b_ada], writes=bh)

    def ffn(self, n, es):
        nc, k = self.nc, self.k
        gi = 0 if n == 1 else 2
        act = self.view(RA0, [8, NT], BF16)
        (sg0, sg1, stmp), _ = self.carve(RB0, [([512], F32)] * 2 + [([16], F32)])
        sg = [sg0, sg1]
        b_act = [[Buf() for _ in range(5)] for _ in range(8)]
        b_sg = [Buf(), Buf()]; b_stmp = Buf()
        sgi = 0
        for (i0, i1) in FFN_GROUPS:
            for i in range(i0, i1, 2):
                sl_g, b_g = self.take_slab(("up_g", n, i))
                sl_v, b_v = self.take_slab(("up_v", n, i))
                for t, (c0, w) in enumerate(TILES):
                    bh = [self.b_h[c][t] for c in range(FC)]
                    for sub in range(2):
                        pg, pv_ = self.next_ps(), self.next_ps()
                        for (pi, sl, bs) in ((pg, sl_g, b_g), (pv_, sl_v, b_v)):
                            for kc in range(8):
                                k.op(k.pe, lambda pi=pi, sl=sl, kc=kc: nc.tensor.matmul(
                                    self.ps[pi][:, 0:w], sl[:, kc, sub * 128:(sub + 1) * 128], self.hT[:, kc, c0:c0 + w],
                                    start=(kc == 0), stop=(kc == 7)), reads=[bs] + bh, writes=[self.b_ps[pi]])
                        s_ = sgi % 2
                        sgi += 1
                        k.op(k.act, lambda: nc.scalar.activation(sg[s_][:, 0:w], self.ps[pg][:, 0:w], AF.Silu),
                             reads=[self.b_ps[pg]], writes=[b_sg[s_]])
                        ci = i + sub - i0
                        k.op(k.dve, lambda: nc.vector.tensor_tensor(act[:, ci, c0:c0 + w], sg[s_][:, 0:w], self.ps[pv_][:, 0:w], ALU.mult),
                             reads=[b_sg[s_], self.b_ps[pv_]], writes=[b_act[ci][t]])
                self.done_slab(2)
                if self.ada_pending:
                    self.ada_step(self.ada_pending.pop(0))
            nk = i1 - i0
            for fp in range(4):
                sl_d, b_d = self.take_slab(("down", n, i0, fp))
                for t, (c0, w) in enumerate(TILES):
                    for sub in range(2):
                        fc = fp * 2 + sub
                        pi = self.next_ps()
                        for kc in range(nk):
                            k.op(k.pe, lambda kc=kc: nc.tensor.matmul(
                                self.ps[pi][:, 0:w], sl_d[:, kc, sub * 128:(sub + 1) * 128], act[:, kc, c0:c0 + w],
                                start=(kc == 0), stop=(kc == nk - 1)), reads=[b_d, b_act[kc][t]], writes=[self.b_ps[pi]])
                        xs = self.xT[:, fc, c0:c0 + w]
                        if t < 4:
                            k.op(k.dve, lambda: nc.vector.scalar_tensor_tensor(xs, self.ps[pi][:, 0:w], self.gat[:, gi, fc, 0:1], xs, ALU.mult, ALU.add),
                                 reads=[self.b_ps[pi], self.b_mod, self.b_x[fc][t]], writes=[self.b_x[fc][t]])
                        else:
                            k.op(k.dve, lambda: nc.vector.tensor_tensor(stmp[:], self.ps[pi][:, 0:w], self.gat[:, gi, fc, 1:17], ALU.mult),
                                 reads=[self.b_ps[pi], self.b_mod], writes=[b_stmp])
                            k.op(k.dve, lambda: nc.vector.tensor_tensor(xs, xs, stmp[:], ALU.add),
                                 reads=[b_stmp, self.b_x[fc][t]], writes=[self.b_x[fc][t]])
                self.done_slab()
                if self.ada_pending:
                    self.ada_step(self.ada_pending.pop(0))


    def spill_x(self):
        k = self.k
        self.b_xsp = Buf("xsp")
        k.dma(k.sp, self.st, self.d_xsp, self.xT.rearrange("p c t -> p (c t)"),
              reads=[b for r in self.b_x for b in r], writes=[self.b_xsp])

    def reload_x(self):
        k = self.k
        k.barrier()
        k.dma(k.sp, self.st, self.xT.rearrange("p c t -> p (c t)"), self.d_xsp,
              reads=[self.b_xsp], writes=[b for r in self.b_x for b in r])

    def mix_prologue(self):
        nc, k = self.nc, self.k
        (self.smallT,), _ = self.carve(RC0, [([NSM], F32)])
        self.b_small = Buf("small")
        k.dma(k.sp, self.ld2, self.smallT, self.d_small, writes=[self.b_small])
        self.crT = self.smallT[:, 0:384].rearrange("p (c j s) -> p c j s", c=8, j=3)
        self.h0T = self.smallT[:, 384:512].rearrange("p (c s) -> p c s", c=8)
        self.cqT = self.smallT[:, 512:1664].rearrange("p (c j s) -> p c j s", c=24, j=3)
        rc = self.rgc
        b = self.b_rgc
        k.op(k.act, lambda: nc.scalar.activation(rc[:, 0], self.pvs("lam", 0, 8), AF.Exp, scale=-1.0), reads=[self.b_pv], writes=[b])
        k.op(k.act, lambda: nc.scalar.activation(rc[:, 0], rc[:, 0], AF.Ln, bias=1.0), reads=[b], writes=[b])
        k.op(k.dve, lambda: nc.vector.tensor_scalar(rc[:, 1], rc[:, 0], -4.0, None, ALU.mult), reads=[b], writes=[b])
        k.op(k.dve, lambda: nc.vector.tensor_scalar(rc[:, 0], rc[:, 0], -8.0, None, ALU.mult), reads=[b], writes=[b])
        k.op(k.dve, lambda: nc.vector.tensor_scalar(rc[:, 2], self.pvs("rba", 0, 8), 0.5, None, ALU.mult), reads=[self.b_pv, b], writes=[b])
        k.op(k.dve, lambda: nc.vector.tensor_scalar(rc[:, 3], self.pvs("rbx", 0, 8), 0.5, None, ALU.mult), reads=[self.b_pv, b], writes=[b])

    def branch_a(self):
        nc, k = self.nc, self.k
        oa = self.view(RX0, [8, NT], BF16)
        self.oa = oa
        self.b_oa = [[Buf() for _ in range(5)] for _ in range(8)]
        (xrp, bt, aa, m2, ge), off = self.carve(RX0 + 33024, [([8], F32), ([NT], F32), ([NT], F32), ([NT], F32), ([NT], F32)])
        tl, off2 = self.carve(off, [([512], F32)] * 10 + [([512], BF16)] * 2 + [([16], F32)] * 2 + [([516], BF16)] * 2 + [([4, 128], BF16)])
        assert off2 <= RB0, off2
        preA, dgA = tl[14:16], tl[16]
        b_preA = [Buf(), Buf()]; b_dgA = Buf()
        XC, TR, TI, GRS, SQ = tl[0:2], tl[2:4], tl[4:6], tl[6:8], tl[8:10]
        XCB = tl[10:12]
        xsn, stmp = tl[12], tl[13]
        b_full = {n: [Buf() for _ in range(5)] for n in ("xrp", "bt", "aa", "m2", "ge")}
        b_t = {n: [Buf(), Buf()] for n in ("xc", "tr", "ti", "grs", "sq", "xcb")}
        b_xsn = Buf(); b_stmp = Buf(); b_pad = Buf()
        rc = self.rgc
        k.op(k.dve, lambda: nc.vector.memset(xrp[:, 0:3], 0.0), writes=[b_pad])
        it = 0
        for c in range(8):
            sl_w, b_w = self.take_slab(("rgw", c))
            sl_p, b_p = self.take_slab(("xrgr", c))
            cw = lambda j: self.pvs("crw", c * 4 + j)
            for j_ in range(4):
                k.op(k.dve, lambda: nc.vector.tensor_scalar(dgA[:, j_, :], self.identb[:], cw(j_), None, ALU.mult),
                     reads=[self.b_gc, self.b_pv], writes=[b_dgA])
            for t, (c0, w) in enumerate(TILES):
                i = it % 2
                it += 1
                bh = [self.b_h[kc][t] for kc in range(FC)]
                px, pg = self.next_ps(), self.next_ps()
                for (pi, sub) in ((px, 0), (pg, 1)):
                    for kc in range(8):
                        k.op(k.pe, lambda: nc.tensor.matmul(self.ps[pi][:, 0:w], sl_p[:, kc, sub * 128:(sub + 1) * 128], self.hT[:, kc, c0:c0 + w],
                                                            start=(kc == 0), stop=(kc == 7)), reads=[b_p] + bh, writes=[self.b_ps[pi]])
                xc = XC[i][:, 0:w]
                if t < 4:
                    p_, bp_ = preA[t % 2], b_preA[t % 2]
                    if t == 0:
                        k.op(k.dve, lambda: nc.vector.memset(p_[:, 0:3], 0.0), writes=[bp_])
                    else:
                        k.op(k.dve, lambda: nc.vector.tensor_copy(p_[:, 0:3], preA[(t - 1) % 2][:, 512:515]), reads=[b_preA[(t - 1) % 2]], writes=[bp_])
                    k.op(k.dve, lambda: nc.vector.tensor_copy(p_[:, 3:515], self.ps[px][:, 0:w]), reads=[self.b_ps[px]], writes=[bp_])
                    pc = self.next_ps()
                    for j in range(4):
                        k.op(k.pe, lambda: nc.tensor.matmul(self.ps[pc][:, 0:w], dgA[:, j, :], p_[:, j:j + 512], start=(j == 0), stop=(j == 3)),
                             reads=[b_dgA, bp_], writes=[self.b_ps[pc]])
                    k.op(k.act, lambda: nc.scalar.activation(xc, self.ps[pc][:, 0:w], AF.Identity, bias=self.pvs("crb", c)),
                         reads=[self.b_ps[pc], self.b_pv], writes=[b_t["xc"][i]])
                    k.op(k.dve, lambda: nc.vector.tensor_scalar(XCB[i][:, 0:w], self.ps[pc][:, 0:w], self.pvs("crb", c), None, ALU.add),
                         reads=[self.b_ps[pc], self.b_pv], writes=[b_t["xcb"][i]])
                else:
                    k.op(k.act, lambda: nc.scalar.copy(xsn, self.ps[px][:, 0:w]), reads=[self.b_ps[px]], writes=[b_xsn])
                    k.op(k.dve, lambda: nc.vector.tensor_scalar(xc, xsn, cw(3), self.pvs("crb", c), ALU.mult, ALU.add),
                         reads=[b_xsn, self.b_pv], writes=[b_t["xc"][i]])
                    for j in (2, 1, 0):
                        k.op(k.dve, lambda: nc.vector.scalar_tensor_tensor(xc, self.crT[:, c, j, :], cw(j), xc, ALU.mult, ALU.add),
                             reads=[self.b_small, self.b_pv, b_t["xc"][i]], writes=[b_t["xc"][i]])
                if t == 4:
                    k.op(k.act, lambda: nc.scalar.copy(XCB[i][:, 0:w], xc), reads=[b_t["xc"][i]], writes=[b_t["xcb"][i]])
                pr, pi_ = self.next_ps(), self.next_ps()
                k.op(k.pe, lambda: nc.tensor.matmul(self.ps[pr][:, 0:w], sl_w[:, 0, 0:128], XCB[i][:, 0:w], start=True, stop=True),
                     reads=[b_w, b_t["xcb"][i]], writes=[self.b_ps[pr]])
                k.op(k.pe, lambda: nc.tensor.matmul(self.ps[pi_][:, 0:w], sl_w[:, 0, 128:256], XCB[i][:, 0:w], start=True, stop=True),
                     reads=[b_w, b_t["xcb"][i]], writes=[self.b_ps[pi_]])
                tr, ti = TR[i][:, 0:w], TI[i][:, 0:w]
                k.op(k.act, lambda: nc.scalar.activation(tr, self.ps[pr][:, 0:w], AF.Tanh, bias=rc[:, 2, c:c + 1], scale=0.5),
                     reads=[self.b_ps[pr], self.b_rgc], writes=[b_t["tr"][i]])
                k.op(k.act, lambda: nc.scalar.activation(ti, self.ps[pi_][:, 0:w], AF.Tanh, bias=rc[:, 3, c:c + 1], scale=0.5),
                     reads=[self.b_ps[pi_], self.b_rgc], writes=[b_t["ti"][i]])
                k.op(k.act, lambda: nc.scalar.activation(aa[:, c0:c0 + w], tr, AF.Exp, bias=rc[:, 1, c:c + 1], scale=rc[:, 1, c:c + 1]),
                     reads=[b_t["tr"][i], self.b_rgc], writes=[b_full["aa"][t]])
                k.op(k.act, lambda: nc.scalar.activation(tr, tr, AF.Exp, bias=rc[:, 0, c:c + 1], scale=rc[:, 0, c:c + 1]),
                     reads=[b_t["tr"][i], self.b_rgc], writes=[b_t["tr"][i]])
                k.op(k.dve, lambda: nc.vector.tensor_scalar(m2[:, c0:c0 + w], tr, -0.25, 0.25, ALU.mult, ALU.add),
                     reads=[b_t["tr"][i]], writes=[b_full["m2"][t]])
                k.op(k.dve, lambda: nc.vector.scalar_tensor_tensor(bt[:, c0:c0 + w], ti, 1.0, xc, ALU.add, ALU.mult),
                     reads=[b_t["ti"][i], b_t["xc"][i]], writes=[b_full["bt"][t]])
                grs, sq = GRS[i][:, 0:w], SQ[i][:, 0:w]
                k.op(k.act, lambda: nc.scalar.copy(grs, self.ps[pg][:, 0:w]), reads=[self.b_ps[pg]], writes=[b_t["grs"][i]])
                k.op(k.act, lambda: nc.scalar.activation(sq, self.ps[pg][:, 0:w], AF.Square), reads=[self.b_ps[pg]], writes=[b_t["sq"][i]])
                k.op(k.dve, lambda: nc.vector.tensor_scalar(sq, sq, 0.044715, 1.0, ALU.mult, ALU.add), reads=[b_t["sq"][i]], writes=[b_t["sq"][i]])
                k.op(k.dve, lambda: nc.vector.tensor_tensor(sq, sq, grs, ALU.mult), reads=[b_t["sq"][i], b_t["grs"][i]], writes=[b_t["sq"][i]])
                k.op(k.act, lambda: nc.scalar.activation(sq, sq, AF.Tanh, scale=0.7978845608028654), reads=[b_t["sq"][i]], writes=[b_t["sq"][i]])
                k.op(k.dve, lambda: nc.vector.scalar_tensor_tensor(ge[:, c0:c0 + w], sq, 1.0, grs, ALU.add, ALU.mult),
                     reads=[b_t["sq"][i], b_t["grs"][i]], writes=[b_full["ge"][t]])
            self.done_slab(2)
            allb = lambda n: b_full[n]
            k.op(k.act, lambda: nc.scalar.activation(m2[:, :], m2[:, :], AF.Sqrt), reads=allb("m2"), writes=allb("m2"))
            k.op(k.dve, lambda: nc.vector.memset(m2[:, 0:1], 0.5), reads=allb("m2"), writes=allb("m2"))
            k.op(k.dve, lambda: nc.vector.tensor_tensor(bt[:, :], bt[:, :], m2[:, :], ALU.mult), reads=allb("m2") + allb("bt"), writes=allb("bt"))
            k.op(k.dve, lambda: nc.vector.tensor_tensor_scan(m2[:, 0:LP], aa[:, 0:LP], bt[:, 0:LP], 0.0, ALU.mult, ALU.add),
                 reads=allb("aa") + allb("bt") + allb("m2"), writes=allb("m2"))
            k.op(k.dve, lambda: nc.vector.tensor_tensor(stmp, aa[:, LP:NT], self.h0T[:, c, :], ALU.mult),
                 reads=allb("aa") + [self.b_small], writes=[b_stmp])
            k.op(k.dve, lambda: nc.vector.tensor_tensor(m2[:, LP:NT], bt[:, LP:NT], stmp, ALU.add),
                 reads=allb("bt") + [b_stmp] + allb("m2"), writes=allb("m2"))
            k.op(k.dve, lambda: nc.vector.scalar_tensor_tensor(oa[:, c, :], ge[:, :], 0.5, m2[:, :], ALU.mult, ALU.mult),
                 reads=allb("ge") + allb("m2"), writes=self.b_oa[c])
            k.op(k.act, lambda: nc.scalar.copy(self.hnewT[:, c, 0:1], m2[:, LP - 1:LP]), reads=allb("m2"), writes=[self.b_hnew])
            k.op(k.act, lambda: nc.scalar.copy(self.hnewT[:, c, 1:17], m2[:, LP:NT]), reads=allb("m2"), writes=[self.b_hnew])
        k.barrier()

    def merge(self, br):
        nc, k = self.nc, self.k
        ob = self.oa if br == 0 else self.ob
        b_ob = self.b_oa if br == 0 else self.b_ob
        m = self.view(RA0 if br == 0 else RB0, [8, NT], BF16)
        b_m = [[Buf() for _ in range(5)] for _ in range(8)]
        if br == 0:
            self.m_a, self.b_ma = m, b_m
        else:
            self.m_b, self.b_mb = m, b_m
        (sg0, sg1), _ = self.carve(RC0 + 6656, [([512], F32)] * 2)
        sg = [sg0, sg1]; b_sg = [Buf(), Buf()]
        it = 0
        for j in range(0, 8, 2):
            sl_b, b_b = self.take_slab(("wb", br, j))
            sl_m, b_g = self.take_slab(("mg", br, j))
            for t, (c0, w) in enumerate(TILES):
                bh = [self.b_h[kc][t] for kc in range(FC)]
                for sub in range(2):
                    jj = j + sub
                    py, pm = self.next_ps(), self.next_ps()
                    for kc in range(8):
                        k.op(k.pe, lambda: nc.tensor.matmul(self.ps[py][:, 0:w], sl_b[:, kc, sub * 128:(sub + 1) * 128], ob[:, kc, c0:c0 + w],
                                                            start=(kc == 0), stop=(kc == 7)), reads=[b_b, b_ob[kc][t]], writes=[self.b_ps[py]])
                    for kc in range(8):
                        k.op(k.pe, lambda: nc.tensor.matmul(self.ps[pm][:, 0:w], sl_m[:, kc, sub * 128:(sub + 1) * 128], self.hT[:, kc, c0:c0 + w],
                                                            start=(kc == 0), stop=(kc == 7)), reads=[b_g] + bh, writes=[self.b_ps[pm]])
                    i = it % 2
                    it += 1
                    k.op(k.act, lambda: nc.scalar.activation(sg[i][:, 0:w], self.ps[pm][:, 0:w], AF.Sigmoid), reads=[self.b_ps[pm]], writes=[b_sg[i]])
                    k.op(k.dve, lambda: nc.vector.tensor_tensor(m[:, jj, c0:c0 + w], sg[i][:, 0:w], self.ps[py][:, 0:w], ALU.mult),
                         reads=[b_sg[i], self.b_ps[py]], writes=[b_m[jj][t]])
            self.done_slab(2)
        k.barrier()

    def out_proj(self, branches=(0, 1)):
        nc, k = self.nc, self.k
        (stmp,), _ = self.carve(RC0 + 6656, [([16], F32)])
        b_stmp = Buf()
        ms = [(self.m_a, self.b_ma), (self.m_b, self.b_mb)] if len(branches) == 2 else [(self.m_a, self.b_ma)]
        for j in range(0, 8, 2):
            sl, b_s = self.take_slab(("wo", j))
            for t, (c0, w) in enumerate(TILES):
                for sub in range(2):
                    fc = j + sub
                    pi = self.next_ps()
                    n = 8 * len(ms)
                    q = 0
                    for (m, b_m) in ms:
                        for kc in range(8):
                            k.op(k.pe, lambda: nc.tensor.matmul(self.ps[pi][:, 0:w], sl[:, kc, sub * 128:(sub + 1) * 128], m[:, kc, c0:c0 + w],
                                                                start=(q == 0), stop=(q == n - 1)), reads=[b_s, b_m[kc][t]], writes=[self.b_ps[pi]])
                            q += 1
                    xs = self.xT[:, fc, c0:c0 + w]
                    if t < 4:
                        k.op(k.dve, lambda: nc.vector.scalar_tensor_tensor(xs, self.ps[pi][:, 0:w], self.gat[:, 1, fc, 0:1], xs, ALU.mult, ALU.add),
                             reads=[self.b_ps[pi], self.b_mod, self.b_x[fc][t]], writes=[self.b_x[fc][t]])
                    else:
                        k.op(k.dve, lambda: nc.vector.tensor_tensor(stmp, self.ps[pi][:, 0:w], self.gat[:, 1, fc, 1:17], ALU.mult),
                             reads=[self.b_ps[pi], self.b_mod], writes=[b_stmp])
                        k.op(k.dve, lambda: nc.vector.tensor_tensor(xs, xs, stmp, ALU.add),
                             reads=[b_stmp, self.b_x[fc][t]], writes=[self.b_x[fc][t]])
            self.done_slab()
        k.barrier()

    def dump_bf(self, name, ap, bufs):
        if name not in self.dbg:
            return
        nc, k = self.nc, self.k
        o = nc.dram_tensor("dbg_" + name, [D, NT], F32, kind="ExternalOutput").ap()
        self.dbg_out[name] = o
        k.barrier()
        hf = self.view(RA0, [FC, NT], F32)
        bb = Buf()
        k.op(k.dve, lambda: nc.vector.tensor_copy(hf, ap), reads=bufs, writes=[bb])
        k.dma(k.sp, self.st, o.rearrange("(c p) t -> p c t", p=128), hf, reads=[bb])
        k.barrier()


    def gdn_consts(self):
        nc, k = self.nc, self.k
        al = lambda n, sh, dt_: nc.alloc_sbuf_tensor(n, sh, dt_)
        self.ident_f = al("ident_f", [128, 128], F32); self.identb = al("identb", [128, 128], BF16)
        self.uinc_b = al("uinc_b", [128, 128], BF16); self.sm_b = al("sm_b", [128, 128], BF16); self.negs_b = al("negs_b", [128, 128], BF16)
        self.ones_f = al("ones_f", [128, 128], F32)
        self.nA = al("nA", [8, 1], F32)
        self.cmk = al("cmk", [128, 7, 128], BF16)
        b = self.b_gc = Buf("gdnconst")
        k.dma(k.pool, self.ldp, self.cmk[:], self.d_cmk.rearrange("p (l j) -> p l j", l=7), writes=[b])
        BIGN = -30000.0
        k.op(k.pool, lambda: nc.gpsimd.memset(self.ones_f[:], 1.0), writes=[b])
        k.op(k.pool, lambda: nc.gpsimd.affine_select(self.ident_f[:], self.ones_f[:], [[-1, 128]], ALU.is_equal, 0.0, base=0, channel_multiplier=1), reads=[b], writes=[b])
        k.op(k.pool, lambda: nc.gpsimd.affine_select(self.uinc_b[:], self.ones_f[:], [[1, 128]], ALU.is_ge, 0.0, base=0, channel_multiplier=-1), reads=[b], writes=[b])
        k.op(k.pool, lambda: nc.gpsimd.affine_select(self.sm_b[:], self.ones_f[:], [[-1, 128]], ALU.is_gt, 0.0, base=0, channel_multiplier=1), reads=[b], writes=[b])
        k.op(k.pool, lambda: nc.gpsimd.tensor_scalar(self.negs_b[:], self.uinc_b[:], BIGN, None, ALU.mult), reads=[b], writes=[b])
        k.op(k.dve, lambda: nc.vector.tensor_copy(self.identb[:], self.ident_f[:]), reads=[b], writes=[b])
        k.op(k.act, lambda: nc.scalar.activation(self.nA[:], self.pv[0:8, PV["alog"][0]:PV["alog"][0] + 1], AF.Exp), reads=[self.b_pv, b], writes=[b])
        k.op(k.dve, lambda: nc.vector.tensor_scalar(self.nA[:], self.nA[:], -1.0, None, ALU.mult), reads=[b], writes=[b])

    def gdn_prologue(self):
        nc, k = self.nc, self.k
        base = RX0
        (self.obst0, self.obs_s), base = self.carve(base, [([NT], BF16)] + [([8, 16], BF16)])
        self.obst = [self.obst0, self.obst0]
        _bo = Buf()
        self.b_obst = [_bo, _bo]
        self.ch_obsp = [k.chan("obsp0"), k.chan("obsp1")]
        self.b_obsp = [Buf() for _ in range(9)]
        (self.cols, self.egrow, self.srow, self.ssave, self.osave), off = self.carve(
            base, [([16, 56], F32), ([NT], F32), ([64], F32), ([8, 80], F32), ([8, 16], F32)])
        self.b_ssave = [Buf() for _ in range(8)]
        self.b_osave = Buf()
        self.gdn_off = off
        self.b_cols = Buf("cols"); self.b_eg = Buf("egrow"); self.b_srow = Buf("srow")
        (g, beta, gcum, glb, egl, begr, eglb, ghi, glo), off2 = self.carve(off, [([NT], F32)] * 9)
        (gbf,), off2 = self.carve(off2, [([LP], BF16)])
        bgh, bgl_, bgb = Buf(), Buf(), Buf()
        assert off2 <= RC0
        eg = self.egrow
        bg, bb, bc, bl, bel, bbe, bgl = [Buf() for _ in range(7)]
        sl, b_s = self.take_slab(("ab",))
        dtb = self.pv[0:8, PV["dtb"][0]:PV["dtb"][0] + 1]
        for t, (c0, w) in enumerate(TILES):
            bh = [self.b_h[kc][t] for kc in range(FC)]
            pa, pb = self.next_ps(), self.next_ps()
            for (pi, o) in ((pa, 0), (pb, 8)):
                for kc in range(8):
                    k.op(k.pe, lambda: nc.tensor.matmul(self.ps[pi][0:8, 0:w], sl[:, kc, o:o + 8], self.hT[:, kc, c0:c0 + w],
                                                        start=(kc == 0), stop=(kc == 7)), reads=[b_s] + bh, writes=[self.b_ps[pi]])
            k.op(k.act, lambda: nc.scalar.activation(g[0:8, c0:c0 + w], self.ps[pa][0:8, 0:w], AF.Exp, bias=dtb), reads=[self.b_ps[pa], self.b_pv], writes=[bg])
            k.op(k.act, lambda: nc.scalar.activation(beta[0:8, c0:c0 + w], self.ps[pb][0:8, 0:w], AF.Sigmoid), reads=[self.b_ps[pb]], writes=[bb])
        self.done_slab()
        k.op(k.act, lambda: nc.scalar.activation(g[0:8, :], g[0:8, :], AF.Ln, bias=1.0), reads=[bg], writes=[bg])
        k.op(k.dve, lambda: nc.vector.tensor_scalar(g[0:8, :], g[0:8, :], self.nA[:], None, ALU.mult), reads=[bg, self.b_gc], writes=[bg])
        for n in range(16):
            cs = slice(n * 128, (n + 1) * 128)
            k.op(k.dve, lambda: nc.vector.tensor_tensor_scan(gcum[0:8, cs], self.ones_f[0:8, :], g[0:8, cs], 0.0, ALU.mult, ALU.add),
                 reads=[bg, self.b_gc], writes=[bc])
        g3 = lambda a: a[0:8, 0:LP].rearrange("p (n j) -> p n j", n=16)
        k.op(k.dve, lambda: nc.vector.tensor_copy(g3(glb), g3(gcum)[:, :, 127:128].broadcast_to([8, 16, 128])), reads=[bc], writes=[bl])
        k.op(k.act, lambda: nc.scalar.activation(eg[0:8, 0:LP], gcum[0:8, 0:LP], AF.Exp), reads=[bc], writes=[self.b_eg])
        k.op(k.dve, lambda: nc.vector.tensor_tensor(egl[0:8, 0:LP], glb[0:8, 0:LP], gcum[0:8, 0:LP], ALU.subtract), reads=[bl, bc], writes=[bel])
        k.op(k.act, lambda: nc.scalar.activation(egl[0:8, 0:LP], egl[0:8, 0:LP], AF.Exp), reads=[bel], writes=[bel])
        k.op(k.dve, lambda: nc.vector.tensor_tensor(begr[0:8, 0:LP], beta[0:8, 0:LP], eg[0:8, 0:LP], ALU.mult), reads=[bb, self.b_eg], writes=[bbe])
        k.op(k.dve, lambda: nc.vector.tensor_copy(g3(eglb), g3(eg)[:, :, 127:128].broadcast_to([8, 16, 128])), reads=[self.b_eg], writes=[bgl])
        k.op(k.act, lambda: nc.scalar.activation(self.srow[0:8, 0:16], g[0:8, LP:NT], AF.Exp), reads=[bg], writes=[self.b_srow])
        k.op(k.dve, lambda: nc.vector.tensor_copy(self.srow[0:8, 16:32], beta[0:8, LP:NT]), reads=[bb, self.b_srow], writes=[self.b_srow])
        k.op(k.dve, lambda: nc.vector.tensor_copy(gbf[0:8, :], g[0:8, 0:LP]), reads=[bg], writes=[bgb])
        k.op(k.dve, lambda: nc.vector.tensor_copy(ghi[0:8, 0:LP], gbf[0:8, :]), reads=[bgb], writes=[bgh])
        k.op(k.dve, lambda: nc.vector.tensor_tensor(glo[0:8, 0:LP], g[0:8, 0:LP], ghi[0:8, 0:LP], ALU.subtract), reads=[bg, bgh], writes=[bgl_])
        for nq in range(4):
            pi = self.next_ps()
            for nn in range(4):
                n = nq * 4 + nn
                for q, (rw, br) in enumerate(((g, bg), (beta, bb), (begr, bbe), (egl, bel), (eglb, bgl), (ghi, bgh), (glo, bgl_))):
                    k.op(k.pe, lambda: nc.tensor.transpose(self.ps[pi][:, nn * 56 + q * 8: nn * 56 + q * 8 + 8], rw[0:8, n * 128:(n + 1) * 128], self.ident_f[0:8, 0:8]),
                         reads=[br, self.b_gc], writes=[self.b_ps[pi]])
            k.op(k.act, lambda: nc.scalar.copy(self.cols[:, nq * 4:nq * 4 + 4, :], self.ps[pi][:, 0:224].rearrange("p (n q) -> p n q", n=4)),
                 reads=[self.b_ps[pi]], writes=[self.b_cols])
        k.barrier()

    def gdn_all(self):
        nc, k = self.nc, self.k
        off = self.gdn_off
        (qn, oT), off = self.carve(off, [([NT], F32)] * 2)
        (zs0, zs1, qT, kT), off = self.carve(off, [([NT], BF16)] * 4)
        (qdT, vT), off = self.carve(off, [([LP], BF16)] * 2)
        zsb = [zs0, zs1]
        (S, Sbf, vnew, egm), off = self.carve(off, [([128], F32), ([128], BF16), ([128], BF16), ([512], F32)])
        pool0 = off
        b_qn = [Buf() for _ in range(5)]; b_oT = [Buf() for _ in range(4)]; b_kn = [Buf() for _ in range(5)]; b_vs = [Buf() for _ in range(5)]
        b_zsb = [[Buf() for _ in range(5)] for _ in range(2)]
        b_qT = [Buf() for _ in range(5)]; b_kT = [Buf() for _ in range(5)]; b_qd = [Buf() for _ in range(4)]; b_vT = [Buf() for _ in range(4)]
        b_S, b_Sbf, b_vnew, b_egm = [Buf() for _ in range(4)]
        col_h = lambda h: (lambda n, q: self.cols[:, n, q * 8 + h:q * 8 + h + 1])
        psb = lambda pi: self.ps[pi][:].bitcast(BF16)
        flat = lambda a: a[:].rearrange("p a b -> p (a b)")
        sub = lambda jj: slice(jj * 128, (jj + 1) * 128)
        mk = lambda l: self.cmk[:, l, :].unsqueeze(1).broadcast_to([128, 4, 128])
        idb4 = self.identb[:].unsqueeze(1).broadcast_to([128, 4, 128])
        recin = []
        o_ = pool0
        for _q in range(4):
            st = {}
            (st['attnT'], st['kd'], st['wT']), o_ = self.carve(o_, [([4, 128], BF16)] * 3)
            (st['u'],), o_ = self.carve(o_, [([4, 128], F32)])
            recin.append(st)
        work0 = o_

        def make_shared(o):
            sh = {}
            (sh["DN"], sh["DT"]), o = self.carve(o, [([4, 128], F32)] * 2)
            (sh["GUh"], sh["GUl"]), o = self.carve(o, [([4, 128], BF16)] * 2)
            for n_ in list(sh.keys()):
                sh["b_" + n_] = Buf()
            return sh, o

        def make_set(o, sh, rin):
            st = dict(sh)
            st.update(rin)
            names = ["L0", "NO", "X", "V", "M0", "M1", "kbg", "bv"]
            vs_, o = self.carve(o, [([4, 128], BF16)] * len(names))
            for n_, v_ in zip(names, vs_):
                st[n_] = v_
            for n_ in names + ["attnT", "kd", "wT", "u"]:
                st["b_" + n_] = Buf()
            return st, o

        sh0, e2 = make_shared(work0)
        sh1, e2 = make_shared(e2)
        sets = []
        for _q in range(4):
            st, e2 = make_set(e2, sh0 if _q % 2 == 0 else sh1, recin[_q])
            sets.append(st)
        assert e2 <= RC0, e2

        (kn, vs, pre0, pre1, pre2, pre3, pre4, pre5, xc0, xc1, sq0, sq1, rs0, rs1, xsn, dg), e1 = self.carve(
            work0, [([NT], F32)] * 2 + [([516], BF16)] * 6 + [([512], F32)] * 2 + [([512], BF16)] * 2 + [([512], F32)] * 2 + [([16], F32)] + [([12, 128], BF16)])
        b_dg = Buf()
        assert e1 <= RC0, e1
        pre = [[pre0, pre1], [pre2, pre3], [pre4, pre5]]
        b_pre = [[Buf(), Buf()] for _ in range(3)]
        xcb = [xc0, xc1]; b_xc = [Buf(), Buf()]
        sq = [sq0, sq1]; b_sq = [Buf(), Buf()]; rs = [rs0, rs1]; b_rs = [Buf(), Buf()]
        b_xsn = Buf()

        def front(h):
            zs, b_zs = zsb[h % 2], b_zsb[h % 2]
            for qi_ in range(3):
                for j_ in range(4):
                    k.op(k.dve, lambda: nc.vector.tensor_scalar(dg[:, qi_ * 4 + j_, :], self.identb[:], self.pvs("cqw", (qi_ * 8 + h) * 4 + j_), None, ALU.mult),
                         reads=[self.b_gc, self.b_pv], writes=[b_dg])
            sl_qk, b_qk = self.take_slab(("qk", h))
            sl_vz, b_vz = self.take_slab(("vz", h))
            dst = [(qn, b_qn), (kn, b_kn), (vs, b_vs)]
            xi = 0
            for t, (c0, w) in enumerate(TILES):
                bh = [self.b_h[kc][t] for kc in range(FC)]
                pss = [self.ps_alloc() for _ in range(4)]
                for (pi, sl, bs, sub) in ((pss[0], sl_qk, b_qk, 0), (pss[1], sl_qk, b_qk, 1), (pss[2], sl_vz, b_vz, 0), (pss[3], sl_vz, b_vz, 1)):
                    for kc in range(8):
                        k.op(k.pe, lambda: nc.tensor.matmul(self.ps[pi][:, 0:w], sl[:, kc, sub * 128:(sub + 1) * 128], self.hT[:, kc, c0:c0 + w],
                                                            start=(kc == 0), stop=(kc == 7)), reads=[bs] + bh, writes=[self.b_ps[pi]])
                        if kc == 3:
                            yield
                    yield
                for qi in range(3):
                    ch = qi * 8 + h
                    cw = lambda j: self.pvs("cqw", ch * 4 + j)
                    pi = pss[qi]
                    x_ = xcb[xi % 2]; bx_ = b_xc[xi % 2]
                    xi += 1
                    d_, bd_ = dst[qi]
                    if t < 4:
                        p_ = pre[qi][t % 2]; bp_ = b_pre[qi][t % 2]
                        if t == 0:
                            k.op(k.dve, lambda: nc.vector.memset(p_[:, 0:3], 0.0), writes=[bp_])
                        else:
                            k.op(k.dve, lambda: nc.vector.tensor_copy(p_[:, 0:3], pre[qi][(t - 1) % 2][:, 512:515]),
                                 reads=[b_pre[qi][(t - 1) % 2]], writes=[bp_])
                        if qi == 1:
                            k.op(k.act, lambda: nc.scalar.copy(p_[:, 3:515], self.ps[pi][:, 0:w]), reads=[self.b_ps[pi]], writes=[bp_])
                        else:
                            k.op(k.dve, lambda: nc.vector.tensor_copy(p_[:, 3:515], self.ps[pi][:, 0:w]), reads=[self.b_ps[pi]], writes=[bp_])
                        pc = self.ps_alloc()
                        for j in range(4):
                            k.op(k.pe, lambda: nc.tensor.matmul(self.ps[pc][:, 0:w], dg[:, qi * 4 + j, :], p_[:, j:j + 512], start=(j == 0), stop=(j == 3)),
                                 reads=[b_dg, bp_], writes=[self.b_ps[pc]])
                        k.op(k.act, lambda: nc.scalar.activation(d_[:, c0:c0 + w], self.ps[pc][:, 0:w], AF.Silu), reads=[self.b_ps[pc]], writes=[bd_[t]])
                        self.ps_free(pc)
                        if qi < 2:
                            self.ps_free(pi)
                        if qi == 2:
                            k.op(k.dve, lambda: nc.vector.tensor_copy(vT[:, c0:c0 + w], d_[:, c0:c0 + w]), reads=[bd_[t]], writes=[b_vT[t]])
                        yield
                        continue
                    else:
                        k.op(k.act, lambda: nc.scalar.copy(xsn, self.ps[pi][:, 0:w]), reads=[self.b_ps[pi]], writes=[b_xsn])
                        k.op(k.dve, lambda: nc.vector.tensor_scalar(x_[:, 0:w], xsn, cw(3), None, ALU.mult), reads=[b_xsn, self.b_pv], writes=[bx_])
                        for j in (2, 1, 0):
                            k.op(k.dve, lambda: nc.vector.scalar_tensor_tensor(x_[:, 0:w], self.cqT[:, ch, j, :], cw(j), x_[:, 0:w], ALU.mult, ALU.add),
                                 reads=[self.b_small, self.b_pv, bx_], writes=[bx_])
                    k.op(k.act, lambda: nc.scalar.activation(d_[:, c0:c0 + w], x_[:, 0:w], AF.Silu), reads=[bx_], writes=[bd_[t]])
                    if qi == 2 and t < 4:
                        k.op(k.dve, lambda: nc.vector.tensor_copy(vT[:, c0:c0 + w], d_[:, c0:c0 + w]), reads=[bd_[t]], writes=[b_vT[t]])
                k.op(k.act, lambda: nc.scalar.activation(zs[:, c0:c0 + w], self.ps[pss[3]][:, 0:w], AF.Silu), reads=[self.b_ps[pss[3]]], writes=[b_zs[t]])
                if t == 4:
                    self.ps_free(pss[0], pss[1])
                self.ps_free(pss[2], pss[3])
                yield
            self.done_slab(2)

        def mid(h):
            xi = 0
            for t, (c0, w) in enumerate(TILES):
                for (src, bsrc, dT, bdT, scl) in ((qn, b_qn, qT, b_qT, 128.0 ** -0.5), (kn, b_kn, kT, b_kT, 1.0)):
                    i = xi % 2
                    xi += 1
                    k.op(k.act, lambda: nc.scalar.activation(sq[i][:, 0:w], src[:, c0:c0 + w], AF.Square), reads=[bsrc[t]], writes=[b_sq[i]])
                    pi = self.next_ps()
                    k.op(k.pe, lambda: nc.tensor.matmul(self.ps[pi][:, 0:w], self.ones_bf[:], sq[i][:, 0:w], start=True, stop=True),
                         reads=[b_sq[i], self.b_const], writes=[self.b_ps[pi]])
                    k.op(k.act, lambda: nc.scalar.activation(rs[i][:, 0:w], self.ps[pi][:, 0:w], AF.Ln, bias=self.epsc[:], scale=1.0),
                         reads=[self.b_ps[pi], self.b_const], writes=[b_rs[i]])
                    k.op(k.act, lambda: nc.scalar.activation(rs[i][:, 0:w], rs[i][:, 0:w], AF.Exp, scale=-0.5), reads=[b_rs[i]], writes=[b_rs[i]])
                    k.op(k.dve, lambda: nc.vector.scalar_tensor_tensor(src[:, c0:c0 + w], src[:, c0:c0 + w], scl, rs[i][:, 0:w], ALU.mult, ALU.mult),
                         reads=[bsrc[t], b_rs[i]], writes=[bsrc[t]])
                    k.op(k.dve, lambda: nc.vector.tensor_copy(dT[:, c0:c0 + w], src[:, c0:c0 + w]), reads=[bsrc[t]], writes=[bdT[t]])
                    yield
            for t in range(4):
                c0 = t * 512
                k.op(k.dve, lambda: nc.vector.tensor_scalar(egm[0:8, :], self.egrow[0:8, c0:c0 + 512], self.ident_f[0:8, h:h + 1], None, ALU.mult),
                     reads=[self.b_eg, self.b_gc], writes=[b_egm])
                pi = self.next_ps()
                k.op(k.pe, lambda: nc.tensor.matmul(self.ps[pi][:, :], self.ones_f[0:8, :], egm[0:8, :], start=True, stop=True),
                     reads=[b_egm, self.b_gc], writes=[self.b_ps[pi]])
                k.op(k.dve, lambda: nc.vector.tensor_tensor(qdT[:, c0:c0 + 512], qn[:, c0:c0 + 512], self.ps[pi][:, :], ALU.mult),
                     reads=[b_qn[t], self.b_ps[pi]], writes=[b_qd[t]])
                yield
            ssv = self.ssave[:, h, :]
            kq_sv = ssv[:, 0:32].rearrange("p (s two) -> p s two", two=2)
            bsv = self.b_ssave[h]
            k.op(k.act, lambda: nc.scalar.copy(kq_sv[:, :, 0], kn[:, LP:NT]), reads=[b_kn[4]], writes=[bsv])
            k.op(k.act, lambda: nc.scalar.copy(kq_sv[:, :, 1], qn[:, LP:NT]), reads=[b_qn[4], bsv], writes=[bsv])
            k.op(k.act, lambda: nc.scalar.copy(ssv[:, 32:48], kn[:, LP:NT]), reads=[b_kn[4], bsv], writes=[bsv])
            k.op(k.act, lambda: nc.scalar.copy(ssv[:, 48:64], vs[:, LP:NT]), reads=[b_vs[4], bsv], writes=[bsv])
            k.op(k.act, lambda: nc.scalar.copy(ssv[:, 64:80], zsb[h % 2][:, LP:NT]), reads=[b_zsb[h % 2][4], bsv], writes=[bsv])
            if "gdn_qk" in self.dbg and h == 0:
                self.dbg_dump("qn0", qn, [128, NT], b_qn); self.dbg_dump("kn0", kn, [128, NT], b_kn); self.dbg_dump("vs0", vs, [128, NT], b_vs)
            yield

        def prep(h, Q, st):
            col = col_h(h)
            ch = lambda jj: slice((Q * 4 + jj) * 128, (Q * 4 + jj + 1) * 128)
            DN, DT, L0, NO, Xb, Vb = st["DN"], st["DT"], st["L0"], st["NO"], st["X"], st["V"]
            GUh, GUl = st["GUh"], st["GUl"]
            Mb = [st["M0"], st["M1"]]; b_M = [st["b_M0"], st["b_M1"]]
            for jj in range(4):
                k.op(k.act, lambda: nc.scalar.activation(GUh[:, jj, :], self.uinc_b[:], AF.Identity, scale=col(Q * 4 + jj, 5)),
                     reads=[self.b_cols, self.b_gc], writes=[st["b_GUh"]])
                k.op(k.act, lambda: nc.scalar.activation(GUl[:, jj, :], self.uinc_b[:], AF.Identity, scale=col(Q * 4 + jj, 6)),
                     reads=[self.b_cols, self.b_gc], writes=[st["b_GUl"]])
            pEN = self.ps_alloc()
            for jj in range(4):
                k.op(k.pe, lambda: nc.tensor.matmul(self.ps[pEN][:, sub(jj)], GUh[:, jj, :], self.sm_b[:], start=True, stop=False),
                     reads=[st["b_GUh"], self.b_gc], writes=[self.b_ps[pEN]])
                k.op(k.pe, lambda: nc.tensor.matmul(self.ps[pEN][:, sub(jj)], GUl[:, jj, :], self.sm_b[:], start=False, stop=False),
                     reads=[st["b_GUl"], self.b_gc], writes=[self.b_ps[pEN]])
                k.op(k.pe, lambda: nc.tensor.matmul(self.ps[pEN][:, sub(jj)], self.identb[:], self.negs_b[:], start=False, stop=True),
                     reads=[self.b_gc], writes=[self.b_ps[pEN]])
            k.op(k.act, lambda: nc.scalar.activation(flat(DN), self.ps[pEN][:, :], AF.Exp), reads=[self.b_ps[pEN]], writes=[st["b_DN"]])
            self.ps_free(pEN)
            pDT = self.ps_alloc()
            for jj in range(4):
                k.op(k.pe, lambda: nc.tensor.transpose(self.ps[pDT][:, sub(jj)], DN[:, jj, :], self.ident_f[:]), reads=[st["b_DN"], self.b_gc], writes=[self.b_ps[pDT]])
            k.op(k.dve, lambda: nc.vector.tensor_tensor(DT[:], self.ps[pDT][:, :].rearrange("p (a b) -> p a b", a=4),
                                                        self.ident_f[:].unsqueeze(1).broadcast_to([128, 4, 128]), ALU.add),
                 reads=[self.b_ps[pDT], self.b_gc], writes=[st["b_DT"]])
            self.ps_free(pDT)
            pKK, pQK = self.ps_alloc(), self.ps_alloc()
            for jj in range(4):
                k.op(k.pe, lambda: nc.tensor.matmul(self.ps[pKK][:, sub(jj)], kT[:, ch(jj)], kT[:, ch(jj)], start=True, stop=True),
                     reads=[b_kT[Q]], writes=[self.b_ps[pKK]])
                k.op(k.pe, lambda: nc.tensor.matmul(self.ps[pQK][:, sub(jj)], kT[:, ch(jj)], qT[:, ch(jj)], start=True, stop=True),
                     reads=[b_kT[Q], b_qT[Q]], writes=[self.b_ps[pQK]])
            for jj in range(4):
                k.op(k.dve, lambda: nc.vector.scalar_tensor_tensor(L0[:, jj, :], self.ps[pKK][:, sub(jj)], col(Q * 4 + jj, 1), DN[:, jj, :], ALU.mult, ALU.mult),
                     reads=[self.b_ps[pKK], self.b_cols, st["b_DN"]], writes=[st["b_L0"]])
            k.op(k.dve, lambda: nc.vector.tensor_tensor(flat(st["attnT"]), self.ps[pQK][:, :], flat(DT), ALU.mult),
                 reads=[self.b_ps[pQK], st["b_DT"]], writes=[st["b_attnT"]])
            self.ps_free(pKK, pQK)
            yield
            pk = self.ps_alloc()
            for jj in range(4):
                k.op(k.pe, lambda: nc.tensor.transpose(psb(pk)[:, sub(jj)], kT[:, ch(jj)], self.identb[:]), reads=[b_kT[Q], self.b_gc], writes=[self.b_ps[pk]])
                k.op(k.pe, lambda: nc.tensor.transpose(psb(pk)[:, 512 + jj * 128:512 + (jj + 1) * 128], vT[:, ch(jj)], self.identb[:]),
                     reads=[b_vT[Q], self.b_gc], writes=[self.b_ps[pk]])
            for jj in range(4):
                n = Q * 4 + jj
                k.op(k.act, lambda: nc.scalar.activation(st["kbg"][:, jj, :], psb(pk)[:, sub(jj)], AF.Identity, scale=col(n, 2)),
                     reads=[self.b_ps[pk], self.b_cols], writes=[st["b_kbg"]])
                k.op(k.act, lambda: nc.scalar.activation(st["kd"][:, jj, :], psb(pk)[:, sub(jj)], AF.Identity, scale=col(n, 3)),
                     reads=[self.b_ps[pk], self.b_cols], writes=[st["b_kd"]])
                k.op(k.act, lambda: nc.scalar.activation(st["bv"][:, jj, :], psb(pk)[:, 512 + jj * 128:512 + (jj + 1) * 128], AF.Identity, scale=col(n, 1)),
                     reads=[self.b_ps[pk], self.b_cols], writes=[st["b_bv"]])
            self.ps_free(pk)
            yield
            k.op(k.pool, lambda: nc.gpsimd.tensor_tensor(NO[:], L0[:], mk(0), ALU.mult), reads=[st["b_L0"], self.b_gc], writes=[st["b_NO"]])
            k.op(k.pool, lambda: nc.gpsimd.tensor_tensor(Xb[:], idb4, NO[:], ALU.add), reads=[st["b_NO"], self.b_gc], writes=[st["b_X"]])
            pT = self.ps_alloc()
            for jj in range(4):
                k.op(k.pe, lambda: nc.tensor.transpose(psb(pT)[:, sub(jj)], Xb[:, jj, :], self.identb[:]), reads=[st["b_X"], self.b_gc], writes=[self.b_ps[pT]])
            k.op(k.act, lambda: nc.scalar.copy(flat(Mb[0]), psb(pT)[:, 0:512]), reads=[self.b_ps[pT]], writes=[b_M[0]])
            self.ps_free(pT)
            yield
            cur = 0
            for l in range(1, 7):
                k.op(k.pool, lambda: nc.gpsimd.tensor_tensor(NO[:], L0[:], mk(l), ALU.mult), reads=[st["b_L0"], self.b_gc], writes=[st["b_NO"]])
                pV = self.ps_alloc()
                for jj in range(4):
                    k.op(k.pe, lambda: nc.tensor.matmul(self.ps[pV][:, sub(jj)], NO[:, jj, :], Mb[cur][:, jj, :], start=True, stop=True),
                         reads=[st["b_NO"], b_M[cur]], writes=[self.b_ps[pV]])
                if l > 1:
                    pT = self.ps_alloc()
                    for jj in range(4):
                        k.op(k.pe, lambda: nc.tensor.transpose(psb(pT)[:, sub(jj)], Mb[cur][:, jj, :], self.identb[:]),
                             reads=[b_M[cur], self.b_gc], writes=[self.b_ps[pT]])
                    k.op(k.act, lambda: nc.scalar.copy(flat(Xb), psb(pT)[:, 0:512]), reads=[self.b_ps[pT]], writes=[st["b_X"]])
                    self.ps_free(pT)
                if Q % 2 == 0:
                    k.op(k.dve, lambda: nc.vector.tensor_copy(flat(Vb), self.ps[pV][:, :]), reads=[self.b_ps[pV]], writes=[st["b_V"]])
                else:
                    k.op(k.act, lambda: nc.scalar.copy(flat(Vb), self.ps[pV][:, :]), reads=[self.b_ps[pV]], writes=[st["b_V"]])
                self.ps_free(pV)
                yield
                pM = self.ps_alloc()
                for jj in range(4):
                    k.op(k.pe, lambda: nc.tensor.matmul(self.ps[pM][:, sub(jj)], Xb[:, jj, :], Vb[:, jj, :], start=True, stop=True),
                         reads=[st["b_X"], st["b_V"]], writes=[self.b_ps[pM]])
                k.op(k.dve, lambda: nc.vector.tensor_tensor(flat(Mb[1 - cur]), self.ps[pM][:, :], flat(Mb[cur]), ALU.add),
                     reads=[self.b_ps[pM], b_M[cur]], writes=[b_M[1 - cur]])
                self.ps_free(pM)
                cur = 1 - cur
                yield
            XT, b_XT = Mb[cur], b_M[cur]
            pu, pw = self.ps_alloc(), self.ps_alloc()
            for jj in range(4):
                k.op(k.pe, lambda: nc.tensor.matmul(self.ps[pu][:, sub(jj)], XT[:, jj, :], st["bv"][:, jj, :], start=True, stop=True),
                     reads=[b_XT, st["b_bv"]], writes=[self.b_ps[pu]])
                k.op(k.pe, lambda: nc.tensor.matmul(self.ps[pw][:, sub(jj)], st["kbg"][:, jj, :], XT[:, jj, :], start=True, stop=True),
                     reads=[b_XT, st["b_kbg"]], writes=[self.b_ps[pw]])
            k.op(k.act, lambda: nc.scalar.copy(flat(st["u"]), self.ps[pu][:, :]), reads=[self.b_ps[pu]], writes=[st["b_u"]])
            k.op(k.dve, lambda: nc.vector.tensor_copy(flat(st["wT"]), self.ps[pw][:, :]), reads=[self.b_ps[pw]], writes=[st["b_wT"]])
            self.ps_free(pu, pw)
            yield

        def rec(h, Q, st):
            col = col_h(h)
            ch = lambda jj: slice((Q * 4 + jj) * 128, (Q * 4 + jj + 1) * 128)
            for jj in range(4):
                n = Q * 4 + jj
                pws, po, pds = self.ps_alloc(), self.ps_alloc(), self.ps_alloc()
                k.op(k.pe, lambda: nc.tensor.matmul(self.ps[pws][:, 0:128], st["wT"][:, jj, :], Sbf, start=True, stop=True),
                     reads=[st["b_wT"], b_Sbf], writes=[self.b_ps[pws]])
                k.op(k.dve, lambda: nc.vector.tensor_tensor(vnew, st["u"][:, jj, :], self.ps[pws][:, 0:128], ALU.subtract),
                     reads=[st["b_u"], self.b_ps[pws]], writes=[b_vnew])
                yield
                k.op(k.pe, lambda: nc.tensor.matmul(self.ps[po][:, 0:128], Sbf, qdT[:, ch(jj)], start=True, stop=False),
                     reads=[b_Sbf, b_qd[Q]], writes=[self.b_ps[po]])
                k.op(k.pe, lambda: nc.tensor.matmul(self.ps[po][:, 0:128], vnew, st["attnT"][:, jj, :], start=False, stop=True),
                     reads=[b_vnew, st["b_attnT"]], writes=[self.b_ps[po]])
                k.op(k.pe, lambda: nc.tensor.matmul(self.ps[pds][:, 0:128], st["kd"][:, jj, :], vnew, start=True, stop=True),
                     reads=[st["b_kd"], b_vnew], writes=[self.b_ps[pds]])
                k.op(k.act, lambda: nc.scalar.copy(oT[:, ch(jj)], self.ps[po][:, 0:128]), reads=[self.b_ps[po]], writes=[b_oT[Q]])
                k.op(k.dve, lambda: nc.vector.scalar_tensor_tensor(S, S, col(n, 4), self.ps[pds][:, 0:128], ALU.mult, ALU.add),
                     reads=[b_S, self.b_cols, self.b_ps[pds]], writes=[b_S])
                k.op(k.act, lambda: nc.scalar.copy(Sbf, S), reads=[b_S], writes=[b_Sbf])
                self.ps_free(pws, po, pds)
                yield

        def run(gens):
            gens = list(gens)
            while gens:
                for g in list(gens):
                    try:
                        next(g)
                    except StopIteration:
                        gens.remove(g)

        def chain(*gs):
            for g in gs:
                yield from g


        def b4(h):
            (sq0, sq1, rs0, rs1, tmp0, tmp1), e4 = self.carve(pool0, [([512], BF16)] * 2 + [([512], F32)] * 4)
            assert e4 <= RC0
            sq = [sq0, sq1]; rs = [rs0, rs1]; tmp = [tmp0, tmp1]
            b_sq = [Buf(), Buf()]; b_rs = [Buf(), Buf()]; b_tmp = [Buf(), Buf()]
            for t, (c0, w) in enumerate(TILES[:4]):
                i = t % 2
                k.op(k.act, lambda: nc.scalar.activation(sq[i][:, 0:w], oT[:, c0:c0 + w], AF.Square), reads=[b_oT[t]], writes=[b_sq[i]])
                pi = self.next_ps()
                k.op(k.pe, lambda: nc.tensor.matmul(self.ps[pi][:, 0:w], self.ones_bf[:], sq[i][:, 0:w], start=True, stop=True),
                     reads=[b_sq[i], self.b_const], writes=[self.b_ps[pi]])
                k.op(k.act, lambda: nc.scalar.activation(rs[i][:, 0:w], self.ps[pi][:, 0:w], AF.Ln, bias=self.epsc[:], scale=1.0 / 128),
                     reads=[self.b_ps[pi], self.b_const], writes=[b_rs[i]])
                k.op(k.act, lambda: nc.scalar.activation(rs[i][:, 0:w], rs[i][:, 0:w], AF.Exp, scale=-0.5), reads=[b_rs[i]], writes=[b_rs[i]])
                k.op(k.dve, lambda: nc.vector.scalar_tensor_tensor(tmp[i][:, 0:w], oT[:, c0:c0 + w], self.pvs("dnn"), rs[i][:, 0:w], ALU.mult, ALU.mult),
                     reads=[b_oT[t], b_rs[i], self.b_pv], writes=[b_tmp[i]])
                k.op(k.dve, lambda: nc.vector.tensor_tensor(self.obst[h % 2][:, c0:c0 + w], tmp[i][:, 0:w], zsb[h % 2][:, c0:c0 + w], ALU.mult),
                     reads=[b_tmp[i], b_zsb[h % 2][t]], writes=[self.b_obst[h % 2]])
                yield
            k.dma(k.sp, self.ch_obsp[h % 2], self.d_obsp[:, h * NT:h * NT + LP], self.obst[h % 2][:, 0:LP],
                  reads=[self.b_obst[h % 2]], writes=[self.b_obsp[h]])
            yield

        for _ in front(0):
            pass
        run([mid(0)])
        for h in range(8):
            k.barrier()
            k.op(k.dve, lambda: nc.vector.memset(S, 0.0), writes=[b_S])
            k.op(k.dve, lambda: nc.vector.memset(Sbf, 0.0), writes=[b_Sbf])
            run([prep(h, q_, sets[q_]) for q_ in range(4)])
            k.barrier()
            g_rec = chain(*[rec(h, q_, sets[q_]) for q_ in range(4)])
            g_front = front(h + 1) if h < 7 else iter(())
            i_ = 0
            for _ in g_rec:
                for _n in range(1 if i_ % 2 == 0 else 2):
                    next(g_front, None)
                i_ += 1
            for _ in g_front:
                pass
            k.dma(k.sp, self.st, self.d_Sp[h], S, reads=[b_S])
            k.barrier()
            gens = [b4(h)]
            if h < 7:
                gens.append(mid(h + 1))
            run(gens)
        k.barrier()

    def gdn_samples(self):
        nc, k = self.nc, self.k
        pool0 = self.gdn_off

        def make_set(o, i):
            st = {}
            (st["S0b"],), o = self.carve(o, [([16, 128], F32)])
            (st["Vm"], st["S0h"], st["kqb"]), o = self.carve(o, [([16, 128], BF16), ([16, 128], BF16), ([16, 2], BF16)])
            (st["egm"], st["Bs"], st["prod"], st["vnT"], st["osm"]), o = self.carve(o, [([32], F32), ([32], F32), ([16], F32), ([16], F32), ([16], F32)])
            (st["ktok"],), o = self.carve(o, [([128], BF16)])
            for n_ in list(st.keys()):
                st["b_" + n_] = Buf()
            st["ch_in"] = k.chan(f"s0in{i}")
            st["ch_out"] = k.chan(f"s0out{i}")
            return st, o

        sets = []
        o = pool0
        for i_ in range(4):
            st_, o = make_set(o, i_)
            sets.append(st_)
        assert o <= RC0

        def head(h, st):
            S0b, Vm, egm, Bs, prod, vnT, osm, ktok = [st[n] for n in ("S0b", "Vm", "egm", "Bs", "prod", "vnT", "osm", "ktok")]
            ssv = self.ssave[:, h, :]
            kq = ssv[:, 0:32].rearrange("p (s two) -> p s two", two=2)
            kns, vss = ssv[:, 32:48], ssv[:, 48:64]
            bsv = self.b_ssave[h]
            k.dma(k.act, st["ch_in"], S0b, self.d_S0[:, h].rearrange("s a b -> a s b"), writes=[st["b_S0b"]])
            k.op(k.dve, lambda: nc.vector.tensor_scalar(egm[0:8, :], self.srow[0:8, 0:32], self.ident_f[0:8, h:h + 1], None, ALU.mult),
                 reads=[self.b_srow, self.b_gc], writes=[st["b_egm"]])
            pi = self.ps_alloc()
            k.op(k.pe, lambda: nc.tensor.matmul(self.ps[pi][:, 0:32], self.ones_f[0:8, :], egm[0:8, :], start=True, stop=True),
                 reads=[st["b_egm"], self.b_gc], writes=[self.b_ps[pi]])
            k.op(k.act, lambda: nc.scalar.copy(Bs, self.ps[pi][:, 0:32]), reads=[self.b_ps[pi]], writes=[st["b_Bs"]])
            self.ps_free(pi)
            Beg, Bbe = Bs[:, 0:16], Bs[:, 16:32]
            yield
            pkq, pqk = self.ps_alloc(), self.ps_alloc()
            k.op(k.dve, lambda: nc.vector.tensor_copy(st["S0h"][:], S0b[:]), reads=[st["b_S0b"]], writes=[st["b_S0h"]])
            k.op(k.act, lambda: nc.scalar.copy(st["kqb"][:], kq), reads=[bsv], writes=[st["b_kqb"]])
            for s_ in range(16):
                k.op(k.pe, lambda: nc.tensor.matmul(self.ps[pkq][:, 2 * s_:2 * s_ + 2], st["S0h"][:, s_, :], st["kqb"][:, s_, :], start=True, stop=True),
                     reads=[st["b_S0h"], st["b_kqb"]], writes=[self.b_ps[pkq]])
            kqv = self.ps[pkq][:, 0:32].rearrange("p (s two) -> p s two", two=2)
            k.op(k.dve, lambda: nc.vector.tensor_tensor(vnT, kqv[:, :, 0], Beg, ALU.mult), reads=[self.b_ps[pkq], st["b_Bs"]], writes=[st["b_vnT"]])
            k.op(k.dve, lambda: nc.vector.tensor_tensor(vnT, vss, vnT, ALU.subtract), reads=[bsv, st["b_vnT"]], writes=[st["b_vnT"]])
            k.op(k.dve, lambda: nc.vector.tensor_tensor(vnT, vnT, Bbe, ALU.mult), reads=[st["b_vnT"], st["b_Bs"]], writes=[st["b_vnT"]])
            k.op(k.dve, lambda: nc.vector.tensor_tensor(prod, kq[:, :, 1], kns, ALU.mult), reads=[bsv], writes=[st["b_prod"]])
            k.op(k.pe, lambda: nc.tensor.matmul(self.ps[pqk][:, 0:16], self.ones_f[:], prod, start=True, stop=True),
                 reads=[st["b_prod"], self.b_gc], writes=[self.b_ps[pqk]])
            k.op(k.dve, lambda: nc.vector.tensor_tensor(osm, kqv[:, :, 1], Beg, ALU.mult), reads=[self.b_ps[pkq], st["b_Bs"]], writes=[st["b_osm"]])
            k.op(k.dve, lambda: nc.vector.tensor_tensor(prod, self.ps[pqk][:, 0:16], vnT, ALU.mult),
                 reads=[self.b_ps[pqk], st["b_vnT"], st["b_prod"]], writes=[st["b_prod"]])
            k.op(k.dve, lambda: nc.vector.tensor_tensor(self.osave[:, h, :], osm, prod, ALU.add), reads=[st["b_osm"], st["b_prod"]], writes=[self.b_osave])
            self.ps_free(pkq, pqk)
            yield
            pkt, pvt = self.ps_alloc(), self.ps_alloc()
            k.op(k.pe, lambda: nc.tensor.transpose(self.ps[pkt][0:16, 0:128], kns, self.ident_f[:]), reads=[bsv, self.b_gc], writes=[self.b_ps[pkt]])
            k.op(k.pe, lambda: nc.tensor.transpose(self.ps[pvt][0:16, 0:128], vnT, self.ident_f[:]), reads=[st["b_vnT"], self.b_gc], writes=[self.b_ps[pvt]])
            k.op(k.act, lambda: nc.scalar.copy(ktok[0:16, :], self.ps[pkt][0:16, 0:128]), reads=[self.b_ps[pkt]], writes=[st["b_ktok"]])
            k.op(k.dve, lambda: nc.vector.tensor_tensor(Vm[0:16], self.ps[pvt][0:16, 0:128].unsqueeze(1).broadcast_to([16, 16, 128]),
                                                        self.ident_f[0:16, 0:16].unsqueeze(2).broadcast_to([16, 16, 128]), ALU.mult),
                 reads=[self.b_ps[pvt], self.b_gc], writes=[st["b_Vm"]])
            self.ps_free(pkt, pvt)
            yield
            for g4 in range(4):
                pi = self.ps_alloc()
                k.op(k.pe, lambda: nc.tensor.matmul(self.ps[pi][:, :], ktok[0:16, :], Vm[0:16, 4 * g4:4 * g4 + 4, :].rearrange("p a b -> p (a b)"), start=True, stop=True),
                     reads=[st["b_ktok"], st["b_Vm"]], writes=[self.b_ps[pi]])
                for ss in range(4):
                    s_ = 4 * g4 + ss
                    k.op(k.dve, lambda: nc.vector.scalar_tensor_tensor(S0b[:, s_, :], S0b[:, s_, :], Bs[:, s_:s_ + 1], self.ps[pi][:, ss * 128:(ss + 1) * 128], ALU.mult, ALU.add),
                         reads=[st["b_S0b"], st["b_Bs"], self.b_ps[pi]], writes=[st["b_S0b"]])
                self.ps_free(pi)
                yield
            k.dma(k.sp, st["ch_out"], self.d_Ss[:, h].rearrange("s a b -> a s b"), S0b, reads=[st["b_S0b"]])
            yield

        def run(gens):
            gens = list(gens)
            while gens:
                for g in list(gens):
                    try:
                        next(g)
                    except StopIteration:
                        gens.remove(g)

        for hp in range(0, 8, 4):
            run([head(hp + i_, sets[i_]) for i_ in range(4)])
        (sq, rs, tmp), o2 = self.carve(o, [([128], BF16), ([128], F32), ([128], F32)])
        assert o2 <= RC0
        bq, br, bt_ = Buf(), Buf(), Buf()
        osf = self.osave[:].rearrange("p h s -> p (h s)")
        k.op(k.act, lambda: nc.scalar.activation(sq, osf, AF.Square), reads=[self.b_osave], writes=[bq])
        pi = self.next_ps()
        k.op(k.pe, lambda: nc.tensor.matmul(self.ps[pi][:, 0:128], self.ones_bf[:], sq, start=True, stop=True), reads=[bq, self.b_const], writes=[self.b_ps[pi]])
        k.op(k.act, lambda: nc.scalar.activation(rs, self.ps[pi][:, 0:128], AF.Ln, bias=self.epsc[:], scale=1.0 / 128), reads=[self.b_ps[pi], self.b_const], writes=[br])
        k.op(k.act, lambda: nc.scalar.activation(rs, rs, AF.Exp, scale=-0.5), reads=[br], writes=[br])
        k.op(k.dve, lambda: nc.vector.scalar_tensor_tensor(tmp, osf, self.pvs("dnn"), rs, ALU.mult, ALU.mult), reads=[self.b_osave, br, self.b_pv], writes=[bt_])
        b_obs = Buf()
        k.op(k.dve, lambda: nc.vector.tensor_tensor(self.obs_s, tmp.rearrange("p (h s) -> p h s", h=8), self.ssave[:, :, 64:80], ALU.mult),
             reads=[bt_] + self.b_ssave, writes=[b_obs])
        with nc.allow_non_contiguous_dma(reason="16-token sample columns of the spilled branch output"):
            k.dma(k.sp, self.ch_obsp[0], self.d_obsp.rearrange("p (h t) -> p h t", h=8)[:, :, LP:NT], self.obs_s, reads=[b_obs], writes=[self.b_obsp[8]])
        k.barrier()
        self.ob = self.view(RX0, [8, NT], BF16)
        k.dma(k.sp, self.ch_obsp[1], self.ob.rearrange("p h t -> p (h t)"), self.d_obsp, reads=self.b_obsp,
              writes=[b for r in self.b_ob for b in r])

    def branch_b(self):
        self.b_ob = [[Buf() for _ in range(5)] for _ in range(8)]
        self.gdn_prologue()
        if self.gdn_stop == "pro":
            return
        self.gdn_all()
        self.gdn_samples()


    def tok_phase(self):
        nc, k = self.nc, self.k
        (rows,), _ = self.carve(RA0, [([4096], F32)])
        b_rows = Buf()
        for sp in range(0, 16, 2):
            pi = self.next_ps()
            for half in range(2):
                sl, b_s = self.take_slab(("tok", (sp + half) * 256))
                for kc in range(8):
                    k.op(k.pe, lambda: nc.tensor.matmul(self.ps[pi][0:19, half * 256:(half + 1) * 256], self.hT[:, kc, LP - 3:NT], sl[:, kc, :],
                                                        start=(kc == 0), stop=(kc == 7)),
                         reads=[b_s, self.b_h[kc][3], self.b_h[kc][4]], writes=[self.b_ps[pi]])
                self.done_slab()
            k.op(k.act, lambda: nc.scalar.copy(rows[0:19, sp * 256:(sp + 2) * 256], self.ps[pi][0:19, :]), reads=[self.b_ps[pi]], writes=[b_rows])
        k.dma(k.sp, self.st, self.d_crp, rows[0:3, 0:1024], reads=[b_rows])
        k.dma(k.sp, self.st, self.d_cqp, rows[0:3, 1024:4096], reads=[b_rows])
        k.dma(k.sp, self.st, self.d_crs[:, 2, :], rows[3:19, 0:1024], reads=[b_rows])
        k.dma(k.sp, self.st, self.d_cqs[:, 2, :], rows[3:19, 1024:4096], reads=[b_rows])
        k.dma(k.sp, self.st, self.d_crs[:, 0:2, :], self.d_crn[:, 1:3, :])
        k.dma(k.sp, self.st, self.d_cqs[:, 0:2, :], self.d_cqn[:, 1:3, :])
        k.barrier()

    def skip_slabs_until(self, pred):
        while self.slab_i < self.nslab and not pred(self.plan[self.slab_i]["tag"]):
            self.take_slab(self.plan[self.slab_i]["tag"])
            self.done_slab()

    def final_norm_out(self, es):
        nc, k = self.nc, self.k
        (sq0, sq1, tmp0, tmp1, rs0, rs1), _ = self.carve(RA0, [([FC, 512], BF16)] * 2 + [([FC, 512], F32)] * 2 + [([512], F32)] * 2)
        sq, tmp, rs = [sq0, sq1], [tmp0, tmp1], [rs0, rs1]
        b_sq = [Buf(), Buf()]; b_tmp = [Buf(), Buf()]; b_rs = [Buf(), Buf()]
        o, _ = PV["n_fin"]
        ys = self.d_y.rearrange("(c p) t -> p c t", p=128)
        for t, (c0, w) in enumerate(TILES):
            i = t % 2
            xs = self.xT[:, :, c0:c0 + w]
            bx = [self.b_x[c][t] for c in range(FC)]
            k.op(k.act, lambda: nc.scalar.activation(sq[i][:, :, 0:w], xs, AF.Square), reads=bx, writes=[b_sq[i]])
            pi = self.next_ps()
            for c in range(FC):
                k.op(k.pe, lambda c=c: nc.tensor.matmul(self.ps[pi][:, 0:w], self.ones_bf[:], sq[i][:, c, 0:w],
                                                       start=(c == 0), stop=(c == FC - 1)),
                     reads=[b_sq[i], self.b_const], writes=[self.b_ps[pi]])
            k.op(k.act, lambda: nc.scalar.activation(rs[i][:, 0:w], self.ps[pi][:, 0:w], AF.Ln, bias=self.epsc[:], scale=1.0 / D),
                 reads=[self.b_ps[pi], self.b_const], writes=[b_rs[i]])
            k.op(k.act, lambda: nc.scalar.activation(rs[i][:, 0:w], rs[i][:, 0:w], AF.Exp, scale=-0.5), reads=[b_rs[i]], writes=[b_rs[i]])
            k.op(k.dve, lambda: nc.vector.tensor_tensor(tmp[i][:, :, 0:w], xs, rs[i][:, 0:w].unsqueeze(1).broadcast_to([128, FC, w]), ALU.mult),
                 reads=bx + [b_rs[i]], writes=[b_tmp[i]])
            k.op(k.dve, lambda: nc.vector.tensor_tensor(tmp[i][:, :, 0:w], tmp[i][:, :, 0:w],
                                                        self.pv[:, o:o + 8].unsqueeze(2).broadcast_to([128, FC, w]), ALU.mult),
                 reads=[b_tmp[i], self.b_pv], writes=[b_tmp[i]])
            k.dma(k.sp, self.st, ys[:, :, c0:c0 + w], tmp[i][:, :, 0:w], reads=[b_tmp[i]])

    def dump_x(self, name):
        if name in self.dbg:
            o = self.nc.dram_tensor("dbg_" + name, [D, NT], F32, kind="ExternalOutput").ap()
            self.dbg_out[name] = o
            self.k.dma(self.k.sp, self.st, o.rearrange("(c p) t -> p c t", p=128), self.xT[:],
                       reads=[b for r in self.b_x for b in r])

    def dump_h(self, name):
        if name in self.dbg:
            nc, k = self.nc, self.k
            o = nc.dram_tensor("dbg_" + name, [D, NT], F32, kind="ExternalOutput").ap()
            self.dbg_out[name] = o
            hf = self.view(RA0, [FC, NT], F32)
            bb = Buf()
            k.barrier()
            k.op(k.dve, lambda: nc.vector.tensor_copy(hf, self.hT[:]), reads=[b for r in self.b_h for b in r], writes=[bb])
            k.dma(k.sp, self.st, o.rearrange("(c p) t -> p c t", p=128), hf, reads=[bb])
            k.barrier()


def build_program(dbg=(), stop_after=None, gdn_stop=None):
    from contextlib import ExitStack
    P = Prog(dbg, gdn_stop)
    nc, k = P.nc, P.k
    P.prologue()
    P.ada(0, 24)
    P.mod_prep(0)
    with ExitStack() as es:
        P.norm_mod(0, es)
        k.barrier()
    P.dump_h("h1")
    P.ada_pending = list(range(24, 72, 2))
    with ExitStack() as es:
        P.ffn(1, es)
        while P.ada_pending:
            P.ada_step(P.ada_pending.pop(0))
        k.barrier()
    P.dump_x("x1")
    if stop_after == "ffn1":
        with ExitStack() as es:
            P.final_norm_out(es)
            k.barrier()
        k.finish()
        return P
    P.mod_prep(1)
    P.mod_prep(2)
    P.norm_mod(1, None)
    k.barrier()
    P.dump_h("h2")
    P.spill_x()
    P.gdn_consts()
    P.mix_prologue()
    k.barrier()
    P.branch_b()
    P.dump_bf("ob", P.ob, [b for r in P.b_ob for b in r])
    if stop_after == "gdn":
        P.skip_slabs_until(lambda tag: False)
        k.finish()
        return P
    P.merge(1)
    P.branch_a()
    P.dump_bf("oa", P.oa, [b for r in P.b_oa for b in r])
    P.merge(0)
    P.reload_x()
    P.out_proj()
    P.dump_x("x2")
    k.dma(k.sp, P.st, P.d_hnew, P.hnewT[:].rearrange("p c s -> p (c s)"), reads=[P.b_hnew])
    P.tok_phase()
    with ExitStack() as es:
        P.norm_mod(2, es)
        k.barrier()
    with ExitStack() as es:
        P.ffn(2, es)
        k.barrier()
    with ExitStack() as es:
        P.final_norm_out(es)
        k.barrier()
    k.finish()
    return P


def doubling_masks():
    i = np.arange(128)
    out = np.zeros((128, 7, 128), np.float32)
    for l in range(7):
        bi, bj = (i >> l)[:, None], (i >> l)[None, :]
        out[:, l, :] = -(((bi & 1) == 1) & (bj == bi - 1)).astype(np.float32)
    return out.reshape(128, 7 * 128)


def host_inputs(inputs):
    W = {n: np.asarray(v, np.float32) for n, v in inputs.items()}
    plan = build_plan()
    wslabs = gather_slabs(plan, W)
    pvec = pack_pvec(W)
    cmask = doubling_masks()
    maps = []
    for b in range(NCORES):
        sl = slice(NS_ * b, NS_ * (b + 1))
        xT = np.ascontiguousarray(np.concatenate([W["x_prompt"][b], W["x_sample"][sl, 0, :]], axis=0).T)
        cT = np.ascontiguousarray(np.concatenate([W["c_prompt"][b:b + 1], W["c_sample"][sl]], axis=0).T)
        sm = np.zeros((128, NSM), np.float32)
        cr = W["state_rglru_conv"][0, sl]
        sm[:, 0:384] = cr.reshape(16, 3, 8, 128).transpose(3, 2, 1, 0).reshape(128, 384)
        sm[:, 384:512] = W["state_rglru_h"][0, sl].reshape(16, 8, 128).transpose(2, 1, 0).reshape(128, 128)
        cq = W["state_delta_conv"][0, sl]
        sm[:, 512:1664] = cq.reshape(16, 3, 24, 128).transpose(3, 2, 1, 0).reshape(128, 1152)
        maps.append(dict(xT=xT, cT=cT, wslabs=wslabs, pvec=pvec, smallT=sm, S0=np.ascontiguousarray(W["state_delta_S"][0, sl]), cmask=cmask,
                         cr_nat=np.ascontiguousarray(cr), cq_nat=np.ascontiguousarray(cq)))
    return maps


_PROG = None


def kernel(**inputs):
    global _PROG
    if _PROG is None:
        _PROG = build_program()
    P = _PROG
    maps = host_inputs(inputs)
    res = run_bass_kernel_spmd(P.nc, maps, core_ids=list(range(NCORES)))
    R = res.results
    B, NSQ = NCORES, NCORES * NS_
    y_p = np.zeros((B, LP, D), np.float32); y_s = np.zeros((NSQ, 1, D), np.float32)
    h_p = np.zeros((1, B, D), np.float32); h_s = np.zeros((1, NSQ, D), np.float32)
    cr_p = np.zeros((1, B, 3, D), np.float32); cr_s = np.zeros((1, NSQ, 3, D), np.float32)
    S_p = np.zeros((1, B, 8, 128, 128), np.float32); S_s = np.zeros((1, NSQ, 8, 128, 128), np.float32)
    cq_p = np.zeros((1, B, 3, 3072), np.float32); cq_s = np.zeros((1, NSQ, 3, 3072), np.float32)
    for b in range(B):
        r = R[b]
        sl = slice(NS_ * b, NS_ * (b + 1))
        yT = np.asarray(r["yT"])
        y_p[b] = yT[:, :LP].T
        y_s[sl, 0] = yT[:, LP:].T
        hn = np.asarray(r["hnewT_o"]).reshape(128, 8, 17)
        hh = hn.transpose(2, 1, 0).reshape(17, D)
        h_p[0, b] = hh[0]; h_s[0, sl] = hh[1:]
        cr_p[0, b] = np.asarray(r["cr_p"]); cq_p[0, b] = np.asarray(r["cq_p"])
        cr_s[0, sl] = np.asarray(r["cr_s"]); cq_s[0, sl] = np.asarray(r["cq_s"])
        S_p[0, b] = np.asarray(r["S_p"]); S_s[0, sl] = np.asarray(r["S_s"])
    return (y_p, y_s, h_p, cr_p, S_p, cq_p, h_s, cr_s, S_s, cq_s)
```
